# Optimizing a Trainium2 kernel written in Bass

```python
import jax, jax.numpy as jnp
from jax import lax
import numpy as np

D_MODEL = 1024
BATCH = 8
SEQ = 2048
DEPTH = 2

CTX_LEN = 256
GRID_W = 64

N_MOD = 6
DEEPNORM_ALPHA = (2.0 * DEPTH) ** 0.25
DEEPNORM_BETA = (8.0 * DEPTH) ** -0.25
LN_EPS = 1e-5
RMS_EPS = 1e-6

MLA_V = 64
MLA_HEADS = D_MODEL // (2 * MLA_V)
MLA_NOPE = 64
MLA_ROPE = 32
MLA_Q_RANK = 384
MLA_KV_RANK = 256
ROPE_AXIS = MLA_ROPE // 2
ROPE_THETA = 10000.0
Q_BLOCK = 128

RWKV_HEAD = 64
RWKV_HEADS = D_MODEL // (2 * RWKV_HEAD)
RWKV_W = RWKV_HEADS * RWKV_HEAD
DECAY_LORA = 64
AAA_LORA = 64
GATE_LORA = 128
GN_EPS = 64e-5
RWKV_IN = 3 * RWKV_W + 2 * DECAY_LORA + 2 * AAA_LORA + GATE_LORA
RWKV_SPLIT = (RWKV_W, 2 * RWKV_W, 3 * RWKV_W, 3 * RWKV_W + 2 * DECAY_LORA, 3 * RWKV_W + 2 * DECAY_LORA + 2 * AAA_LORA)

EVEN_SPLIT = (MLA_Q_RANK, MLA_Q_RANK + MLA_KV_RANK, MLA_Q_RANK + MLA_KV_RANK + MLA_ROPE)
EVEN_IN = EVEN_SPLIT[-1] + RWKV_IN

LRU_WIDTH = D_MODEL
LRU_BLOCKS = 8
LRU_BLOCK = LRU_WIDTH // LRU_BLOCKS
LRU_C = 8.0
CONV_W = 4

D_FF = 4 * D_MODEL

kernel_name = 'hybrid_mla_rwkv7_rglru_diffusion_trunk'


def layer_norm(x, g, b):
    xf = x.astype(jnp.float32)
    mu = jnp.mean(xf, axis=-1, keepdims=True)
    var = jnp.mean(jnp.square(xf - mu), axis=-1, keepdims=True)
    return ((xf - mu) * lax.rsqrt(var + LN_EPS)).astype(x.dtype) * g + b


def rms_norm(x, g):
    xf = x.astype(jnp.float32)
    return (xf * lax.rsqrt(jnp.mean(jnp.square(xf), axis=-1, keepdims=True) + RMS_EPS)).astype(x.dtype) * g


def modulate(x, shift, scale):
    return x * (1.0 + scale) + shift


def squared_relu_mlp(h, w1, w2):
    return jnp.square(jax.nn.relu(h @ w1)) @ w2


def axial_rope_tables(n):
    rows_n = n // GRID_W
    rows = jnp.repeat(jnp.arange(rows_n, dtype=jnp.float32), GRID_W)
    cols = jnp.tile(jnp.arange(GRID_W, dtype=jnp.float32), rows_n)
    inv_freq = ROPE_THETA ** (-jnp.arange(0, ROPE_AXIS, 2, dtype=jnp.float32) / ROPE_AXIS)
    ang_r = rows[:, None] * inv_freq
    ang_c = cols[:, None] * inv_freq
    ang = jnp.concatenate([ang_r, ang_r, ang_c, ang_c], axis=-1)
    return jnp.cos(ang), jnp.sin(ang)


def rotate_half_axial(x):
    xs = x.reshape(x.shape[:-1] + (2, 2, ROPE_AXIS // 2))
    x1, x2 = xs[..., 0, :], xs[..., 1, :]
    return jnp.stack([-x2, x1], axis=-2).reshape(x.shape)


def apply_rope(x, cos, sin):
    return x * cos.astype(x.dtype) + rotate_half_axial(x) * sin.astype(x.dtype)


def softmax_attend(q, k, v):
    s = jnp.einsum('bqhd,bkhd->bhqk', q, k).astype(jnp.float32) * (MLA_NOPE + MLA_ROPE) ** -0.5
    p = jax.nn.softmax(s, axis=-1).astype(v.dtype)
    return jnp.einsum('bhqk,bkhd->bqhd', p, v)


def blocked_attend(q, k, v):
    b, n, h, dk = q.shape
    qb = q.reshape(b, n // Q_BLOCK, Q_BLOCK, h, dk).transpose(1, 0, 2, 3, 4)
    out = lax.map(lambda blk: softmax_attend(blk, k, v), qb)
    return out.transpose(1, 0, 2, 3, 4).reshape(b, n, h, v.shape[-1])


def mla_features(f_q, f_kv, f_kr, q_norm, w_uq, kv_norm, w_uk, w_uv, rope):
    b, n = f_q.shape[:2]
    q = (rms_norm(f_q, q_norm) @ w_uq).reshape(b, n, MLA_HEADS, MLA_NOPE + MLA_ROPE)
    q_nope, q_rope = q[..., :MLA_NOPE], q[..., MLA_NOPE:]
    ckv = rms_norm(f_kv, kv_norm)
    k_nope = (ckv @ w_uk).reshape(b, n, MLA_HEADS, MLA_NOPE)
    v = (ckv @ w_uv).reshape(b, n, MLA_HEADS, MLA_V)
    k_rope = f_kr
    if rope is not None:
        cos, sin = rope
        q_rope = apply_rope(q_rope, cos[:, None, :], sin[:, None, :])
        k_rope = apply_rope(k_rope, cos, sin)
    q = jnp.concatenate([q_nope, q_rope], axis=-1)
    k = jnp.concatenate([k_nope, jnp.broadcast_to(k_rope[:, :, None, :], (b, n, MLA_HEADS, MLA_ROPE))], axis=-1)
    return q, k, v


def centred_shift(f):
    prev = jnp.pad(f[:, :-1], ((0, 0), (1, 0), (0, 0)))
    nxt = jnp.pad(f[:, 1:], ((0, 0), (0, 1), (0, 0)))
    return 0.5 * (prev + nxt)


def rwkv_features(f, mu, w0, w2, a0, a2, g2, k_k, k_a):
    f = f.astype(jnp.float32)
    f = f + mu * (centred_shift(f) - f)
    r, k, v, wl, al, gl = jnp.split(f, RWKV_SPLIT, axis=-1)
    b, n = f.shape[:2]
    heads = lambda t: t.reshape(t.shape[:-1] + (RWKV_HEADS, RWKV_HEAD))
    w_raw = w0 + jnp.einsum('bndl,dlc->bndc', jnp.tanh(wl.reshape(b, n, 2, DECAY_LORA)), w2)
    decay = jnp.exp(-jnp.exp(-jax.nn.softplus(-w_raw) - 0.5))
    a = jax.nn.sigmoid(a0 + jnp.einsum('bndl,dlc->bndc', al.reshape(b, n, 2, AAA_LORA), a2))
    g = jax.nn.sigmoid(gl) @ g2
    kk = heads(k * k_k)
    kk = kk * lax.rsqrt(jnp.sum(kk * kk, axis=-1, keepdims=True) + 1e-12)
    k_dir = heads(k[:, :, None, :] * (1.0 + (a - 1.0) * k_a))
    return heads(r), k_dir, heads(v), kk, heads(a), heads(decay), g


def rwkv7_scan(r, decay, k, v, kk, a, s0, reverse):
    def step(S, inp):
        r_t, w_t, k_t, v_t, kk_t, a_t = inp
        sa = jnp.einsum('bhij,bhj->bhi', S, kk_t)
        S = S * w_t[:, :, None, :] - sa[..., :, None] * (kk_t * a_t)[..., None, :] + v_t[..., :, None] * k_t[..., None, :]
        return S, jnp.einsum('bhij,bhj->bhi', S, r_t)
    xs = tuple(jnp.moveaxis(t.astype(jnp.float32), 1, 0) for t in (r, decay, k, v, kk, a))
    s_final, ys = lax.scan(step, s0, xs, reverse=reverse)
    return s_final, jnp.moveaxis(ys, 0, 1)


def rwkv_output(y, r, k_dir, v, r_k, gn_w, gn_b, g):
    b, n = y.shape[:2]
    mu = jnp.mean(y, axis=-1, keepdims=True)
    var = jnp.mean(jnp.square(y - mu), axis=-1, keepdims=True)
    yn = ((y - mu) * lax.rsqrt(var + GN_EPS)).reshape(b, n, RWKV_W) * gn_w + gn_b
    bonus = (jnp.sum(r[:, :, None] * k_dir * r_k, axis=(2, -1))[..., None] * v).reshape(b, n, RWKV_W)
    return (yn + bonus) * g


def depthwise_conv(x, w, b):
    y = lax.conv_general_dilated(x, w[:, None, :].astype(x.dtype), window_strides=(1,),
                                 padding=[(CONV_W // 2, CONV_W - 1 - CONV_W // 2)],
                                 dimension_numbers=('NWC', 'WIO', 'NWC'), feature_group_count=x.shape[-1])
    return y + b


def block_diag_linear(x, w, b):
    xb = x.reshape(x.shape[:-1] + (LRU_BLOCKS, LRU_BLOCK))
    return jnp.einsum('bnkc,kcd->bnkd', xb, w).reshape(x.shape) + b


def rglru_inputs(x, wa, ba, wx, bx, lam):
    xf = x.astype(jnp.float32)
    r = jax.nn.sigmoid(block_diag_linear(xf, wa, ba))
    i = jax.nn.sigmoid(block_diag_linear(xf, wx, bx))
    log_a = -LRU_C * r * jax.nn.softplus(-lam)
    return jnp.exp(log_a), jnp.sqrt(-jnp.expm1(2.0 * log_a)) * (i * xf)


def _lin_combine(e1, e2):
    a1, b1 = e1
    a2, b2 = e2
    return a1 * a2, a2 * b1 + b2


def rglru_scan(a, u, h0, reverse):
    a_cum, h = lax.associative_scan(_lin_combine, (a, u), axis=1, reverse=reverse)
    return h + a_cum * h0[:, None, :]


def even_mixer(hl, hc, w_in, q_norm, w_uq, kv_norm, w_uk, w_uv, mu, w0, w2, a0, a2, g2, k_k, k_a, r_k,
               gn_w, gn_b, w_out, ctx_out):
    b, n = hl.shape[:2]
    fq_l, fkv_l, fkr_l, frw_l = jnp.split(hl @ w_in, EVEN_SPLIT, axis=-1)
    fq_c, fkv_c, fkr_c, frw_c = jnp.split(hc @ w_in, EVEN_SPLIT, axis=-1)
    q_l, k_l, v_l = mla_features(fq_l, fkv_l, fkr_l, q_norm, w_uq, kv_norm, w_uk, w_uv, axial_rope_tables(n))
    q_c, k_c, v_c = mla_features(fq_c, fkv_c, fkr_c, q_norm, w_uq, kv_norm, w_uk, w_uv, None)
    att_l = blocked_attend(q_l, jnp.concatenate([k_l, k_c], axis=1),
                           jnp.concatenate([v_l, v_c], axis=1)).reshape(b, n, MLA_HEADS * MLA_V)
    r_l, kd_l, vr_l, kk_l, a_l, w_l, g_l = rwkv_features(frw_l, mu, w0, w2, a0, a2, g2, k_k, k_a)
    r_c, kd_c, vr_c, kk_c, a_c, w_c, g_c = rwkv_features(frw_c, mu, w0, w2, a0, a2, g2, k_k, k_a)
    s0 = jnp.zeros((b, RWKV_HEADS, RWKV_HEAD, RWKV_HEAD), jnp.float32)
    ys_l, ys_c = [], []
    for d, reverse in ((0, False), (1, True)):
        s_ctx, y_c = rwkv7_scan(r_c, w_c[:, :, d], kd_c[:, :, d], vr_c, kk_c, a_c[:, :, d], s0, reverse)
        _, y_l = rwkv7_scan(r_l, w_l[:, :, d], kd_l[:, :, d], vr_l, kk_l, a_l[:, :, d], s_ctx, reverse)
        ys_l.append(y_l)
        ys_c.append(y_c)
    rw_l = rwkv_output(ys_l[0] + ys_l[1], r_l, kd_l, vr_l, r_k, gn_w, gn_b, g_l).astype(hl.dtype)
    y_lat = jnp.concatenate([att_l, rw_l], axis=-1) @ w_out
    if ctx_out:
        att_c = softmax_attend(q_c, k_c, v_c).reshape(b, hc.shape[1], MLA_HEADS * MLA_V)
        rw_c = rwkv_output(ys_c[0] + ys_c[1], r_c, kd_c, vr_c, r_k, gn_w, gn_b, g_c).astype(hc.dtype)
        y_ctx = jnp.concatenate([att_c, rw_c], axis=-1) @ w_out
    else:
        y_ctx = None
    return y_lat, y_ctx


def odd_mixer(hl, hc, w_in, conv_w, conv_b, ga_w, ga_b, gx_w, gx_b, lam, w_out, ctx_out):
    b = hl.shape[0]
    gate_l, xr_l = jnp.split(hl @ w_in, 2, axis=-1)
    gate_c, xr_c = jnp.split(hc @ w_in, 2, axis=-1)
    xr_l = depthwise_conv(xr_l, conv_w, conv_b)
    xr_c = depthwise_conv(xr_c, conv_w, conv_b)
    hs_l, hs_c = [], []
    for d, reverse in ((0, False), (1, True)):
        a_c, u_c = rglru_inputs(xr_c, ga_w[d], ga_b[d], gx_w[d], gx_b[d], lam[d])
        h_c = rglru_scan(a_c, u_c, jnp.zeros((b, LRU_WIDTH), jnp.float32), reverse)
        h0 = h_c[:, 0] if reverse else h_c[:, -1]
        a_l, u_l = rglru_inputs(xr_l, ga_w[d], ga_b[d], gx_w[d], gx_b[d], lam[d])
        hs_l.append(rglru_scan(a_l, u_l, h0, reverse))
        hs_c.append(h_c)
    y_lat = (jax.nn.gelu(gate_l) * (hs_l[0] + hs_l[1]).astype(hl.dtype)) @ w_out
    if ctx_out:
        y_ctx = (jax.nn.gelu(gate_c) * (hs_c[0] + hs_c[1]).astype(hc.dtype)) @ w_out
    else:
        y_ctx = None
    return y_lat, y_ctx


def trunk_layer(x, xc, c, c_ctx, mixer_fn, mixer_params, mod_w, mod_b, ln1_g, ln1_b, mlp_w1, mlp_w2,
                ln2_g, ln2_b, ctx_out):
    mod_l = jnp.split(jax.nn.silu(c) @ mod_w + mod_b, N_MOD, axis=-1)
    mod_c = jnp.split(jax.nn.silu(c_ctx) @ mod_w + mod_b, N_MOD, axis=-1)
    sh1, sc1, g1, sh2, sc2, g2 = [m[:, None, :] for m in mod_l]
    csh1, csc1, cg1, csh2, csc2, cg2 = mod_c
    y_l, y_c = mixer_fn(modulate(x, sh1, sc1), modulate(xc, csh1, csc1), *mixer_params, ctx_out=ctx_out)
    x = layer_norm(DEEPNORM_ALPHA * x + g1 * y_l, ln1_g, ln1_b)
    x = layer_norm(DEEPNORM_ALPHA * x + g2 * squared_relu_mlp(modulate(x, sh2, sc2), mlp_w1, mlp_w2), ln2_g, ln2_b)
    if ctx_out:
        xc = layer_norm(DEEPNORM_ALPHA * xc + cg1 * y_c, ln1_g, ln1_b)
        xc = layer_norm(DEEPNORM_ALPHA * xc + cg2 * squared_relu_mlp(modulate(xc, csh2, csc2), mlp_w1, mlp_w2),
                        ln2_g, ln2_b)
    return x, xc


def setup_inputs(seed: int = 0) -> dict:
    key = jax.random.key(seed)
    ks = jax.random.split(key, 64)
    keys = iter([ks[i] for i in range(64)])

    def nrm(shape, scale):
        return scale * jax.random.normal(next(keys), shape, jnp.float32)

    def uni(shape, lo, hi):
        return jax.random.uniform(next(keys), shape, jnp.float32, lo, hi)

    def lru_lambda(shape):
        p = uni(shape, 0.9, 0.999) ** (1.0 / LRU_C)
        return jnp.log(p) - jnp.log1p(-p)

    D = D_MODEL
    return {
        'x': nrm((BATCH, SEQ, D), 1.0),
        'c': nrm((BATCH, D), 1.0),
        'ctx': nrm((BATCH, CTX_LEN, D), 1.0),
        'c_ctx': nrm((D,), 1.0),
        'l0_mod_w': nrm((D, N_MOD * D), 0.5 * D ** -0.5),
        'l0_mod_b': nrm((N_MOD * D,), 0.02),
        'l0_w_in': nrm((D, EVEN_IN), D ** -0.5),
        'l0_mla_q_norm': 1.0 + nrm((MLA_Q_RANK,), 0.05),
        'l0_mla_w_uq': nrm((MLA_Q_RANK, MLA_HEADS * (MLA_NOPE + MLA_ROPE)), MLA_Q_RANK ** -0.5),
        'l0_mla_kv_norm': 1.0 + nrm((MLA_KV_RANK,), 0.05),
        'l0_mla_w_uk': nrm((MLA_KV_RANK, MLA_HEADS * MLA_NOPE), MLA_KV_RANK ** -0.5),
        'l0_mla_w_uv': nrm((MLA_KV_RANK, MLA_HEADS * MLA_V), MLA_KV_RANK ** -0.5),
        'l0_rwkv_mu': uni((RWKV_IN,), 0.0, 1.0),
        'l0_rwkv_w0': uni((2, RWKV_W), -6.0, 1.0),
        'l0_rwkv_w2': nrm((2, DECAY_LORA, RWKV_W), 0.5 * DECAY_LORA ** -0.5),
        'l0_rwkv_a0': nrm((2, RWKV_W), 0.5),
        'l0_rwkv_a2': nrm((2, AAA_LORA, RWKV_W), 0.5 * AAA_LORA ** -0.5),
        'l0_rwkv_g2': nrm((GATE_LORA, RWKV_W), GATE_LORA ** -0.5),
        'l0_rwkv_k_k': 0.85 + nrm((RWKV_W,), 0.05),
        'l0_rwkv_k_a': 1.0 + nrm((RWKV_W,), 0.05),
        'l0_rwkv_r_k': nrm((RWKV_HEADS, RWKV_HEAD), 0.1),
        'l0_rwkv_gn_w': 1.0 + nrm((RWKV_W,), 0.05),
        'l0_rwkv_gn_b': nrm((RWKV_W,), 0.02),
        'l0_w_out': nrm((MLA_HEADS * MLA_V + RWKV_W, D), DEEPNORM_BETA * D ** -0.5),
        'l0_ln1_g': 1.0 + nrm((D,), 0.05),
        'l0_ln1_b': nrm((D,), 0.02),
        'l0_mlp_w1': nrm((D, D_FF), D ** -0.5),
        'l0_mlp_w2': nrm((D_FF, D), DEEPNORM_BETA * D_FF ** -0.5),
        'l0_ln2_g': 1.0 + nrm((D,), 0.05),
        'l0_ln2_b': nrm((D,), 0.02),
        'l1_mod_w': nrm((D, N_MOD * D), 0.5 * D ** -0.5),
        'l1_mod_b': nrm((N_MOD * D,), 0.02),
        'l1_w_in': nrm((D, 2 * LRU_WIDTH), D ** -0.5),
        'l1_conv_w': nrm((CONV_W, LRU_WIDTH), CONV_W ** -0.5),
        'l1_conv_b': nrm((LRU_WIDTH,), 0.02),
        'l1_lru_ga_w': nrm((2, LRU_BLOCKS, LRU_BLOCK, LRU_BLOCK), LRU_BLOCK ** -0.5),
        'l1_lru_ga_b': nrm((2, LRU_WIDTH), 0.02),
        'l1_lru_gx_w': nrm((2, LRU_BLOCKS, LRU_BLOCK, LRU_BLOCK), LRU_BLOCK ** -0.5),
        'l1_lru_gx_b': nrm((2, LRU_WIDTH), 0.02),
        'l1_lru_lambda': lru_lambda((2, LRU_WIDTH)),
        'l1_w_out': nrm((LRU_WIDTH, D), DEEPNORM_BETA * LRU_WIDTH ** -0.5),
        'l1_ln1_g': 1.0 + nrm((D,), 0.05),
        'l1_ln1_b': nrm((D,), 0.02),
        'l1_mlp_w1': nrm((D, D_FF), D ** -0.5),
        'l1_mlp_w2': nrm((D_FF, D), DEEPNORM_BETA * D_FF ** -0.5),
        'l1_ln2_g': 1.0 + nrm((D,), 0.05),
        'l1_ln2_b': nrm((D,), 0.02),
    }


def reference(x, c, ctx, c_ctx,
              l0_mod_w, l0_mod_b, l0_w_in, l0_mla_q_norm, l0_mla_w_uq, l0_mla_kv_norm, l0_mla_w_uk, l0_mla_w_uv,
              l0_rwkv_mu, l0_rwkv_w0, l0_rwkv_w2, l0_rwkv_a0, l0_rwkv_a2, l0_rwkv_g2, l0_rwkv_k_k, l0_rwkv_k_a,
              l0_rwkv_r_k, l0_rwkv_gn_w, l0_rwkv_gn_b, l0_w_out,
              l0_ln1_g, l0_ln1_b, l0_mlp_w1, l0_mlp_w2, l0_ln2_g, l0_ln2_b,
              l1_mod_w, l1_mod_b, l1_w_in, l1_conv_w, l1_conv_b, l1_lru_ga_w, l1_lru_ga_b, l1_lru_gx_w, l1_lru_gx_b,
              l1_lru_lambda, l1_w_out,
              l1_ln1_g, l1_ln1_b, l1_mlp_w1, l1_mlp_w2, l1_ln2_g, l1_ln2_b):
    mixers = (
        (even_mixer, (l0_w_in, l0_mla_q_norm, l0_mla_w_uq, l0_mla_kv_norm, l0_mla_w_uk, l0_mla_w_uv,
                      l0_rwkv_mu, l0_rwkv_w0, l0_rwkv_w2, l0_rwkv_a0, l0_rwkv_a2, l0_rwkv_g2, l0_rwkv_k_k,
                      l0_rwkv_k_a, l0_rwkv_r_k, l0_rwkv_gn_w, l0_rwkv_gn_b, l0_w_out)),
        (odd_mixer, (l1_w_in, l1_conv_w, l1_conv_b, l1_lru_ga_w, l1_lru_ga_b, l1_lru_gx_w, l1_lru_gx_b,
                     l1_lru_lambda, l1_w_out)),
    )
    commons = (
        (l0_mod_w, l0_mod_b, l0_ln1_g, l0_ln1_b, l0_mlp_w1, l0_mlp_w2, l0_ln2_g, l0_ln2_b),
        (l1_mod_w, l1_mod_b, l1_ln1_g, l1_ln1_b, l1_mlp_w1, l1_mlp_w2, l1_ln2_g, l1_ln2_b),
    )
    xc = ctx
    for layer in range(DEPTH):
        mixer_fn, mixer_params = mixers[layer % 2]
        x, xc = trunk_layer(x, xc, c, c_ctx, mixer_fn, mixer_params, *commons[layer],
                            ctx_out=(layer < DEPTH - 1))
    return x
```

```python
import contextlib
import numpy as np
import concourse.bass as bass
import concourse.mybir as mybir
from concourse.bass_utils import run_bass_kernel_spmd

F32 = mybir.dt.float32
BF16 = mybir.dt.bfloat16
AF = mybir.ActivationFunctionType
ALU = mybir.AluOpType
AX = mybir.AxisListType

NTOK = 2304
NCTX = 256
NLAT = 2048
TILES = [(0, 256), (256, 512), (768, 512), (1280, 512), (1792, 512)]
ALPHA = 4.0 ** 0.25
CDEC = float(np.exp(-0.5))
SCALE = 96.0 ** -0.5
NCH = 18

ENGS = ("pe", "act", "dve", "pool", "sp")
NDMASEM = 8


class Buf:
    __slots__ = ("name", "w", "r")

    def __init__(self, name=""):
        self.name = name
        self.w = None
        self.r = []


class T:
    __slots__ = ("ap", "b")

    def __init__(self, ap, name=""):
        self.ap = ap
        self.b = Buf(name)


def _b(x):
    return x.b if isinstance(x, T) else x


class Sched:
    def __init__(self, nc):
        self.nc = nc
        self.streams = {e: [] for e in ENGS}
        self.cnt = {e: 0 for e in ENGS}
        self.waited = {}
        self.sem = {}
        self.dma_n = {"sp": 0, "pool": 0, "act": 0}
        self.pending_nosig = {e: False for e in ENGS}
        self.ninst = 0

    def _semkeys(self):
        keys = list(ENGS)
        for q in ("sp", "pool", "act"):
            for i in range(NDMASEM):
                keys.append(("dma", q, i))
        return keys

    def _need(self, eng, tok, waits):
        if tok is None:
            return
        key, val = tok
        if self.waited.get((eng, key), 0) >= val:
            return
        if key == eng and eng == "pe":
            return
        self.waited[(eng, key)] = val
        waits[key] = max(waits.get(key, 0), val)

    def _deps(self, eng, reads, writes):
        waits = {}
        for b in reads:
            self._need(eng, _b(b).w, waits)
        for b in writes:
            b = _b(b)
            self._need(eng, b.w, waits)
            for t in b.r:
                self._need(eng, t, waits)
        return waits

    def _commit(self, tok, reads, writes):
        for b in reads:
            b = _b(b)
            b.r.append(tok)
            if len(b.r) > 48:
                d = {}
                for k, v in b.r:
                    d[k] = max(d.get(k, 0), v)
                b.r = list(d.items())
        for b in writes:
            b = _b(b)
            b.w = tok
            b.r = []

    def op(self, eng, fn, reads=(), writes=(), sig=True):
        waits = self._deps(eng, reads, writes)
        if sig:
            self.cnt[eng] += 1
            self.pending_nosig[eng] = False
        else:
            self.pending_nosig[eng] = True
        tok = (eng, self.cnt[eng] if sig else self.cnt[eng] + 1)
        self._commit(tok, reads, writes)
        self.streams[eng].append((waits, fn, eng if sig else None, 1))
        self.ninst += 1

    def dma(self, q, out, in_, reads=(), writes=()):
        n = self.dma_n[q]
        self.dma_n[q] += 1
        slot = n % NDMASEM
        key = ("dma", q, slot)
        val = 16 * (n // NDMASEM + 1)
        waits = self._deps(q, reads, writes)
        if val > 16:
            self._need(q, (key, val - 16), waits)
        tok = (key, val)
        self._commit(tok, reads, writes)
        self.streams[q].append((waits, E("dma_start", out=out, in_=in_), key, 16))
        self.ninst += 1
        return tok

    def all_tokens(self):
        toks = [(e, self.cnt[e]) for e in ENGS if self.cnt[e] > 0]
        for q, n in self.dma_n.items():
            for slot in range(min(n, NDMASEM)):
                last = ((n - 1 - slot) // NDMASEM) * NDMASEM + slot
                toks.append((("dma", q, slot), 16 * (last // NDMASEM + 1)))
        return toks

    def barrier(self):
        for e in ENGS:
            assert not self.pending_nosig[e], e
        toks = self.all_tokens()
        for e in ENGS:
            waits = {}
            for t in toks:
                self._need(e, t, waits)
            if waits:
                self.streams[e].append((waits, None, None, 0))

    def run(self):
        nc = self.nc
        self.barrier()
        with contextlib.ExitStack() as st:
            for k in self._semkeys():
                nm = k if isinstance(k, str) else f"d_{k[1]}_{k[2]}"
                self.sem[k] = st.enter_context(nc.semaphore("s_" + nm))
            block = st.enter_context(nc.Block())
            sem = self.sem

            def mk(ename):
                stream = self.streams[ename]

                def body(e):
                    for waits, fn, sigkey, inc in stream:
                        for k, v in waits.items():
                            e.wait_ge(sem[k], v)
                        if fn is not None:
                            ins = fn(e)
                            if sigkey is not None:
                                ins.then_inc(sem[sigkey], inc)
                return body

            block.tensor(mk("pe"))
            block.scalar(mk("act"))
            block.vector(mk("dve"))
            block.gpsimd(mk("pool"))
            block.sync(mk("sp"))


ARENA_F32 = 47104


class KB:
    def __init__(self, nc, arena, banks, dbg):
        self.nc = nc
        self.S = Sched(nc)
        self.arena = arena
        self.banks = banks
        self.off = 0
        self.dbg = dbg
        self.dram = {}
        self.eng_rr = 0

    def alloc(self, n, dt=F32, name=""):
        words = (n + 1) // 2 if dt == BF16 else n
        words = (words + 7) // 8 * 8
        assert self.off + words <= ARENA_F32, (name, self.off, words)
        ap = self.arena[:, self.off:self.off + words]
        self.off += words
        if dt == BF16:
            ap = ap.bitcast(BF16)[:, 0:n]
        else:
            ap = ap[:, 0:n]
        return T(ap, name)

    def mark(self):
        return self.off

    def reset(self, mark):
        self.S.barrier()
        self.off = mark

    def pbank(self, i):
        return T(self.banks[i][:, :], f"bank{i}")

    def phalf(self, i):
        return T(self.banks[i // 2][:, (i % 2) * 256:(i % 2 + 1) * 256], f"half{i}")

    def dram_t(self, name, shape, dt):
        kind = "ExternalOutput" if name in self.dbg else "Internal"
        t = self.nc.dram_tensor(name, list(shape), dt, kind=kind).ap()
        self.dram[name] = t
        return t

    def mm(self, out, out_ap, pairs, reads):
        n = len(pairs)
        for j, (l, r) in enumerate(pairs):
            self.S.op("pe", E("matmul", out_ap, lhsT=l, rhs=r, start=(j == 0), stop=(j == n - 1)),
                      reads=reads, writes=[out], sig=(j == n - 1))

    def mm1(self, out, out_ap, l, r, reads, start=True, stop=True, sig=True):
        self.S.op("pe", E("matmul", out_ap, lhsT=l, rhs=r, start=start, stop=stop), reads=reads, writes=[out], sig=sig)

    def op(self, eng, fn, reads, writes):
        self.S.op(eng, fn, reads=reads, writes=writes)

    def load(self, dst, dst_ap, src_ap, q="sp"):
        return self.S.dma(q, dst_ap, src_ap, writes=[dst])

    def store(self, dst_ap, src, src_ap, q="sp"):
        return self.S.dma(q, dst_ap, src_ap, reads=[src])


def E(name, *a, **kw):
    return lambda e: getattr(e, name)(*a, **kw)


def bc(ap, shape):
    return ap.to_broadcast(list(shape))


def _colmap():
    cols = {}
    off = 0

    def add(name, n):
        nonlocal off
        cols[name] = off
        off += n
    add("cT", 8); add("cctxT", 8)
    for L in range(2):
        p = f"l{L}_"
        add(p + "mod_b", 48)
        for nm in ("ln1_g", "ln1_b", "ln2_g", "ln2_b"):
            add(p + nm, 8)
    add("q_norm", 3); add("kv_norm", 2); add("mu", 15)
    add("w0", 8); add("a0", 8)
    for nm in ("k_k", "k_a", "r_k", "gn_w", "gn_b"):
        add(nm, 4)
    add("conv_w", 32)
    add("conv_b", 8)
    add("ga_b", 16); add("gx_b", 16); add("lam", 16)
    return cols, off


COLS, NCOL = _colmap()


def _pcol(v, n):
    return np.ascontiguousarray(np.asarray(v, np.float32).reshape(n, 128).T)


def build_program(dbg=(), stop_after=None, start_layer=0):
    nc = bass.Bass("TRN2", target_bir_lowering=False)
    IN = {}

    def inp(name, shape, dt=F32):
        IN[name] = nc.dram_tensor(name, list(shape), dt, kind="ExternalInput").ap()
        return IN[name]

    xT_in = inp("xT", [1024, NTOK])
    vec = inp("vec", [128, NCOL])
    cm = inp("cmats", [128, 9, 128])
    rope_in = inp("rope", [128, 2, NLAT])
    ind_in = inp("ind8", [128, 8, 8])
    msk_in = inp("rwmask", [128, 2, 1280])
    W = {}
    for L in range(2):
        p = f"l{L}_"
        W[p + "mod_w"] = inp(p + "mod_w", [1024, 6144])
        W[p + "w_out"] = inp(p + "w_out", [1024, 1024])
        W[p + "mlp_w1"] = inp(p + "mlp_w1", [1024, 4096])
        W[p + "mlp_w2"] = inp(p + "mlp_w2", [4096, 1024])
    W["w_in0"] = inp("w_in0", [1024, 2752])
    W["w_uq"] = inp("w_uq", [384, 768]); W["w_uq_rot"] = inp("w_uq_rot", [384, 768])
    W["w_uk"] = inp("w_uk", [256, 512]); W["w_uv"] = inp("w_uv", [256, 512])
    W["w2pad"] = inp("w2pad", [128, 2, 512]); W["a2pad"] = inp("a2pad", [128, 2, 512]); W["g2"] = inp("g2", [128, 512])
    W["w_in1"] = inp("w_in1", [1024, 2048])
    W["ga_w"] = inp("ga_w", [128, 2, 8, 128]); W["gx_w"] = inp("gx_w", [128, 2, 8, 128])
    outT = nc.dram_tensor("outT", [1024, NLAT], F32, kind="ExternalOutput").ap()

    with contextlib.ExitStack() as st:
        arena = st.enter_context(nc.sbuf_tensor("arena", [128, ARENA_F32], F32))
        banks = [st.enter_context(nc.psum_tensor(f"bank{i}", [128, 512], F32)) for i in range(8)]
        kb = KB(nc, arena, banks, set(dbg))
        S = kb.S
        xT = kb.dram_t("xTs", [1024, NTOK], F32)
        F0T = kb.dram_t("F0T", [22, 128, NTOK], F32)
        mixT = kb.dram_t("mixT", [1024, NTOK], BF16)
        RWU = kb.dram_t("RWU", [4, 2, NCH, 128, 7 * 128], F32)
        RWGL = kb.dram_t("RWGL", [4, 2, 128, NCH], F32)
        RWG = kb.dram_t("RWG", [4, 128, NTOK], F32)
        RWBON = kb.dram_t("RWBON", [4, 128, NTOK], F32)
        G1T = kb.dram_t("G1T", [16, 128, NTOK], F32)
        HID = None

        vecs = kb.alloc(NCOL, F32, "vecs")
        cmf = kb.alloc(9 * 128, F32, "cmf")
        cmb = kb.alloc(9 * 128, BF16, "cmb")
        modv = [kb.alloc(96, F32, f"modv{L}") for L in range(2)]
        mod1p = [kb.alloc(96, F32, f"mod1p{L}") for L in range(2)]
        kb.load(vecs, vecs.ap, vec[:, :])
        kb.load(cmf, cmf.ap, cm.rearrange("p a b -> p (a b)"))
        kb.load(cmb, cmb.ap, cm.rearrange("p a b -> p (a b)"), q="pool")
        PERSIST = kb.mark()

        def V(name, i=0, n=1):
            o = COLS[name] + i
            return vecs.ap[:, o:o + n]

        def CMF(i):
            return cmf.ap[:, i * 128:(i + 1) * 128]

        def CMB(i):
            return cmb.ap[:, i * 128:(i + 1) * 128]

        def MOD(L, j, k, which):
            o = (j * 8 + k) * 2 + which
            return modv[L].ap[:, o:o + 1]

        def MOD1P(L, j, k, which):
            o = (j * 8 + k) * 2 + which
            return mod1p[L].ap[:, o:o + 1]

        def xview(t):
            return t.rearrange("(k p) n -> p k n", p=128)


        def stage_mod(L):
            p = f"l{L}_"
            m0 = kb.mark()
            cs = kb.alloc(16, F32, "cs")
            mwb = [kb.alloc(8 * 512, F32, f"mw{i}") for i in range(2)]
            ps = kb.pbank(0)
            csv = cs.ap.rearrange("p (k w) -> p k w", w=2)
            kb.op("act", E("activation", out=csv[:, :, 0], in_=V("cT", 0, 8), func=AF.Silu), [vecs], [cs])
            kb.op("act", E("activation", out=csv[:, :, 1], in_=V("cctxT", 0, 8), func=AF.Silu), [vecs], [cs])
            mwv = W[p + "mod_w"].rearrange("(k p) n -> p k n", p=128)
            for piece in range(12):
                mw = mwb[piece % 2]
                mw3 = mw.ap.rearrange("p (k n) -> p k n", k=8)
                kb.load(mw, mw3, mwv[:, :, piece * 512:(piece + 1) * 512], q="sp")
                for j in range(4):
                    oc = piece * 4 + j
                    kb.mm(ps, ps.ap[:, oc * 2:oc * 2 + 2],
                          [(mw3[:, k, j * 128:(j + 1) * 128], csv[:, k, :]) for k in range(8)], [mw, cs])
            mv3 = modv[L].ap.rearrange("p (a w) -> p a w", w=2)
            kb.op("dve", E("tensor_tensor", out=mv3, in0=ps.ap[:, 0:96].rearrange("p (a w) -> p a w", w=2),
                                                    in1=bc(V(p + "mod_b", 0, 48).rearrange("p (a o) -> p a o", o=1), [128, 48, 2]), op=ALU.add),
                  [ps, vecs], [modv[L]])
            kb.op("dve", E("tensor_scalar", out=mod1p[L].ap, in0=modv[L].ap, scalar1=1.0, scalar2=None, op0=ALU.add),
                  [modv[L]], [mod1p[L]])
            kb.reset(m0)

        def modulate_tile(L, jsh, jsc, xt3, ht3, n, which, xT_T, hT_T):
            for k in range(8):
                if k % 2 == 0:
                    kb.op("act", E("activation", out=ht3[:, k, :n], in_=xt3[:, k, :n], func=AF.Identity,
                                                             scale=MOD1P(L, jsc, k, which), bias=MOD(L, jsh, k, which)),
                          [xT_T, modv[L], mod1p[L]], [hT_T])
                else:
                    kb.op("dve", E("tensor_scalar", out=ht3[:, k, :n], in0=xt3[:, k, :n], scalar1=MOD1P(L, jsc, k, which),
                                                                scalar2=MOD(L, jsh, k, which), op0=ALU.mult, op1=ALU.add),
                          [xT_T, modv[L], mod1p[L]], [hT_T])

        def load_wbf(dst, dst3, src, nk, ncols, cpp=None):
            sv = src.rearrange("(k p) n -> p k n", p=128)
            for k in range(nk):
                kb.load(dst, dst3[:, k, :], sv[:, k, :], q="pool")

        def stage_win(L, src_x, ncolsW, wname, chunks, dstT, gelu_chunks=()):
            m0 = kb.mark()
            wt = kb.alloc(8 * ncolsW, BF16, "w_in")
            w3 = wt.ap.rearrange("p (k n) -> p k n", k=8)
            load_wbf(wt, w3, W[wname], 8, ncolsW)
            xb = [kb.alloc(8 * 512, F32, f"xb{i}") for i in range(2)]
            hb = [kb.alloc(8 * 512, BF16, f"hb{i}") for i in range(2)]
            stg = [kb.alloc(512, F32, f"stg{i}") for i in range(4)]
            pb = [kb.pbank(i) for i in range(4)]
            xv = xview(src_x)
            it = 0
            for ti, (t0, n) in enumerate(TILES):
                which = 1 if ti == 0 else 0
                xt = xb[ti % 2]; ht = hb[ti % 2]
                xt3 = xt.ap.rearrange("p (k n) -> p k n", k=8); ht3 = ht.ap.rearrange("p (k n) -> p k n", k=8)
                kb.load(xt, xt3[:, :, :n], xv[:, :, t0:t0 + n])
                modulate_tile(L, 0, 1, xt3, ht3, n, which, xt, ht)
                for ci, (c0, M) in enumerate(chunks):
                    ps = pb[it % 4]; sg = stg[it % 4]
                    kb.mm(ps, ps.ap[:M, :n], [(w3[:, k, c0:c0 + M], ht3[:, k, :n]) for k in range(8)], [wt, ht])
                    if ci in gelu_chunks:
                        kb.op("act", E("activation", out=sg.ap[:M, :n], in_=ps.ap[:M, :n], func=AF.Gelu), [ps], [sg])
                    elif it % 2 == 0:
                        kb.op("act", E("copy", out=sg.ap[:M, :n], in_=ps.ap[:M, :n]), [ps], [sg])
                    else:
                        kb.op("dve", E("tensor_copy", out=sg.ap[:M, :n], in_=ps.ap[:M, :n]), [ps], [sg])
                    kb.store(dstT[ci, 0:M, t0:t0 + n], sg, sg.ap[:M, :n])
                    it += 1
            kb.reset(m0)

        def ln_tile(zt, zt3, n, gname, bname, dst3_dram, scr, pbs):
            ps_m, ps_q = pbs
            zb = scr["zb"]; zb3 = zb.ap.rearrange("p (k n) -> p k n", k=8)
            kb.op("act", E("activation", out=zb3[:, :, :n], in_=zt3[:, :, :n], func=AF.Square), [zt], [zb])
            kb.mm(ps_m, ps_m.ap[:, :n], [(CMF(1), zt3[:, k, :n]) for k in range(8)], [zt, cmf])
            kb.mm(ps_q, ps_q.ap[:, :n], [(CMB(1), zb3[:, k, :n]) for k in range(8)], [zb, cmb])
            mean = scr["mean"]; rstd = scr["rstd"]; tmp = scr["tmp"]
            kb.op("act", E("copy", out=mean.ap[:, :n], in_=ps_m.ap[:, :n]), [ps_m], [mean])
            kb.op("act", E("activation", out=tmp.ap[:, :n], in_=ps_m.ap[:, :n], func=AF.Square), [ps_m], [tmp])
            kb.op("dve", E("tensor_tensor", out=tmp.ap[:, :n], in0=ps_q.ap[:, :n], in1=tmp.ap[:, :n], op=ALU.subtract), [ps_q, tmp], [tmp])
            kb.op("dve", E("tensor_scalar", out=tmp.ap[:, :n], in0=tmp.ap[:, :n], scalar1=0.0, scalar2=None, op0=ALU.max), [tmp], [tmp])
            kb.op("act", E("activation", out=tmp.ap[:, :n], in_=tmp.ap[:, :n], func=AF.Sqrt, bias=1e-5), [tmp], [tmp])
            kb.op("dve", E("reciprocal", out=rstd.ap[:, :n], in_=tmp.ap[:, :n]), [tmp], [rstd])
            mb = bc(mean.ap[:, :n].rearrange("p (o n) -> p o n", o=1), [128, 8, n])
            rb = bc(rstd.ap[:, :n].rearrange("p (o n) -> p o n", o=1), [128, 8, n])
            kb.op("dve", E("tensor_tensor", out=zt3[:, :, :n], in0=zt3[:, :, :n], in1=mb, op=ALU.subtract), [zt, mean], [zt])
            kb.op("pool", E("tensor_tensor", out=zt3[:, :, :n], in0=zt3[:, :, :n], in1=rb, op=ALU.mult), [zt, rstd], [zt])
            for k in range(8):
                eng = "act" if k % 2 == 0 else "dve"
                if eng == "act":
                    kb.op("act", E("activation", out=zt3[:, k, :n], in_=zt3[:, k, :n], func=AF.Identity,
                                                             scale=V(gname, k), bias=V(bname, k)), [zt, vecs], [zt])
                else:
                    kb.op("dve", E("tensor_scalar", out=zt3[:, k, :n], in0=zt3[:, k, :n], scalar1=V(gname, k), scalar2=V(bname, k),
                                                                op0=ALU.mult, op1=ALU.add), [zt, vecs], [zt])
            kb.store(dst3_dram, zt, zt3[:, :, :n])

        def stage_wout(L, src_x, dst_x, tiles):
            p = f"l{L}_"
            m0 = kb.mark()
            wt = kb.alloc(8 * 1024, BF16, "w_out")
            w3 = wt.ap.rearrange("p (k n) -> p k n", k=8)
            load_wbf(wt, w3, W[p + "w_out"], 8, 1024)
            xb = [kb.alloc(8 * 512, F32, f"xb{i}") for i in range(2)]
            mb_ = [kb.alloc(8 * 512, BF16, f"mb{i}") for i in range(2)]
            zt_ = [kb.alloc(8 * 512, F32, f"z{i}") for i in range(2)]
            scr = {"zb": kb.alloc(8 * 512, BF16, "zb"), "mean": kb.alloc(512, F32, "mean"), "rstd": kb.alloc(512, F32, "rstd"), "tmp": kb.alloc(512, F32, "tmp")}
            pb = [kb.pbank(i) for i in range(4)]
            pst = (kb.pbank(4), kb.pbank(5))
            xv = xview(src_x); dv = xview(dst_x); mv = mixT.rearrange("(k p) n -> p k n", p=128)
            it = 0
            for (ti, t0, n, d0) in tiles:
                which = 1 if ti == 0 else 0
                xt = xb[ti % 2]; mt = mb_[ti % 2]; zt = zt_[ti % 2]
                xt3 = xt.ap.rearrange("p (k n) -> p k n", k=8); mt3 = mt.ap.rearrange("p (k n) -> p k n", k=8); zt3 = zt.ap.rearrange("p (k n) -> p k n", k=8)
                kb.load(xt, xt3[:, :, :n], xv[:, :, t0:t0 + n])
                kb.load(mt, mt3[:, :, :n], mv[:, :, t0:t0 + n], q="act")
                kb.op("pool", E("tensor_scalar", out=xt3[:, :, :n], in0=xt3[:, :, :n], scalar1=ALPHA, scalar2=None, op0=ALU.mult), [xt], [xt])
                for oc in range(8):
                    ps = pb[it % 4]; it += 1
                    kb.mm(ps, ps.ap[:, :n], [(w3[:, k, oc * 128:(oc + 1) * 128], mt3[:, k, :n]) for k in range(8)], [wt, mt])
                    kb.op("dve", E("scalar_tensor_tensor", out=zt3[:, oc, :n], in0=ps.ap[:, :n], scalar=MOD(L, 2, oc, which),
                                                                                in1=xt3[:, oc, :n], op0=ALU.mult, op1=ALU.add), [ps, xt, modv[L]], [zt])
                ln_tile(zt, zt3, n, p + "ln1_g", p + "ln1_b", dv[:, :, d0:d0 + n], scr, pst)
            kb.reset(m0)

        def stage_mlp(L, src_x, dst_x, tiles, TW=256):
            p = f"l{L}_"
            m0 = kb.mark()
            w1 = kb.alloc(8 * 4096, BF16, "w1"); w13 = w1.ap.rearrange("p (k n) -> p k n", k=8)
            w2 = kb.alloc(32 * 1024, BF16, "w2"); w23 = w2.ap.rearrange("p (k n) -> p k n", k=32)
            load_wbf(w1, w13, W[p + "mlp_w1"], 8, 4096)
            load_wbf(w2, w23, W[p + "mlp_w2"], 32, 1024)
            xt = kb.alloc(8 * TW, F32, "xt"); xt3 = xt.ap.rearrange("p (k n) -> p k n", k=8)
            ht = kb.alloc(8 * TW, BF16, "ht"); ht3 = ht.ap.rearrange("p (k n) -> p k n", k=8)
            hid = kb.alloc(32 * TW, BF16, "hid"); hid3 = hid.ap.rearrange("p (k n) -> p k n", k=32)
            rl = [kb.alloc(TW, F32, f"rl{i}") for i in range(3)]
            zt = kb.alloc(8 * TW, F32, "zt"); zt3 = zt.ap.rearrange("p (k n) -> p k n", k=8)
            scr = {"zb": kb.alloc(8 * TW, BF16, "zb"), "mean": kb.alloc(TW, F32, "mean"), "rstd": kb.alloc(TW, F32, "rstd"), "tmp": kb.alloc(TW, F32, "tmp")}
            pb = [kb.pbank(i) for i in range(4)]
            pst = (kb.pbank(4), kb.pbank(5))
            xv = xview(src_x); dv = xview(dst_x)
            it = 0
            for (ti, t0, n, d0) in tiles:
                which = 1 if ti == 0 else 0
                kb.load(xt, xt3[:, :, :n], xv[:, :, t0:t0 + n])
                modulate_tile(L, 3, 4, xt3, ht3, n, which, xt, ht)
                for fc in range(32):
                    ps = pb[it % 4]; r = rl[it % 3]; it += 1
                    kb.mm(ps, ps.ap[:, :n], [(w13[:, k, fc * 128:(fc + 1) * 128], ht3[:, k, :n]) for k in range(8)], [w1, ht])
                    kb.op("act", E("activation", out=r.ap[:, :n], in_=ps.ap[:, :n], func=AF.Relu), [ps], [r])
                    eng = "dve" if fc % 2 == 0 else "pool"
                    kb.op(eng, E("tensor_tensor", out=hid3[:, fc, :n], in0=r.ap[:, :n], in1=r.ap[:, :n], op=ALU.mult), [r], [hid])
                kb.op("pool", E("tensor_scalar", out=xt3[:, :, :n], in0=xt3[:, :, :n], scalar1=ALPHA, scalar2=None, op0=ALU.mult), [xt, ht], [xt])
                for oc in range(8):
                    ps = pb[it % 4]; it += 1
                    kb.mm(ps, ps.ap[:, :n], [(w23[:, fc, oc * 128:(oc + 1) * 128], hid3[:, fc, :n]) for fc in range(32)], [w2, hid])
                    kb.op("dve", E("scalar_tensor_tensor", out=zt3[:, oc, :n], in0=ps.ap[:, :n], scalar=MOD(L, 5, oc, which),
                                                                                in1=xt3[:, oc, :n], op0=ALU.mult, op1=ALU.add), [ps, xt, modv[L]], [zt])
                ln_tile(zt, zt3, n, p + "ln2_g", p + "ln2_b", dv[:, :, d0:d0 + n], scr, pst)
            kb.reset(m0)


        def stage_mla():
            m0 = kb.mark()
            qT = kb.alloc(8 * NTOK, BF16, "qT"); qT3 = qT.ap.rearrange("p (h n) -> p h n", h=8)
            kT = kb.alloc(8 * NTOK, BF16, "kT"); kT3 = kT.ap.rearrange("p (h n) -> p h n", h=8)
            Va = kb.alloc(18 * 8 * 65, BF16, "Vaug"); Va4 = Va.ap.rearrange("p (c h e) -> p c h e", c=18, h=8)
            negM = kb.alloc(8, F32, "negM")
            qmx = kb.alloc(1, F32, "qmx"); kmx = kb.alloc(1, F32, "kmx")
            m1 = kb.mark()
            wq = kb.alloc(3 * 768, BF16, "wq"); wq3 = wq.ap.rearrange("p (k n) -> p k n", k=3)
            wr = kb.alloc(3 * 768, BF16, "wr"); wr3 = wr.ap.rearrange("p (k n) -> p k n", k=3)
            wk = kb.alloc(2 * 512, BF16, "wk"); wk3 = wk.ap.rearrange("p (k n) -> p k n", k=2)
            wv = kb.alloc(2 * 512, BF16, "wv"); wv3 = wv.ap.rearrange("p (k n) -> p k n", k=2)
            load_wbf(wq, wq3, W["w_uq"], 3, 768); load_wbf(wr, wr3, W["w_uq_rot"], 3, 768)
            load_wbf(wk, wk3, W["w_uk"], 2, 512); load_wbf(wv, wv3, W["w_uv"], 2, 512)
            rope = kb.alloc(2 * NLAT, F32, "rope"); rope3 = rope.ap.rearrange("p (a n) -> p a n", a=2)
            kb.load(rope, rope.ap, rope_in.rearrange("p a n -> p (a n)"))
            ind = kb.alloc(64, BF16, "ind"); ind3 = ind.ap.rearrange("p (a b) -> p a b", a=8)
            kb.load(ind, ind.ap, ind_in.rearrange("p a b -> p (a b)"), q="pool")
            kb.op("pool", E("memset", Va4[:, :, :, 64:65], 1.0), [], [Va])
            kb.op("pool", E("memset", qmx.ap[0:8, :], 0.0), [], [qmx])
            kb.op("pool", E("memset", kmx.ap[0:8, :], 0.0), [], [kmx])
            fq = kb.alloc(3 * 512, F32, "fq"); fq3 = fq.ap.rearrange("p (k n) -> p k n", k=3)
            fkv = kb.alloc(2 * 512, F32, "fkv"); fkv3 = fkv.ap.rearrange("p (k n) -> p k n", k=2)
            krp = kb.alloc(512, F32, "krp"); krr = kb.alloc(512, F32, "krr")
            sq = kb.alloc(3 * 512, F32, "sq"); sq3 = sq.ap.rearrange("p (k n) -> p k n", k=3)
            sd = kb.alloc(512, F32, "sd"); rs = kb.alloc(512, F32, "rs")
            qn = kb.alloc(3 * 512, BF16, "qn"); qn3 = qn.ap.rearrange("p (k n) -> p k n", k=3)
            ckv = kb.alloc(2 * 512, BF16, "ckv"); ckv3 = ckv.ap.rearrange("p (k n) -> p k n", k=2)
            t1 = [kb.alloc(512, F32, f"t1_{i}") for i in range(2)]
            t2 = [kb.alloc(512, F32, f"t2_{i}") for i in range(2)]
            krf = kb.alloc(512, F32, "krf")
            sqq = kb.alloc(8 * 512, BF16, "sqq"); sqq3 = sqq.ap.rearrange("p (h n) -> p h n", h=8)
            nmx = kb.alloc(1, F32, "nmx")
            pA = [kb.pbank(i) for i in range(3)]
            pR = [kb.pbank(3), kb.pbank(4)]
            pK = [kb.pbank(5), kb.pbank(6)]
            pN = kb.pbank(7)
            ia = 0; ir = 0; ik = 0
            for ti, (t0, n) in enumerate(TILES):
                lat = ti > 0
                kb.load(fq, fq3[:, :, :n], F0T[0:3, :, t0:t0 + n].rearrange("k p n -> p k n"))
                kb.load(fkv, fkv3[:, :, :n], F0T[3:5, :, t0:t0 + n].rearrange("k p n -> p k n"))
                kb.load(krp, krp.ap[64:96, :n], F0T[5, 64:96, t0:t0 + n])
                if lat:
                    kb.load(krr, krr.ap[64:96, :n], F0T[6, 64:96, t0:t0 + n])
                kb.op("act", E("activation", out=sq3[:, :, :n], in_=fq3[:, :, :n], func=AF.Square), [fq], [sq])
                kb.mm(pN, pN.ap[:, :n], [(CMF(2), sq3[:, k, :n]) for k in range(3)], [sq, cmf])
                kb.op("act", E("activation", out=sd.ap[:, :n], in_=pN.ap[:, :n], func=AF.Sqrt, bias=1e-6), [pN], [sd])
                kb.op("dve", E("reciprocal", out=rs.ap[:, :n], in_=sd.ap[:, :n]), [sd], [rs])
                for k in range(3):
                    kb.op("dve", E("scalar_tensor_tensor", out=qn3[:, k, :n], in0=fq3[:, k, :n], scalar=V("q_norm", k), in1=rs.ap[:, :n],
                                                                      op0=ALU.mult, op1=ALU.mult), [fq, rs, vecs], [qn])
                if "DBGT" in kb.dbg and ti == 0:
                    dt_ = kb.dram_t("DBGT", [4, 128, 3 * 512], F32)
                    kb.store(dt_[0], sq, sq.ap); kb.store(dt_[1, :, 0:512], sd, sd.ap); kb.store(dt_[2, :, 0:512], rs, rs.ap)
                    kb.store(dt_[3], fq, fq.ap)
                    dt2 = kb.dram_t("DBGT2", [128, 3 * 512], BF16)
                    kb.store(dt2[:, :], qn, qn.ap)
                kb.op("act", E("activation", out=sq3[:, 0:2, :n], in_=fkv3[:, :, :n], func=AF.Square), [fkv, qn], [sq])
                kb.mm(pN, pN.ap[:, :n], [(CMF(3), sq3[:, k, :n]) for k in range(2)], [sq, cmf])
                kb.op("act", E("activation", out=sd.ap[:, :n], in_=pN.ap[:, :n], func=AF.Sqrt, bias=1e-6), [pN], [sd])
                kb.op("dve", E("reciprocal", out=rs.ap[:, :n], in_=sd.ap[:, :n]), [sd], [rs])
                for k in range(2):
                    kb.op("dve", E("scalar_tensor_tensor", out=ckv3[:, k, :n], in0=fkv3[:, k, :n], scalar=V("kv_norm", k), in1=rs.ap[:, :n],
                                                                      op0=ALU.mult, op1=ALU.mult), [fkv, rs, vecs], [ckv])
                if lat:
                    c0 = t0 - NCTX
                    kb.op("dve", E("tensor_tensor", out=krf.ap[64:96, :n], in0=krp.ap[64:96, :n], in1=rope3[64:96, 0, c0:c0 + n], op=ALU.mult), [krp, rope], [krf])
                    kb.op("pool", E("tensor_tensor", out=krr.ap[64:96, :n], in0=krr.ap[64:96, :n], in1=rope3[64:96, 1, c0:c0 + n], op=ALU.mult), [krr, rope], [krr])
                    kb.op("pool", E("tensor_tensor", out=krf.ap[64:96, :n], in0=krf.ap[64:96, :n], in1=krr.ap[64:96, :n], op=ALU.add), [krf, krr], [krf])
                    ksrc = krf
                else:
                    ksrc = krp
                kb.op("pool", E("tensor_copy", out=kT3[64:96, :, t0:t0 + n],
                                                                 in_=bc(ksrc.ap[64:96, :n].rearrange("p (o n) -> p o n", o=1), [32, 8, n])), [ksrc], [kT])
                for h in range(8):
                    pq = pA[ia % 3]; ia += 1
                    kb.mm(pq, pq.ap[0:96, :n], [(wq3[:, k, h * 96:(h + 1) * 96], qn3[:, k, :n]) for k in range(3)], [wq, qn])
                    kb.op("act", E("copy", out=qT3[0:64, h, t0:t0 + n], in_=pq.ap[0:64, :n]), [pq], [qT])
                    if lat:
                        pr = pR[ir % 2]; a1 = t1[ir % 2]; a2 = t2[ir % 2]; ir += 1
                        kb.mm(pr, pr.ap[0:96, :n], [(wr3[:, k, h * 96:(h + 1) * 96], qn3[:, k, :n]) for k in range(3)], [wr, qn])
                        kb.op("dve", E("tensor_tensor", out=a1.ap[64:96, :n], in0=pr.ap[64:96, :n], in1=rope3[64:96, 1, c0:c0 + n], op=ALU.mult), [pr, rope], [a1])
                        kb.op("dve", E("tensor_tensor", out=a2.ap[64:96, :n], in0=pq.ap[64:96, :n], in1=rope3[64:96, 0, c0:c0 + n], op=ALU.mult), [pq, rope], [a2])
                        kb.op("pool", E("tensor_tensor", out=qT3[64:96, h, t0:t0 + n], in0=a1.ap[64:96, :n], in1=a2.ap[64:96, :n], op=ALU.add), [a1, a2], [qT])
                    else:
                        kb.op("act", E("copy", out=qT3[64:96, h, t0:t0 + n], in_=pq.ap[64:96, :n]), [pq], [qT])
                    pk = pK[ik % 2]; ik += 1
                    kb.mm(pk, pk.ap[0:64, :n], [(wk3[:, k, h * 64:(h + 1) * 64], ckv3[:, k, :n]) for k in range(2)], [wk, ckv])
                    kb.op("dve", E("tensor_copy", out=kT3[0:64, h, t0:t0 + n], in_=pk.ap[0:64, :n]), [pk], [kT])
                for j in range(n // 128):
                    pk = pK[ik % 2]; ik += 1
                    kc = (t0 + j * 128) // 128
                    kb.mm(pk, pk.ap[:, :], [(ckv3[:, k, j * 128:(j + 1) * 128], wv3[:, k, :]) for k in range(2)], [wv, ckv])
                    kb.op("act", E("copy", out=Va4[:, kc, :, 0:64], in_=pk.ap.rearrange("p (h e) -> p h e", h=8)), [pk], [Va])
                for (src3, src, mx) in ((qT3, qT, qmx), (kT3, kT, kmx)):
                    kb.op("act", E("activation", out=sqq3[0:96, :, :n], in_=src3[0:96, :, t0:t0 + n], func=AF.Square), [src], [sqq])
                    pk = pK[ik % 2]; ik += 1
                    kb.mm(pk, pk.ap[0:8, :n], [(ind3[0:96, h, :], sqq3[0:96, h, :n]) for h in range(8)], [ind, sqq])
                    kb.op("dve", E("tensor_reduce", out=nmx.ap[0:8, :], in_=pk.ap[0:8, :n], axis=AX.X, op=ALU.max), [pk], [nmx])
                    kb.op("dve", E("tensor_tensor", out=mx.ap[0:8, :], in0=mx.ap[0:8, :], in1=nmx.ap[0:8, :], op=ALU.max), [mx, nmx], [mx])
            dg = kb.alloc(8, F32, "dg")
            kb.op("dve", E("tensor_tensor", out=nmx.ap[0:8, :], in0=qmx.ap[0:8, :], in1=kmx.ap[0:8, :], op=ALU.mult), [qmx, kmx], [nmx])
            kb.op("act", E("activation", out=nmx.ap[0:8, :], in_=nmx.ap[0:8, :], func=AF.Sqrt), [nmx], [nmx])
            kb.op("dve", E("tensor_scalar", out=dg.ap[0:8, :], in0=CMF(0)[0:8, 0:8], scalar1=nmx.ap[0:8, 0:1], scalar2=-1.03 * SCALE, op0=ALU.mult, op1=ALU.mult),
                  [nmx, cmf], [dg])
            pk = pK[ik % 2]; ik += 1
            kb.mm(pk, pk.ap[:, 0:8], [(CMF(7)[0:8, :], dg.ap[0:8, :])], [cmf, dg])
            kb.op("dve", E("tensor_copy", out=negM.ap, in_=pk.ap[:, 0:8]), [pk], [negM])
            if "DBGQK" in kb.dbg:
                dq = kb.dram_t("DBGQK", [2, 128, 8 * NTOK], BF16)
                kb.store(dq[0], qT, qT.ap); kb.store(dq[1], kT, kT.ap)
                dv_ = kb.dram_t("DBGV", [128, 18 * 8 * 65], BF16)
                kb.store(dv_[:, :], Va, Va.ap)
                dn = kb.dram_t("DBGNM", [128, 8], F32)
                kb.store(dn[:, :], negM, negM.ap)
            kb.reset(m1)
            PT = [kb.alloc(512, BF16, f"PT{i}") for i in range(4)]
            osb = [kb.alloc(512, F32, f"osb{i}") for i in range(2)]
            rec = [kb.alloc(512, F32, f"rec{i}") for i in range(2)]
            att = [kb.alloc(512, BF16, f"att{i}") for i in range(2)]
            pS = [kb.pbank(i) for i in range(4)]
            pO = [kb.pbank(4), kb.pbank(5)]
            pB = [kb.pbank(6), kb.pbank(7)]
            isc = 0; io = 0
            for h in range(8):
                for ti, (t0, n) in enumerate(TILES):
                    nk = 2 if ti == 0 else 18
                    po = pO[io % 2]; ob = osb[io % 2]; rc = rec[io % 2]; ao = att[io % 2]; pb_ = pB[io % 2]; io += 1
                    slots = {}
                    for i in range(nk + 2):
                        if i < nk:
                            ps = pS[isc % 4]; pt = PT[isc % 4]; isc += 1
                            slots[i] = (ps, pt)
                            kb.mm1(ps, ps.ap[:, :n], kT3[0:96, h, i * 128:(i + 1) * 128], qT3[0:96, h, t0:t0 + n], [kT, qT])
                            kb.op("act", E("activation", out=pt.ap[:, :n], in_=ps.ap[:, :n], func=AF.Exp, scale=SCALE, bias=negM.ap[:, h:h + 1]),
                                  [ps, negM], [pt])
                        if i >= 2:
                            j = i - 2
                            ps, pt = slots.pop(j)
                            kb.mm1(po, po.ap[0:65, :n], Va4[:, j, h, :], pt.ap[:, :n], [Va, pt], start=(j == 0), stop=(j == nk - 1), sig=(j == nk - 1))
                    kb.op("dve", E("tensor_copy", out=ob.ap[0:65, :n], in_=po.ap[0:65, :n]), [po], [ob])
                    kb.mm(pb_, pb_.ap[0:64, :n], [(CMF(6)[0:65, 0:64], ob.ap[0:65, :n])], [cmf, ob])
                    kb.op("dve", E("reciprocal", out=rc.ap[0:64, :n], in_=pb_.ap[0:64, :n]), [pb_], [rc])
                    kb.op("pool", E("tensor_tensor", out=ao.ap[0:64, :n], in0=ob.ap[0:64, :n], in1=rc.ap[0:64, :n], op=ALU.mult), [ob, rc], [ao])
                    kb.store(mixT[h * 64:(h + 1) * 64, t0:t0 + n], ao, ao.ap[0:64, :n])
            kb.reset(m0)


        def stage_lru():
            m0 = kb.mark()
            gaw = kb.alloc(2048, BF16, "gaw"); gaw4 = gaw.ap.rearrange("p (d k e) -> p d k e", d=2, k=8)
            gxw = kb.alloc(2048, BF16, "gxw"); gxw4 = gxw.ap.rearrange("p (d k e) -> p d k e", d=2, k=8)
            kb.load(gaw, gaw.ap, W["ga_w"].rearrange("p d k e -> p (d k e)"), q="pool")
            kb.load(gxw, gxw.ap, W["gx_w"].rearrange("p d k e -> p (d k e)"), q="pool")
            cl = kb.alloc(16, F32, "cl")
            kb.op("act", E("activation", out=cl.ap, in_=V("lam", 0, 16), func=AF.Exp, scale=-1.0), [vecs], [cl])
            kb.op("act", E("activation", out=cl.ap, in_=cl.ap, func=AF.Ln, bias=1.0), [cl], [cl])
            kb.op("dve", E("tensor_scalar", out=cl.ap, in0=cl.ap, scalar1=-8.0, scalar2=None, op0=ALU.mult), [cl], [cl])
            XW = 2310
            xh = kb.alloc(XW, F32, "xh")
            kb.op("pool", E("memset", xh.ap, 0.0), [], [xh])
            xc = kb.alloc(NTOK, F32, "xc"); xcb = kb.alloc(NTOK, BF16, "xcb")
            rr = kb.alloc(NTOK, F32, "rr"); ig = kb.alloc(NTOK, F32, "ig"); aa = kb.alloc(NTOK, F32, "aa")
            a2 = kb.alloc(NTOK, F32, "a2"); uu = kb.alloc(NTOK, F32, "uu")
            hd = [kb.alloc(NTOK, F32, f"hd{d}") for d in range(2)]
            gg = kb.alloc(NLAT, F32, "gg"); yy = kb.alloc(NLAT, BF16, "yy")
            pb = [kb.pbank(i) for i in range(4)]
            ip = 0
            segs = [(0, 0, 256), (256, 259, 2048)]
            for c in range(8):
                kb.load(xh, xh.ap[:, 2:258], G1T[8 + c, :, 0:256])
                kb.load(xh, xh.ap[:, 261:2309], G1T[8 + c, :, 256:NTOK])
                kb.load(gg, gg.ap, G1T[c, :, 256:NTOK])
                for (o0, b0, ln_) in segs:
                    kb.op("act", E("activation", out=xc.ap[:, o0:o0 + ln_], in_=xh.ap[:, b0:b0 + ln_], func=AF.Identity,
                                   scale=V("conv_w", 0 * 8 + c), bias=V("conv_b", c)), [xh, vecs], [xc])
                    for j in range(1, 4):
                        kb.op("dve", E("scalar_tensor_tensor", out=xc.ap[:, o0:o0 + ln_], in0=xh.ap[:, b0 + j:b0 + j + ln_], scalar=V("conv_w", j * 8 + c),
                                       in1=xc.ap[:, o0:o0 + ln_], op0=ALU.mult, op1=ALU.add), [xh, xc, vecs], [xc])
                kb.op("pool", E("tensor_copy", out=xcb.ap, in_=xc.ap), [xc], [xcb])
                if "DBGL1" in kb.dbg:
                    kb.store(kb.dram["DBGL1"][0, c], xc, xc.ap)
                for d in range(2):
                    for (t0, n) in TILES:
                        ps = pb[ip % 4]; ip += 1
                        kb.mm(ps, ps.ap[:, :n], [(gaw4[:, d, c, :], xcb.ap[:, t0:t0 + n])], [gaw, xcb])
                        kb.op("act", E("activation", out=rr.ap[:, t0:t0 + n], in_=ps.ap[:, :n], func=AF.Sigmoid, bias=V("ga_b", d * 8 + c)), [ps, vecs], [rr])
                        ps = pb[ip % 4]; ip += 1
                        kb.mm(ps, ps.ap[:, :n], [(gxw4[:, d, c, :], xcb.ap[:, t0:t0 + n])], [gxw, xcb])
                        kb.op("act", E("activation", out=ig.ap[:, t0:t0 + n], in_=ps.ap[:, :n], func=AF.Sigmoid, bias=V("gx_b", d * 8 + c)), [ps, vecs], [ig])
                    kb.op("act", E("activation", out=aa.ap, in_=rr.ap, func=AF.Exp, scale=cl.ap[:, d * 8 + c:d * 8 + c + 1]), [rr, cl], [aa])
                    kb.op("pool", E("tensor_tensor", out=a2.ap, in0=aa.ap, in1=aa.ap, op=ALU.mult), [aa], [a2])
                    kb.op("act", E("activation", out=a2.ap, in_=a2.ap, func=AF.Sqrt, scale=-1.0, bias=1.0), [a2], [a2])
                    kb.op("pool", E("tensor_tensor", out=uu.ap, in0=ig.ap, in1=xc.ap, op=ALU.mult), [ig, xc], [uu])
                    kb.op("dve", E("tensor_tensor", out=uu.ap, in0=uu.ap, in1=a2.ap, op=ALU.mult), [uu, a2], [uu])
                    h_ = hd[d]
                    if d == 0:
                        kb.op("dve", E("tensor_tensor_scan", out=h_.ap, data0=aa.ap, data1=uu.ap, initial=0.0, op0=ALU.mult, op1=ALU.add), [aa, uu], [h_])
                    else:
                        kb.op("dve", E("tensor_tensor_scan", out=h_.ap[:, 0:256][:, ::-1], data0=aa.ap[:, 0:256][:, ::-1], data1=uu.ap[:, 0:256][:, ::-1],
                                       initial=0.0, op0=ALU.mult, op1=ALU.add), [aa, uu], [h_])
                        kb.op("dve", E("tensor_tensor_scan", out=h_.ap[:, 256:NTOK][:, ::-1], data0=aa.ap[:, 256:NTOK][:, ::-1], data1=uu.ap[:, 256:NTOK][:, ::-1],
                                       initial=h_.ap[:, 0:1], op0=ALU.mult, op1=ALU.add), [aa, uu, h_], [h_])
                    if "DBGL1" in kb.dbg:
                        kb.store(kb.dram["DBGL1"][1 + d, c], h_, h_.ap)
                kb.op("pool", E("tensor_tensor", out=hd[0].ap[:, 256:NTOK], in0=hd[0].ap[:, 256:NTOK], in1=hd[1].ap[:, 256:NTOK], op=ALU.add), [hd[0], hd[1]], [hd[0]])
                kb.op("dve", E("tensor_tensor", out=yy.ap, in0=hd[0].ap[:, 256:NTOK], in1=gg.ap, op=ALU.mult), [hd[0], gg], [yy])
                kb.store(mixT[c * 128:(c + 1) * 128, 256:NTOK], yy, yy.ap)
            kb.reset(m0)


        def stage_rwkv():
            m0 = kb.mark()
            W_ = 2312
            ORDER = [list(range(NCH)), [1, 0] + list(range(NCH - 1, 1, -1))]
            import os
            SKIP_RWA = bool(os.environ.get("ONLY_RWB"))
            X = [kb.alloc(W_, F32, f"X{i}") for i in range(5)]
            tw = kb.alloc(NTOK, F32, "tw"); alr = kb.alloc(NTOK, F32, "alr"); sg = kb.alloc(NTOK, F32, "sg")
            rS = kb.alloc(NTOK, F32, "rS"); kS = kb.alloc(NTOK, F32, "kS"); vS = kb.alloc(NTOK, F32, "vS")
            kk = kb.alloc(NTOK, F32, "kk"); pbn = kb.alloc(NTOK, F32, "pbn")
            msk = kb.alloc(NTOK + 1, F32, "msk")
            ob = [kb.alloc(NTOK, F32, f"ob{i}") for i in range(3)]
            stg = [kb.alloc(512, F32, f"stg{i}") for i in range(2)]
            mud = kb.alloc(30, F32, "mud"); oka = kb.alloc(4, F32, "oka"); glt = kb.alloc(NCH, F32, "glt")
            w2p = kb.alloc(1024, F32, "w2p"); a2p = kb.alloc(1024, F32, "a2p"); g2t = kb.alloc(512, F32, "g2t")
            w2p3 = w2p.ap.rearrange("p (d n) -> p d n", d=2); a2p3 = a2p.ap.rearrange("p (d n) -> p d n", d=2)
            kb.load(w2p, w2p.ap, W["w2pad"].rearrange("p d n -> p (d n)"))
            kb.load(a2p, a2p.ap, W["a2pad"].rearrange("p d n -> p (d n)"))
            kb.load(g2t, g2t.ap, W["g2"][:, :])
            pb = [kb.pbank(i) for i in range(4)]
            ipb = [0]

            def nps():
                p_ = pb[ipb[0] % 4]; ipb[0] += 1
                return p_
            kb.op("dve", E("tensor_scalar", out=mud.ap[:, 0:15], in0=V("mu", 0, 15), scalar1=-1.0, scalar2=1.0, op0=ALU.mult, op1=ALU.add), [vecs], [mud])
            kb.op("dve", E("tensor_scalar", out=mud.ap[:, 15:30], in0=V("mu", 0, 15), scalar1=0.5, scalar2=None, op0=ALU.mult), [vecs, mud], [mud])
            kb.op("dve", E("tensor_scalar", out=oka.ap, in0=V("k_a", 0, 4), scalar1=-1.0, scalar2=1.0, op0=ALU.mult, op1=ALU.add), [vecs], [oka])
            kb.op("pool", E("memset", msk.ap, 1.0), [], [msk])
            kb.op("pool", E("memset", msk.ap[:, 0:NTOK + 1:128], 0.0), [msk], [msk])
            kb.op("pool", E("memset", X[1].ap, 0.0), [], [X[1]])
            maskf = msk.ap[:, 0:NTOK]; maskr = msk.ap[:, 1:NTOK + 1]

            def shift(j, dst, func=None):
                fc, fh, ss, uu = X[0], X[1], X[2], X[3]
                kb.load(fc, fc.ap[:, 0:NTOK], F0T[7 + j, :, :])
                kb.load(fh, fh.ap[:, 1:257], F0T[7 + j, :, 0:256])
                kb.load(fh, fh.ap[:, 258:2306], F0T[7 + j, :, 256:NTOK])
                kb.op("pool", E("tensor_tensor", out=ss.ap[:, 0:256], in0=fh.ap[:, 0:256], in1=fh.ap[:, 2:258], op=ALU.add), [fh], [ss])
                kb.op("pool", E("tensor_tensor", out=ss.ap[:, 256:NTOK], in0=fh.ap[:, 257:2305], in1=fh.ap[:, 259:2307], op=ALU.add), [fh, ss], [ss])
                kb.op("act", E("activation", out=uu.ap[:, 0:NTOK], in_=fc.ap[:, 0:NTOK], func=AF.Identity, scale=mud.ap[:, j:j + 1]), [fc, mud], [uu])
                kb.op("dve", E("scalar_tensor_tensor", out=dst.ap[:, 0:NTOK], in0=ss.ap[:, 0:NTOK], scalar=mud.ap[:, 15 + j:16 + j], in1=uu.ap[:, 0:NTOK],
                               op0=ALU.mult, op1=ALU.add), [ss, uu, mud], [dst])
                if func is not None:
                    kb.op("act", E("activation", out=dst.ap[:, 0:NTOK], in_=dst.ap[:, 0:NTOK], func=func), [dst], [dst])
            if not SKIP_RWA:
                shift(12, tw, AF.Tanh); shift(13, alr); shift(14, sg, AF.Sigmoid)
            RWU7 = [[RWU[hp, d].rearrange("c p (i t) -> p c i t", i=7) for d in range(2)] for hp in range(4)]
            for hp in range(0 if SKIP_RWA else 4):
                shift(hp, rS); shift(4 + hp, kS); shift(8 + hp, vS)
                for d in range(2):
                    kb.store(RWU7[hp][d][:, :, 6, :], vS, vS.ap.rearrange("p (c t) -> p c t", t=128))
                sq = X[4]
                kb.op("act", E("activation", out=kk.ap, in_=kS.ap, func=AF.Identity, scale=V("k_k", hp)), [kS, vecs], [kk])
                kb.op("pool", E("tensor_tensor", out=sq.ap[:, 0:NTOK], in0=kk.ap, in1=kk.ap, op=ALU.mult), [kk], [sq])
                for (t0, n) in TILES:
                    ps = nps()
                    kb.mm(ps, ps.ap[:, :n], [(CMF(4), sq.ap[:, t0:t0 + n])], [cmf, sq])
                    kb.op("act", E("activation", out=X[3].ap[:, t0:t0 + n], in_=ps.ap[:, :n], func=AF.Sqrt, bias=1e-12), [ps], [X[3]])
                kb.op("dve", E("reciprocal", out=X[3].ap[:, 0:NTOK], in_=X[3].ap[:, 0:NTOK]), [X[3]], [X[3]])
                kb.op("pool", E("tensor_tensor", out=kk.ap, in0=kk.ap, in1=X[3].ap[:, 0:NTOK], op=ALU.mult), [kk, X[3]], [kk])
                if "DBGRW" in kb.dbg:
                    kb.store(kb.dram["DBGRW"][0, hp], kk, kk.ap)
                iob = 0
                for d in range(2):
                    x1, x2, x3, x4, x5 = [X[i] for i in range(5)]
                    x1a = x1.ap[:, 0:NTOK]; x2a = x2.ap[:, 0:NTOK]; x3a = x3.ap[:, 0:NTOK]; x4a = x4.ap[:, 0:NTOK]; x5a = x5.ap[:, 0:NTOK]
                    for (t0, n) in TILES:
                        ps = nps()
                        kb.mm(ps, ps.ap[:, :n], [(w2p3[:, d, hp * 128:(hp + 1) * 128], tw.ap[:, t0:t0 + n])], [w2p, tw])
                        kb.op("act", E("activation", out=x1.ap[:, t0:t0 + n], in_=ps.ap[:, :n], func=AF.Sigmoid, bias=V("w0", d * 4 + hp)), [ps, vecs], [x1])
                        ps = nps()
                        kb.mm(ps, ps.ap[:, :n], [(a2p3[:, d, hp * 128:(hp + 1) * 128], alr.ap[:, t0:t0 + n])], [a2p, alr])
                        kb.op("act", E("activation", out=x2.ap[:, t0:t0 + n], in_=ps.ap[:, :n], func=AF.Sigmoid, bias=V("a0", d * 4 + hp)), [ps, vecs], [x2])
                    if "DBGRW" in kb.dbg:
                        kb.store(kb.dram["DBGRW"][1 + d, hp], x1, x1a)
                        kb.store(kb.dram["DBGRW"][3 + d, hp], x2, x2a)
                    kb.op("dve", E("tensor_scalar", out=x3a, in0=x2a, scalar1=V("k_a", hp), scalar2=oka.ap[:, hp:hp + 1], op0=ALU.mult, op1=ALU.add), [x2, vecs, oka], [x3])
                    kb.op("pool", E("tensor_tensor", out=x3a, in0=x3a, in1=kS.ap, op=ALU.mult), [x3, kS], [x3])
                    kb.op("pool", E("tensor_tensor", out=x2a, in0=x2a, in1=kk.ap, op=ALU.mult), [x2, kk], [x2])
                    if d == 0:
                        kb.op("dve", E("tensor_tensor_scan", out=x4a, data0=maskf, data1=x1a, initial=0.0, op0=ALU.mult, op1=ALU.add), [msk, x1], [x4])
                    else:
                        kb.op("dve", E("tensor_tensor_scan", out=x4a[:, ::-1], data0=maskr[:, ::-1], data1=x1a[:, ::-1], initial=0.0, op0=ALU.mult, op1=ALU.add), [msk, x1], [x4])
                    kb.op("pool", E("tensor_tensor", out=x1a, in0=x4a, in1=x1a, op=ALU.subtract), [x4, x1], [x1])
                    kb.op("act", E("activation", out=x5a, in_=x1a, func=AF.Exp, scale=-CDEC), [x1], [x5])
                    o = ob[iob % 3]; iob += 1
                    kb.op("pool", E("tensor_tensor", out=o.ap, in0=kk.ap, in1=x5a, op=ALU.mult), [kk, x5], [o])
                    kb.store(RWU7[hp][d][:, :, 0, :], o, o.ap.rearrange("p (c t) -> p c t", t=128))
                    kb.op("act", E("activation", out=x5a, in_=x4a, func=AF.Exp, scale=-CDEC), [x4, o], [x5])
                    o = ob[iob % 3]; iob += 1
                    kb.op("dve", E("tensor_tensor", out=o.ap, in0=rS.ap, in1=x5a, op=ALU.mult), [rS, x5], [o])
                    kb.store(RWU7[hp][d][:, :, 1, :], o, o.ap.rearrange("p (c t) -> p c t", t=128))
                    e1v = x5a.rearrange("p (c t) -> p c t", t=128)
                    kb.op("dve", E("tensor_copy", out=glt.ap, in_=(e1v[:, :, 127] if d == 0 else e1v[:, :, 0])), [x5], [glt])
                    kb.store(RWGL[hp, d], glt, glt.ap)
                    kb.op("act", E("activation", out=x1a, in_=x4a, func=AF.Exp, scale=CDEC), [x4, x1], [x1])
                    o = ob[iob % 3]; iob += 1
                    kb.op("pool", E("tensor_tensor", out=o.ap, in0=x3a, in1=x1a, op=ALU.mult), [x3, x1], [o])
                    kb.store(RWU7[hp][d][:, :, 2, :], o, o.ap.rearrange("p (c t) -> p c t", t=128))
                    o = ob[iob % 3]; iob += 1
                    kb.op("dve", E("tensor_tensor", out=o.ap, in0=x2a, in1=x1a, op=ALU.mult), [x2, x1], [o])
                    kb.store(RWU7[hp][d][:, :, 3, :], o, o.ap.rearrange("p (c t) -> p c t", t=128))
                    x1v = x1a.rearrange("p (c t) -> p c t", t=128)
                    kb.op("dve", E("tensor_tensor", out=x1v, in0=x1v, in1=bc(glt.ap.rearrange("p (c o) -> p c o", o=1), [128, NCH, 128]), op=ALU.mult), [x1, glt], [x1])
                    o = ob[iob % 3]; iob += 1
                    kb.op("pool", E("tensor_tensor", out=o.ap, in0=x3a, in1=x1a, op=ALU.mult), [x3, x1], [o])
                    kb.store(RWU7[hp][d][:, :, 4, :], o, o.ap.rearrange("p (c t) -> p c t", t=128))
                    o = ob[iob % 3]; iob += 1
                    kb.op("dve", E("tensor_tensor", out=o.ap, in0=x2a, in1=x1a, op=ALU.mult), [x2, x1], [o])
                    kb.store(RWU7[hp][d][:, :, 5, :], o, o.ap.rearrange("p (c t) -> p c t", t=128))
                    if d == 0:
                        kb.op("dve", E("scalar_tensor_tensor", out=pbn.ap, in0=rS.ap, scalar=V("r_k", hp), in1=x3a, op0=ALU.mult, op1=ALU.mult), [rS, x3, vecs], [pbn])
                    else:
                        kb.op("dve", E("scalar_tensor_tensor", out=x4a, in0=rS.ap, scalar=V("r_k", hp), in1=x3a, op0=ALU.mult, op1=ALU.mult), [rS, x3, vecs, x4], [x4])
                        kb.op("pool", E("tensor_tensor", out=pbn.ap, in0=pbn.ap, in1=x4a, op=ALU.add), [pbn, x4], [pbn])
                ist = 0
                for (t0, n) in TILES:
                    ps = nps(); sgb = stg[ist % 2]; ist += 1
                    kb.mm(ps, ps.ap[:, :n], [(CMF(4), pbn.ap[:, t0:t0 + n])], [cmf, pbn])
                    kb.op("dve", E("tensor_tensor", out=sgb.ap[:, :n], in0=ps.ap[:, :n], in1=vS.ap[:, t0:t0 + n], op=ALU.mult), [ps, vS], [sgb])
                    kb.store(RWBON[hp, :, t0:t0 + n], sgb, sgb.ap[:, :n])
                    ps = nps(); sgb = stg[ist % 2]; ist += 1
                    kb.mm(ps, ps.ap[:, :n], [(g2t.ap[:, hp * 128:(hp + 1) * 128], sg.ap[:, t0:t0 + n])], [g2t, sg])
                    kb.op("act", E("copy", out=sgb.ap[:, :n], in_=ps.ap[:, :n]), [ps], [sgb])
                    kb.store(RWG[hp, :, t0:t0 + n], sgb, sgb.ap[:, :n])
            kb.reset(m0)
            if stop_after == "rwa":
                return
            import os
            yT = [kb.alloc(NTOK, F32, f"yT{hp}") for hp in range(4)]
            m1 = kb.mark()
            NU = 4
            mk = kb.alloc(2 * 1280, F32, "mk"); mk3 = mk.ap.rearrange("p (d m) -> p d m", d=2)
            kb.load(mk, mk.ap, msk_in.rearrange("p d m -> p (d m)"))
            glall = kb.alloc(4 * 2 * NCH, F32, "glall"); gl4 = glall.ap.rearrange("p (a d c) -> p a d c", a=4, d=2)
            kb.load(glall, gl4, RWGL.rearrange("a d p c -> p a d c"))
            U7 = [[kb.alloc(896, F32, f"U7_{p}_{u}") for u in range(NU)] for p in range(2)]
            PD = [[kb.alloc(768, F32, f"PD_{p}_{u}") for u in range(NU)] for p in range(2)]
            for p in range(2):
                for u in range(NU):
                    kb.op("pool", E("memset", PD[p][u].ap, 0.0), [], [PD[p][u]])
            KBV = [kb.alloc(384, F32, f"KBV{u}") for u in range(NU)]
            VP = [kb.alloc(256, F32, f"VP{u}") for u in range(NU)]
            UP = [kb.alloc(256, F32, f"UP{u}") for u in range(NU)]
            BCt = [kb.alloc(512, F32, f"BC{u}") for u in range(NU)]
            ZDt = [kb.alloc(512, F32, f"ZD{u}") for u in range(NU)]
            X0T = [kb.alloc(256, F32, f"X0T{u}") for u in range(NU)]
            XX = [[kb.alloc(512, F32, f"XX{q}{u}") for u in range(NU)] for q in range(2)]
            RR = [[kb.alloc(256, F32, f"RR{q}{u}") for u in range(NU)] for q in range(2)]
            XIN = [kb.alloc(128, F32, f"XIN{u}") for u in range(NU)]
            UNt = [kb.alloc(128, F32, f"UN{u}") for u in range(NU)]
            Sst = [[kb.alloc(128, F32, f"S{p}{u}") for u in range(NU)] for p in range(2)]
            TMPS = [kb.alloc(128, F32, f"tmpS{u}") for u in range(NU)]
            for u in range(NU):
                kb.op("pool", E("memset", VP[u].ap, 0.0), [], [VP[u]])
                kb.op("pool", E("memset", UP[u].ap, 0.0), [], [UP[u]])
            bks = [kb.pbank(i) for i in range(8)]
            ib = [0]

            def nb():
                b_ = bks[ib[0] % 8]; ib[0] += 1
                return b_
            NST = int(os.environ.get('RWB_STEPS', NCH))
            ident2 = bc(CMF(0).rearrange("p (o n) -> p o n", o=1), [128, 2, 128])
            gstep = 0
            for d in range(2):
                for u in range(NU):
                    kb.op("pool", E("memset", Sst[gstep % 2][u].ap, 0.0), [], [Sst[gstep % 2][u]])

                def loads(s_, p):
                    for u in range(NU):
                        hp = u
                        c = ORDER[d][s_]
                        src = RWU[hp, d, c]
                        kb.load(U7[p][u], U7[p][u].ap, src[:, :])
                        pd4 = PD[p][u].ap.rearrange("p (w a t) -> p w a t", w=3, a=2)
                        for w, slot in enumerate((2, 3, 0)):
                            kb.load(PD[p][u], pd4[0:64, w, 0, :], src[0:64, slot * 128:(slot + 1) * 128])
                            kb.load(PD[p][u], pd4[64:128, w, 1, :], src[64:128, slot * 128:(slot + 1) * 128])
                loads(0, gstep % 2)
                for s_ in range(NST):
                    p = gstep % 2; po = 1 - p
                    if s_ + 1 < NST:
                        loads(s_ + 1, po)
                    c = ORDER[d][s_]
                    for u in range(NU):
                        u7 = U7[p][u]; pd4 = PD[p][u].ap.rearrange("p (w a t) -> p w a t", w=3, a=2)
                        bT = nb()
                        for j, slot in enumerate((4, 5, 6)):
                            kb.S.op("pe", E("transpose", out=bT.ap[:, j * 128:(j + 1) * 128], in_=u7.ap[:, slot * 128:(slot + 1) * 128], identity=CMF(0)),
                                    reads=[u7, cmf], writes=[bT], sig=(j == 2))
                        kb.op("act", E("copy", out=KBV[u].ap, in_=bT.ap[:, 0:384]), [bT], [KBV[u]])
                        vp64 = VP[u].ap.rearrange("p (a c) -> p a c", c=64)
                        kb.op("pool", E("tensor_copy", out=vp64[:, 0:4:3, :], in_=KBV[u].ap[:, 256:384].rearrange("p (a c) -> p a c", a=2)), [KBV[u]], [VP[u]])
                        rk_ = u7.ap[:, 0:256]
                        b1 = nb()
                        for a_ in range(2):
                            kb.mm1(b1, b1.ap[:, a_ * 256:(a_ + 1) * 256], pd4[:, 0, a_, :], rk_, [PD[p][u], u7])
                        kb.op("dve", E("tensor_tensor", out=BCt[u].ap, in0=b1.ap, in1=mk3[:, d, 0:512], op=ALU.mult), [b1, mk], [BCt[u]])
                        b1 = nb()
                        for a_ in range(2):
                            kb.mm1(b1, b1.ap[:, a_ * 256:(a_ + 1) * 256], pd4[:, 1, a_, :], rk_, [PD[p][u], u7])
                        kb.op("dve", E("tensor_tensor", out=ZDt[u].ap, in0=b1.ap, in1=mk3[:, d, 512:1024], op=ALU.mult), [b1, mk], [ZDt[u]])
                        zd3 = ZDt[u].ap.rearrange("p (a m) -> p a m", a=2)
                        kb.op("pool", E("tensor_tensor", out=RR[0][u].ap.rearrange("p (a m) -> p a m", a=2), in0=zd3[:, :, 0:128], in1=ident2, op=ALU.add), [ZDt[u], cmf], [RR[0][u]])
                        b1 = nb()
                        for a_ in range(2):
                            kb.mm1(b1, b1.ap[:, a_ * 128:(a_ + 1) * 128], pd4[:, 2, a_, :], u7.ap[:, 384:512], [PD[p][u], u7])
                        kb.op("dve", E("tensor_tensor", out=X0T[u].ap, in0=b1.ap[:, 0:256], in1=mk3[:, d, 1024:1280], op=ALU.mult), [b1, mk], [X0T[u]])
                    for lvl in range(6):
                        q0 = lvl % 2; q1 = 1 - q0
                        hs = {}
                        for u in range(NU):
                            b1 = nb(); hs[u] = b1
                            for a_ in range(2):
                                if lvl == 0:
                                    Xk = ZDt[u].ap[:, a_ * 256:a_ * 256 + 128]; XTk = X0T[u].ap[:, a_ * 128:(a_ + 1) * 128]; rd = [ZDt[u], X0T[u]]
                                else:
                                    Xk = XX[q0][u].ap[:, a_ * 256:a_ * 256 + 128]; XTk = XX[q0][u].ap[:, a_ * 256 + 128:a_ * 256 + 256]; rd = [XX[q0][u]]
                                if lvl < 5:
                                    kb.mm1(b1, b1.ap[:, a_ * 256:a_ * 256 + 128], XTk, Xk, rd)
                                kb.mm1(b1, b1.ap[:, a_ * 256 + 128:a_ * 256 + 256], Xk, XTk, rd)
                        for u in range(NU):
                            b1 = hs[u]
                            if lvl < 5:
                                kb.op("act", E("copy", out=XX[q1][u].ap, in_=b1.ap), [b1], [XX[q1][u]])
                            else:
                                kb.op("act", E("copy", out=XX[q1][u].ap.rearrange("p (a m) -> p a m", a=2)[:, :, 128:256],
                                               in_=b1.ap.rearrange("p (a m) -> p a m", a=2)[:, :, 128:256]), [b1], [XX[q1][u]])
                        for u in range(NU):
                            b1 = nb(); hs[u] = b1
                            for a_ in range(2):
                                kb.mm1(b1, b1.ap[:, a_ * 128:(a_ + 1) * 128], XX[q1][u].ap[:, a_ * 256 + 128:a_ * 256 + 256], RR[q0][u].ap[:, a_ * 128:(a_ + 1) * 128],
                                       [XX[q1][u], RR[q0][u]])
                        for u in range(NU):
                            b1 = hs[u]
                            kb.op("dve", E("tensor_tensor", out=RR[q1][u].ap, in0=b1.ap[:, 0:256], in1=RR[q0][u].ap, op=ALU.add), [b1, RR[q0][u]], [RR[q1][u]])
                    RF = RR[0]
                    hx = {}
                    for u in range(NU):
                        hp = u
                        kb.op("pool", E("tensor_scalar", out=TMPS[u].ap, in0=Sst[p][u].ap, scalar1=gl4[:, hp, d, c:c + 1], scalar2=None, op0=ALU.mult), [Sst[p][u], glall], [TMPS[u]])
                        b1 = nb(); hx[u] = b1
                        u7 = U7[p][u]
                        kb.mm1(b1, b1.ap[:, 0:128], u7.ap[:, 0:128], Sst[p][u].ap, [u7, Sst[p][u]], start=True, stop=False, sig=False)
                        kb.mm1(b1, b1.ap[:, 0:64], BCt[u].ap[:, 0:128], KBV[u].ap[:, 256:320], [BCt[u], KBV[u]], start=False, stop=False, sig=False)
                        kb.mm1(b1, b1.ap[:, 64:128], BCt[u].ap[:, 256:384], KBV[u].ap[:, 320:384], [BCt[u], KBV[u]], start=False, stop=True, sig=True)
                    for u in range(NU):
                        kb.op("act", E("copy", out=XIN[u].ap, in_=hx[u].ap[:, 0:128]), [hx[u]], [XIN[u]])
                    for u in range(NU):
                        b1 = nb(); hx[u] = b1
                        for a_ in range(2):
                            kb.mm1(b1, b1.ap[:, a_ * 64:(a_ + 1) * 64], RF[u].ap[:, a_ * 128:(a_ + 1) * 128], XIN[u].ap[:, a_ * 64:(a_ + 1) * 64], [RF[u], XIN[u]])
                    for u in range(NU):
                        kb.op("dve", E("tensor_scalar", out=UNt[u].ap, in0=hx[u].ap[:, 0:128], scalar1=-1.0, scalar2=None, op0=ALU.mult), [hx[u]], [UNt[u]])
                        up64 = UP[u].ap.rearrange("p (a c) -> p a c", c=64)
                        kb.op("pool", E("tensor_copy", out=up64[:, 0:4:3, :], in_=UNt[u].ap.rearrange("p (a c) -> p a c", a=2)), [UNt[u]], [UP[u]])
                    for u in range(NU):
                        b1 = nb(); hx[u] = b1
                        kb.mm(b1, b1.ap[:, 0:128], [(KBV[u].ap[:, 0:128], KBV[u].ap[:, 256:384]), (KBV[u].ap[:, 128:256], UNt[u].ap)], [KBV[u], UNt[u]])
                    for u in range(NU):
                        kb.op("dve", E("tensor_tensor", out=Sst[po][u].ap, in0=hx[u].ap[:, 0:128], in1=CMF(8), op=ALU.mult), [hx[u], cmf], [Sst[po][u]])
                        kb.op("pool", E("tensor_tensor", out=Sst[po][u].ap, in0=Sst[po][u].ap, in1=TMPS[u].ap, op=ALU.add), [Sst[po][u], TMPS[u]], [Sst[po][u]])
                    for u in range(NU):
                        hp = u
                        b1 = nb(); u7 = U7[p][u]
                        vp3 = VP[u].ap.rearrange("p (a c) -> p a c", a=2); up3 = UP[u].ap.rearrange("p (a c) -> p a c", a=2)
                        kb.mm(b1, b1.ap[:, 0:128], [(Sst[p][u].ap, u7.ap[:, 128:256]),
                                                     (vp3[:, 0, :], BCt[u].ap[:, 128:256]), (vp3[:, 1, :], BCt[u].ap[:, 384:512]),
                                                     (up3[:, 0, :], ZDt[u].ap[:, 128:256]), (up3[:, 1, :], ZDt[u].ap[:, 384:512])],
                              [Sst[p][u], u7, VP[u], UP[u], BCt[u], ZDt[u]])
                        if "DBGYD" in kb.dbg:
                            kb.op("dve", E("tensor_copy", out=TMPS[u].ap, in_=b1.ap[:, 0:128]), [b1], [TMPS[u]])
                            kb.store(kb.dram["DBGYD"][d, hp, :, c * 128:(c + 1) * 128], TMPS[u], TMPS[u].ap)
                        ycol = yT[hp].ap[:, c * 128:(c + 1) * 128]
                        if d == 0:
                            kb.op("dve", E("tensor_copy", out=ycol, in_=b1.ap[:, 0:128]), [b1], [yT[hp]])
                        else:
                            kb.op("dve", E("tensor_tensor", out=ycol, in0=b1.ap[:, 0:128], in1=ycol, op=ALU.add), [b1, yT[hp]], [yT[hp]])
                    gstep += 1
            if "DBGY" in kb.dbg:
                for hp in range(4):
                    kb.store(kb.dram["DBGY"][hp], yT[hp], yT[hp].ap)
            kb.reset(m1)
            gb = [kb.alloc(512, F32, f"gb{i}") for i in range(2)]
            bb_ = [kb.alloc(512, F32, f"bb{i}") for i in range(2)]
            dv_ = [kb.alloc(512, F32, f"dv{i}") for i in range(2)]
            sq_ = [kb.alloc(512, F32, f"sq{i}") for i in range(2)]
            oo = [kb.alloc(512, BF16, f"oo{i}") for i in range(2)]
            pbk = [kb.pbank(i) for i in range(4)]
            it = 0
            for hp in range(4):
                for (t0, n) in TILES:
                    g_ = gb[it % 2]; b_ = bb_[it % 2]; dd = dv_[it % 2]; qq = sq_[it % 2]; o_ = oo[it % 2]
                    pm = pbk[(2 * it) % 4]; pv = pbk[(2 * it + 1) % 4]; it += 1
                    kb.load(g_, g_.ap[:, :n], RWG[hp, :, t0:t0 + n])
                    kb.load(b_, b_.ap[:, :n], RWBON[hp, :, t0:t0 + n])
                    ysl = yT[hp].ap[:, t0:t0 + n]
                    kb.mm(pm, pm.ap[:, :n], [(CMF(5), ysl)], [cmf, yT[hp]])
                    kb.op("dve", E("tensor_tensor", out=dd.ap[:, :n], in0=ysl, in1=pm.ap[:, :n], op=ALU.subtract), [yT[hp], pm], [dd])
                    kb.op("act", E("activation", out=qq.ap[:, :n], in_=dd.ap[:, :n], func=AF.Square), [dd], [qq])
                    kb.mm(pv, pv.ap[:, :n], [(CMF(5), qq.ap[:, :n])], [cmf, qq])
                    kb.op("act", E("activation", out=qq.ap[:, :n], in_=pv.ap[:, :n], func=AF.Sqrt, bias=64e-5), [pv, qq], [qq])
                    kb.op("dve", E("reciprocal", out=qq.ap[:, :n], in_=qq.ap[:, :n]), [qq], [qq])
                    kb.op("dve", E("scalar_tensor_tensor", out=dd.ap[:, :n], in0=dd.ap[:, :n], scalar=V("gn_w", hp), in1=qq.ap[:, :n], op0=ALU.mult, op1=ALU.mult), [dd, qq, vecs], [dd])
                    kb.op("dve", E("scalar_tensor_tensor", out=dd.ap[:, :n], in0=dd.ap[:, :n], scalar=V("gn_b", hp), in1=b_.ap[:, :n], op0=ALU.add, op1=ALU.add), [dd, b_, vecs], [dd])
                    kb.op("pool", E("tensor_tensor", out=o_.ap[:, :n], in0=dd.ap[:, :n], in1=g_.ap[:, :n], op=ALU.mult), [dd, g_], [o_])
                    kb.store(mixT[512 + hp * 128:512 + (hp + 1) * 128, t0:t0 + n], o_, o_.ap[:, :n])
            kb.reset(m0)

        ALLT = [(ti, t0, n, t0) for ti, (t0, n) in enumerate(TILES)]
        LATT = [(ti, t0, n, t0 - NCTX) for ti, (t0, n) in enumerate(TILES) if ti > 0]

        MLPA = [(0 if t0 == 0 else 1, t0, 256, t0) for t0 in range(0, NTOK, 256)]
        MLPL = [(1, t0, 256, t0 - NCTX) for t0 in range(NCTX, NTOK, 256)]
        LATI = [(ti, t0, n, t0) for ti, (t0, n) in enumerate(TILES) if ti > 0]
        chunks0 = [(i * 128, 128) for i in range(5)] + [(640, 96), (736, 96)] + [(832 + i * 128, 128) for i in range(15)]
        chunks1 = [(i * 128, 128) for i in range(16)]
        if "DBGL1" in kb.dbg:
            kb.dram_t("DBGL1", [3, 8, 128, NTOK], F32)
        if "DBGRW" in kb.dbg:
            kb.dram_t("DBGRW", [5, 4, 128, NTOK], F32)
        if "DBGY" in kb.dbg:
            kb.dram_t("DBGY", [4, 128, NTOK], F32)
        if "DBGYD" in kb.dbg:
            kb.dram_t("DBGYD", [2, 4, 128, NTOK], F32)

        def fin():
            S.run()
            return nc
        import os as _os
        if _os.environ.get("ONLY_RWB"):
            stage_rwkv()
            return fin()
        if start_layer == 0:
            stage_mod(0)
            if "DBGMOD" in kb.dbg:
                dm = kb.dram_t("DBGMOD", [128, 96], F32)
                kb.store(dm[:, :], modv[0], modv[0].ap)
            if stop_after == "mod":
                return fin()
            stage_win(0, xT_in, 2752, "w_in0", chunks0, F0T)
            if stop_after == "win0":
                return fin()
            stage_mla()
            if stop_after == "mla":
                return fin()
            stage_rwkv()
            if stop_after == "rwkv":
                return fin()
            stage_wout(0, xT_in, xT, ALLT)
            if stop_after == "wout0":
                return fin()
            stage_mlp(0, xT, xT, MLPA)
            if stop_after == "mlp0":
                return fin()
            x1src = xT
        else:
            x1src = xT_in
        stage_mod(1)
        stage_win(1, x1src, 2048, "w_in1", chunks1, G1T, gelu_chunks=set(range(8)))
        if stop_after == "win1":
            return fin()
        stage_lru()
        if stop_after == "lru":
            return fin()
        stage_wout(1, x1src, xT, LATI)
        if stop_after == "wout1":
            return fin()
        stage_mlp(1, xT, outT, MLPL)
        S.run()
    return nc


def _rope_tables():
    rows = np.repeat(np.arange(32, dtype=np.float32), 64)
    cols = np.tile(np.arange(64, dtype=np.float32), 32)
    inv = (10000.0 ** (-np.arange(0, 16, 2, dtype=np.float32) / 16)).astype(np.float32)
    ar = rows[:, None] * inv
    ac = cols[:, None] * inv
    ang = np.concatenate([ar, ar, ac, ac], -1)
    cos = np.cos(ang); sin = np.sin(ang)
    sgn = np.tile(np.concatenate([-np.ones(8), np.ones(8)]), 2).astype(np.float32)
    out = np.zeros((128, 2, NLAT), np.float32)
    out[64:96, 0, :] = cos.T
    out[64:96, 1, :] = (sin * sgn).T
    return out


def _rot_perm():
    perm = np.zeros(32, np.int64)
    for a in range(2):
        for h in range(2):
            for f in range(8):
                perm[a * 16 + h * 8 + f] = a * 16 + (1 - h) * 8 + f
    return perm


def prep_inputs(inputs):
    I = {k: np.asarray(v) for k, v in inputs.items()}
    shared = {}
    cmats = np.zeros((128, 9, 128), np.float32)
    cmats[:, 0] = np.eye(128)
    cmats[:, 1] = 1.0 / 1024
    cmats[:, 2] = 1.0 / 384
    cmats[:, 3] = 1.0 / 256
    bo = np.zeros((128, 128), np.float32); bo[:64, :64] = 1; bo[64:, 64:] = 1
    cmats[:, 4] = bo
    cmats[:, 5] = bo / 64
    cmats[64, 6, :64] = 1.0
    cmats[:, 7] = 1.0
    cmats[:, 8] = bo
    shared["cmats"] = cmats
    shared["rope"] = _rope_tables()
    ind = np.zeros((128, 8, 8), np.float32)
    for h in range(8):
        ind[:96, h, h] = 1.0
    shared["ind8"] = ind
    ii = np.arange(128)[:, None]; tt = np.arange(128)[None, :]
    msk = np.zeros((128, 2, 1280), np.float32)
    for d, (st_, inc_) in enumerate((((ii < tt), (ii <= tt)), ((ii > tt), (ii >= tt)))):
        st_ = st_.astype(np.float32); inc_ = inc_.astype(np.float32)
        msk[:, d, 0:512] = np.concatenate([st_, inc_, st_, inc_], 1)
        msk[:, d, 512:1024] = np.concatenate([-st_, inc_, -st_, inc_], 1)
        msk[:, d, 1024:1280] = np.concatenate([-st_.T, -st_.T], 1)
    shared["rwmask"] = msk
    for L in range(2):
        p = f"l{L}_"
        for nm in ("mod_w", "w_out", "mlp_w1", "mlp_w2"):
            shared[p + nm] = np.ascontiguousarray(I[p + nm], np.float32)
    w_in = I["l0_w_in"]
    perm = _rot_perm()
    w0e = np.zeros((1024, 2752), np.float32)
    w0e[:, 0:640] = w_in[:, 0:640]
    w0e[:, 640 + 64:640 + 96] = w_in[:, 640:672]
    w0e[:, 736 + 64:736 + 96] = w_in[:, 640:672][:, perm]
    w0e[:, 832:] = w_in[:, 672:]
    shared["w_in0"] = w0e
    wuq = I["l0_mla_w_uq"].reshape(384, 8, 96)
    wrot = np.zeros_like(wuq)
    wrot[:, :, 64:96] = wuq[:, :, 64:96][:, :, perm]
    shared["w_uq"] = np.ascontiguousarray(wuq.reshape(384, 768))
    shared["w_uq_rot"] = np.ascontiguousarray(wrot.reshape(384, 768))
    shared["w_uk"] = np.ascontiguousarray(I["l0_mla_w_uk"])
    shared["w_uv"] = np.ascontiguousarray(I["l0_mla_w_uv"])
    for nm, src in (("w2pad", "l0_rwkv_w2"), ("a2pad", "l0_rwkv_a2")):
        a = np.zeros((128, 2, 512), np.float32)
        a[0:64, 0] = I[src][0]; a[64:128, 1] = I[src][1]
        shared[nm] = a
    shared["g2"] = np.ascontiguousarray(I["l0_rwkv_g2"])
    shared["w_in1"] = np.ascontiguousarray(I["l1_w_in"])
    shared["ga_w"] = np.ascontiguousarray(np.transpose(I["l1_lru_ga_w"], (2, 0, 1, 3)))
    shared["gx_w"] = np.ascontiguousarray(np.transpose(I["l1_lru_gx_w"], (2, 0, 1, 3)))
    vbase = np.zeros((128, NCOL), np.float32)

    def put(name, arr, n):
        vbase[:, COLS[name]:COLS[name] + n] = _pcol(arr, n)
    put("cctxT", I["c_ctx"], 8)
    for L in range(2):
        p = f"l{L}_"
        put(p + "mod_b", I[p + "mod_b"], 48)
        for nm in ("ln1_g", "ln1_b", "ln2_g", "ln2_b"):
            put(p + nm, I[p + nm], 8)
    put("q_norm", I["l0_mla_q_norm"], 3); put("kv_norm", I["l0_mla_kv_norm"], 2); put("mu", I["l0_rwkv_mu"], 15)
    put("w0", I["l0_rwkv_w0"].reshape(-1), 8); put("a0", I["l0_rwkv_a0"].reshape(-1), 8)
    put("k_k", I["l0_rwkv_k_k"], 4); put("k_a", I["l0_rwkv_k_a"], 4); put("r_k", I["l0_rwkv_r_k"].reshape(-1), 4)
    put("gn_w", I["l0_rwkv_gn_w"], 4); put("gn_b", I["l0_rwkv_gn_b"], 4)
    put("conv_w", I["l1_conv_w"].reshape(-1), 32); put("conv_b", I["l1_conv_b"], 8)
    put("ga_b", I["l1_lru_ga_b"].reshape(-1), 16); put("gx_b", I["l1_lru_gx_b"].reshape(-1), 16)
    put("lam", I["l1_lru_lambda"].reshape(-1), 16)
    per_core = []
    for b in range(8):
        v = vbase.copy()
        v[:, COLS["cT"]:COLS["cT"] + 8] = _pcol(I["c"][b], 8)
        xTb = np.ascontiguousarray(np.concatenate([I["ctx"][b].T, I["x"][b].T], axis=1), np.float32)
        per_core.append({"xT": xTb, "vec": v})
    return shared, per_core


_NC_CACHE = {}


def kernel(**inputs):
    shared, per_core = prep_inputs(inputs)
    if "nc" not in _NC_CACHE:
        _NC_CACHE["nc"] = build_program()
    nc = _NC_CACHE["nc"]
    in_maps = [dict(shared, **pc) for pc in per_core]
    res = run_bass_kernel_spmd(nc, in_maps, core_ids=list(range(8)))
    out = np.stack([np.ascontiguousarray(r["outT"].T) for r in res.results], axis=0)
    return out.astype(np.float32)
```

```python
import contextlib
import numpy as np
import concourse.bass as bass
import concourse.mybir as mybir
from concourse.bass_utils import run_bass_kernel_spmd

F32 = mybir.dt.float32
BF16 = mybir.dt.bfloat16
AF = mybir.ActivationFunctionType
ALU = mybir.AluOpType
AX = mybir.AxisListType

NTOK = 2304
NCTX = 256
NLAT = 2048
TILES = [(0, 256), (256, 512), (768, 512), (1280, 512), (1792, 512)]
ALPHA = 4.0 ** 0.25
CDEC = float(np.exp(-0.5))
SCALE = 96.0 ** -0.5
NCH = 18

ENGS = ("pe", "act", "dve", "pool", "sp")
NDMASEM = 8


class Buf:
    __slots__ = ("name", "w", "r")

    def __init__(self, name=""):
        self.name = name
        self.w = None
        self.r = []


class T:
    __slots__ = ("ap", "b")

    def __init__(self, ap, name=""):
        self.ap = ap
        self.b = Buf(name)


def _b(x):
    return x.b if isinstance(x, T) else x


class Sched:
    def __init__(self, nc):
        self.nc = nc
        self.streams = {e: [] for e in ENGS}
        self.cnt = {e: 0 for e in ENGS}
        self.waited = {}
        self.sem = {}
        self.dma_n = {"sp": 0, "pool": 0, "act": 0}
        self.pending_nosig = {e: False for e in ENGS}
        self.ninst = 0

    def _semkeys(self):
        keys = list(ENGS)
        for q in ("sp", "pool", "act"):
            for i in range(NDMASEM):
                keys.append(("dma", q, i))
        return keys

    def _need(self, eng, tok, waits):
        if tok is None:
            return
        key, val = tok
        if self.waited.get((eng, key), 0) >= val:
            return
        if key == eng and eng == "pe":
            return
        self.waited[(eng, key)] = val
        waits[key] = max(waits.get(key, 0), val)

    def _deps(self, eng, reads, writes):
        waits = {}
        for b in reads:
            self._need(eng, _b(b).w, waits)
        for b in writes:
            b = _b(b)
            self._need(eng, b.w, waits)
            for t in b.r:
                self._need(eng, t, waits)
        return waits

    def _commit(self, tok, reads, writes):
        for b in reads:
            b = _b(b)
            b.r.append(tok)
            if len(b.r) > 48:
                d = {}
                for k, v in b.r:
                    d[k] = max(d.get(k, 0), v)
                b.r = list(d.items())
        for b in writes:
            b = _b(b)
            b.w = tok
            b.r = []

    def op(self, eng, fn, reads=(), writes=(), sig=True):
        waits = self._deps(eng, reads, writes)
        if sig:
            self.cnt[eng] += 1
            self.pending_nosig[eng] = False
        else:
            self.pending_nosig[eng] = True
        tok = (eng, self.cnt[eng] if sig else self.cnt[eng] + 1)
        self._commit(tok, reads, writes)
        self.streams[eng].append((waits, fn, eng if sig else None, 1))
        self.ninst += 1

    def dma(self, q, out, in_, reads=(), writes=()):
        n = self.dma_n[q]
        self.dma_n[q] += 1
        slot = n % NDMASEM
        key = ("dma", q, slot)
        val = 16 * (n // NDMASEM + 1)
        waits = self._deps(q, reads, writes)
        if val > 16:
            self._need(q, (key, val - 16), waits)
        tok = (key, val)
        self._commit(tok, reads, writes)
        self.streams[q].append((waits, E("dma_start", out=out, in_=in_), key, 16))
        self.ninst += 1
        return tok

    def all_tokens(self):
        toks = [(e, self.cnt[e]) for e in ENGS if self.cnt[e] > 0]
        for q, n in self.dma_n.items():
            for slot in range(min(n, NDMASEM)):
                last = ((n - 1 - slot) // NDMASEM) * NDMASEM + slot
                toks.append((("dma", q, slot), 16 * (last // NDMASEM + 1)))
        return toks

    def barrier(self):
        for e in ENGS:
            assert not self.pending_nosig[e], e
        toks = self.all_tokens()
        for e in ENGS:
            waits = {}
            for t in toks:
                self._need(e, t, waits)
            if waits:
                self.streams[e].append((waits, None, None, 0))

    def run(self):
        nc = self.nc
        self.barrier()
        with contextlib.ExitStack() as st:
            for k in self._semkeys():
                nm = k if isinstance(k, str) else f"d_{k[1]}_{k[2]}"
                self.sem[k] = st.enter_context(nc.semaphore("s_" + nm))
            block = st.enter_context(nc.Block())
            sem = self.sem

            def mk(ename):
                stream = self.streams[ename]

                def body(e):
                    for waits, fn, sigkey, inc in stream:
                        for k, v in waits.items():
                            e.wait_ge(sem[k], v)
                        if fn is not None:
                            ins = fn(e)
                            if sigkey is not None:
                                ins.then_inc(sem[sigkey], inc)
                return body

            block.tensor(mk("pe"))
            block.scalar(mk("act"))
            block.vector(mk("dve"))
            block.gpsimd(mk("pool"))
            block.sync(mk("sp"))


ARENA_F32 = 47104


class KB:
    def __init__(self, nc, arena, banks, dbg):
        self.nc = nc
        self.S = Sched(nc)
        self.arena = arena
        self.banks = banks
        self.off = 0
        self.dbg = dbg
        self.dram = {}
        self.eng_rr = 0

    def alloc(self, n, dt=F32, name=""):
        words = (n + 1) // 2 if dt == BF16 else n
        words = (words + 7) // 8 * 8
        assert self.off + words <= ARENA_F32, (name, self.off, words)
        ap = self.arena[:, self.off:self.off + words]
        self.off += words
        if dt == BF16:
            ap = ap.bitcast(BF16)[:, 0:n]
        else:
            ap = ap[:, 0:n]
        return T(ap, name)

    def mark(self):
        return self.off

    def reset(self, mark):
        self.S.barrier()
        self.off = mark

    def pbank(self, i):
        return T(self.banks[i][:, :], f"bank{i}")

    def phalf(self, i):
        return T(self.banks[i // 2][:, (i % 2) * 256:(i % 2 + 1) * 256], f"half{i}")

    def dram_t(self, name, shape, dt):
        kind = "ExternalOutput" if name in self.dbg else "Internal"
        t = self.nc.dram_tensor(name, list(shape), dt, kind=kind).ap()
        self.dram[name] = t
        return t

    def mm(self, out, out_ap, pairs, reads):
        n = len(pairs)
        for j, (l, r) in enumerate(pairs):
            self.S.op("pe", E("matmul", out_ap, lhsT=l, rhs=r, start=(j == 0), stop=(j == n - 1)),
                      reads=reads, writes=[out], sig=(j == n - 1))

    def mm1(self, out, out_ap, l, r, reads, start=True, stop=True, sig=True):
        self.S.op("pe", E("matmul", out_ap, lhsT=l, rhs=r, start=start, stop=stop), reads=reads, writes=[out], sig=sig)

    def op(self, eng, fn, reads, writes):
        self.S.op(eng, fn, reads=reads, writes=writes)

    def load(self, dst, dst_ap, src_ap, q="sp"):
        return self.S.dma(q, dst_ap, src_ap, writes=[dst])

    def store(self, dst_ap, src, src_ap, q="sp"):
        return self.S.dma(q, dst_ap, src_ap, reads=[src])


def E(name, *a, **kw):
    return lambda e: getattr(e, name)(*a, **kw)


def bc(ap, shape):
    return ap.to_broadcast(list(shape))


def _colmap():
    cols = {}
    off = 0

    def add(name, n):
        nonlocal off
        cols[name] = off
        off += n
    add("cT", 8); add("cctxT", 8)
    for L in range(2):
        p = f"l{L}_"
        add(p + "mod_b", 48)
        for nm in ("ln1_g", "ln1_b", "ln2_g", "ln2_b"):
            add(p + nm, 8)
    add("q_norm", 3); add("kv_norm", 2); add("mu", 15)
    add("w0", 8); add("a0", 8)
    for nm in ("k_k", "k_a", "r_k", "gn_w", "gn_b"):
        add(nm, 4)
    add("conv_w", 32)
    add("conv_b", 8)
    add("ga_b", 16); add("gx_b", 16); add("lam", 16)
    return cols, off


COLS, NCOL = _colmap()


def _pcol(v, n):
    return np.ascontiguousarray(np.asarray(v, np.float32).reshape(n, 128).T)


def build_program(dbg=(), stop_after=None, start_layer=0):
    nc = bass.Bass("TRN2", target_bir_lowering=False)
    IN = {}

    def inp(name, shape, dt=F32):
        IN[name] = nc.dram_tensor(name, list(shape), dt, kind="ExternalInput").ap()
        return IN[name]

    xT_in = inp("xT", [1024, NTOK])
    vec = inp("vec", [128, NCOL])
    cm = inp("cmats", [128, 9, 128])
    rope_in = inp("rope", [128, 2, NLAT])
    ind_in = inp("ind8", [128, 8, 8])
    msk_in = inp("rwmask", [128, 2, 1280])
    W = {}
    for L in range(2):
        p = f"l{L}_"
        W[p + "mod_w"] = inp(p + "mod_w", [1024, 6144])
        W[p + "w_out"] = inp(p + "w_out", [1024, 1024])
        W[p + "mlp_w1"] = inp(p + "mlp_w1", [1024, 4096])
        W[p + "mlp_w2"] = inp(p + "mlp_w2", [4096, 1024])
    W["w_in0"] = inp("w_in0", [1024, 2752])
    W["w_uq"] = inp("w_uq", [384, 768]); W["w_uq_rot"] = inp("w_uq_rot", [384, 768])
    W["w_uk"] = inp("w_uk", [256, 512]); W["w_uv"] = inp("w_uv", [256, 512])
    W["w2pad"] = inp("w2pad", [128, 2, 512]); W["a2pad"] = inp("a2pad", [128, 2, 512]); W["g2"] = inp("g2", [128, 512])
    W["w_in1"] = inp("w_in1", [1024, 2048])
    W["ga_w"] = inp("ga_w", [128, 2, 8, 128]); W["gx_w"] = inp("gx_w", [128, 2, 8, 128])
    outT = nc.dram_tensor("outT", [1024, NLAT], F32, kind="ExternalOutput").ap()

    with contextlib.ExitStack() as st:
        arena = st.enter_context(nc.sbuf_tensor("arena", [128, ARENA_F32], F32))
        banks = [st.enter_context(nc.psum_tensor(f"bank{i}", [128, 512], F32)) for i in range(8)]
        kb = KB(nc, arena, banks, set(dbg))
        S = kb.S
        xT = kb.dram_t("xTs", [1024, NTOK], F32)
        F0T = kb.dram_t("F0T", [22, 128, NTOK], F32)
        mixT = kb.dram_t("mixT", [1024, NTOK], BF16)
        RWU = kb.dram_t("RWU", [4, 2, NCH, 128, 7 * 128], F32)
        RWGL = kb.dram_t("RWGL", [4, 2, 128, NCH], F32)
        RWG = kb.dram_t("RWG", [4, 128, NTOK], F32)
        RWBON = kb.dram_t("RWBON", [4, 128, NTOK], F32)
        G1T = kb.dram_t("G1T", [16, 128, NTOK], F32)
        HID = None

        vecs = kb.alloc(NCOL, F32, "vecs")
        cmf = kb.alloc(9 * 128, F32, "cmf")
        cmb = kb.alloc(9 * 128, BF16, "cmb")
        modv = [kb.alloc(96, F32, f"modv{L}") for L in range(2)]
        mod1p = [kb.alloc(96, F32, f"mod1p{L}") for L in range(2)]
        kb.load(vecs, vecs.ap, vec[:, :])
        kb.load(cmf, cmf.ap, cm.rearrange("p a b -> p (a b)"))
        kb.load(cmb, cmb.ap, cm.rearrange("p a b -> p (a b)"), q="pool")
        PERSIST = kb.mark()

        def V(name, i=0, n=1):
            o = COLS[name] + i
            return vecs.ap[:, o:o + n]

        def CMF(i):
            return cmf.ap[:, i * 128:(i + 1) * 128]

        def CMB(i):
            return cmb.ap[:, i * 128:(i + 1) * 128]

        def MOD(L, j, k, which):
            o = (j * 8 + k) * 2 + which
            return modv[L].ap[:, o:o + 1]

        def MOD1P(L, j, k, which):
            o = (j * 8 + k) * 2 + which
            return mod1p[L].ap[:, o:o + 1]

        def xview(t):
            return t.rearrange("(k p) n -> p k n", p=128)


        def stage_mod(L):
            p = f"l{L}_"
            m0 = kb.mark()
            cs = kb.alloc(16, F32, "cs")
            mwb = [kb.alloc(8 * 512, F32, f"mw{i}") for i in range(2)]
            ps = kb.pbank(0)
            csv = cs.ap.rearrange("p (k w) -> p k w", w=2)
            kb.op("act", E("activation", out=csv[:, :, 0], in_=V("cT", 0, 8), func=AF.Silu), [vecs], [cs])
            kb.op("act", E("activation", out=csv[:, :, 1], in_=V("cctxT", 0, 8), func=AF.Silu), [vecs], [cs])
            mwv = W[p + "mod_w"].rearrange("(k p) n -> p k n", p=128)
            for piece in range(12):
                mw = mwb[piece % 2]
                mw3 = mw.ap.rearrange("p (k n) -> p k n", k=8)
                kb.load(mw, mw3, mwv[:, :, piece * 512:(piece + 1) * 512], q="sp")
                for j in range(4):
                    oc = piece * 4 + j
                    kb.mm(ps, ps.ap[:, oc * 2:oc * 2 + 2],
                          [(mw3[:, k, j * 128:(j + 1) * 128], csv[:, k, :]) for k in range(8)], [mw, cs])
            mv3 = modv[L].ap.rearrange("p (a w) -> p a w", w=2)
            kb.op("dve", E("tensor_tensor", out=mv3, in0=ps.ap[:, 0:96].rearrange("p (a w) -> p a w", w=2),
                                                    in1=bc(V(p + "mod_b", 0, 48).rearrange("p (a o) -> p a o", o=1), [128, 48, 2]), op=ALU.add),
                  [ps, vecs], [modv[L]])
            kb.op("dve", E("tensor_scalar", out=mod1p[L].ap, in0=modv[L].ap, scalar1=1.0, scalar2=None, op0=ALU.add),
                  [modv[L]], [mod1p[L]])
            kb.reset(m0)

        def modulate_tile(L, jsh, jsc, xt3, ht3, n, which, xT_T, hT_T):
            for k in range(8):
                if k % 2 == 0:
                    kb.op("act", E("activation", out=ht3[:, k, :n], in_=xt3[:, k, :n], func=AF.Identity,
                                                             scale=MOD1P(L, jsc, k, which), bias=MOD(L, jsh, k, which)),
                          [xT_T, modv[L], mod1p[L]], [hT_T])
                else:
                    kb.op("dve", E("tensor_scalar", out=ht3[:, k, :n], in0=xt3[:, k, :n], scalar1=MOD1P(L, jsc, k, which),
                                                                scalar2=MOD(L, jsh, k, which), op0=ALU.mult, op1=ALU.add),
                          [xT_T, modv[L], mod1p[L]], [hT_T])

        def load_wbf(dst, dst3, src, nk, ncols, cpp=None):
            sv = src.rearrange("(k p) n -> p k n", p=128)
            for k in range(nk):
                kb.load(dst, dst3[:, k, :], sv[:, k, :], q="pool")

        def stage_win(L, src_x, ncolsW, wname, chunks, dstT, gelu_chunks=()):
            m0 = kb.mark()
            wt = kb.alloc(8 * ncolsW, BF16, "w_in")
            w3 = wt.ap.rearrange("p (k n) -> p k n", k=8)
            load_wbf(wt, w3, W[wname], 8, ncolsW)
            xb = [kb.alloc(8 * 512, F32, f"xb{i}") for i in range(2)]
            hb = [kb.alloc(8 * 512, BF16, f"hb{i}") for i in range(2)]
            stg = [kb.alloc(512, F32, f"stg{i}") for i in range(4)]
            pb = [kb.pbank(i) for i in range(4)]
            xv = xview(src_x)
            it = 0
            for ti, (t0, n) in enumerate(TILES):
                which = 1 if ti == 0 else 0
                xt = xb[ti % 2]; ht = hb[ti % 2]
                xt3 = xt.ap.rearrange("p (k n) -> p k n", k=8); ht3 = ht.ap.rearrange("p (k n) -> p k n", k=8)
                kb.load(xt, xt3[:, :, :n], xv[:, :, t0:t0 + n])
                modulate_tile(L, 0, 1, xt3, ht3, n, which, xt, ht)
                for ci, (c0, M) in enumerate(chunks):
                    ps = pb[it % 4]; sg = stg[it % 4]
                    kb.mm(ps, ps.ap[:M, :n], [(w3[:, k, c0:c0 + M], ht3[:, k, :n]) for k in range(8)], [wt, ht])
                    if ci in gelu_chunks:
                        kb.op("act", E("activation", out=sg.ap[:M, :n], in_=ps.ap[:M, :n], func=AF.Gelu), [ps], [sg])
                    elif it % 2 == 0:
                        kb.op("act", E("copy", out=sg.ap[:M, :n], in_=ps.ap[:M, :n]), [ps], [sg])
                    else:
                        kb.op("dve", E("tensor_copy", out=sg.ap[:M, :n], in_=ps.ap[:M, :n]), [ps], [sg])
                    kb.store(dstT[ci, 0:M, t0:t0 + n], sg, sg.ap[:M, :n])
                    it += 1
            kb.reset(m0)

        def ln_tile(zt, zt3, n, gname, bname, dst3_dram, scr, pbs):
            ps_m, ps_q = pbs
            zb = scr["zb"]; zb3 = zb.ap.rearrange("p (k n) -> p k n", k=8)
            kb.op("act", E("activation", out=zb3[:, :, :n], in_=zt3[:, :, :n], func=AF.Square), [zt], [zb])
            kb.mm(ps_m, ps_m.ap[:, :n], [(CMF(1), zt3[:, k, :n]) for k in range(8)], [zt, cmf])
            kb.mm(ps_q, ps_q.ap[:, :n], [(CMB(1), zb3[:, k, :n]) for k in range(8)], [zb, cmb])
            mean = scr["mean"]; rstd = scr["rstd"]; tmp = scr["tmp"]
            kb.op("act", E("copy", out=mean.ap[:, :n], in_=ps_m.ap[:, :n]), [ps_m], [mean])
            kb.op("act", E("activation", out=tmp.ap[:, :n], in_=ps_m.ap[:, :n], func=AF.Square), [ps_m], [tmp])
            kb.op("dve", E("tensor_tensor", out=tmp.ap[:, :n], in0=ps_q.ap[:, :n], in1=tmp.ap[:, :n], op=ALU.subtract), [ps_q, tmp], [tmp])
            kb.op("dve", E("tensor_scalar", out=tmp.ap[:, :n], in0=tmp.ap[:, :n], scalar1=0.0, scalar2=None, op0=ALU.max), [tmp], [tmp])
            kb.op("act", E("activation", out=tmp.ap[:, :n], in_=tmp.ap[:, :n], func=AF.Sqrt, bias=1e-5), [tmp], [tmp])
            kb.op("dve", E("reciprocal", out=rstd.ap[:, :n], in_=tmp.ap[:, :n]), [tmp], [rstd])
            for k in range(8):
                e1, e2 = ("dve", "pool") if k % 2 == 0 else ("pool", "dve")
                kb.op(e1, E("tensor_tensor", out=zt3[:, k, :n], in0=zt3[:, k, :n], in1=mean.ap[:, :n], op=ALU.subtract), [zt, mean], [zt])
                kb.op(e2, E("tensor_tensor", out=zt3[:, k, :n], in0=zt3[:, k, :n], in1=rstd.ap[:, :n], op=ALU.mult), [zt, rstd], [zt])
                kb.op("act", E("activation", out=zt3[:, k, :n], in_=zt3[:, k, :n], func=AF.Identity, scale=V(gname, k), bias=V(bname, k)), [zt, vecs], [zt])
            kb.store(dst3_dram, zt, zt3[:, :, :n])

        def stage_wout(L, src_x, dst_x, tiles):
            p = f"l{L}_"
            m0 = kb.mark()
            wt = kb.alloc(8 * 1024, BF16, "w_out")
            w3 = wt.ap.rearrange("p (k n) -> p k n", k=8)
            load_wbf(wt, w3, W[p + "w_out"], 8, 1024)
            xb = [kb.alloc(8 * 512, F32, f"xb{i}") for i in range(2)]
            mb_ = [kb.alloc(8 * 512, BF16, f"mb{i}") for i in range(2)]
            zt_ = [kb.alloc(8 * 512, F32, f"z{i}") for i in range(2)]
            scr = {"zb": kb.alloc(8 * 512, BF16, "zb"), "mean": kb.alloc(512, F32, "mean"), "rstd": kb.alloc(512, F32, "rstd"), "tmp": kb.alloc(512, F32, "tmp")}
            pb = [kb.pbank(i) for i in range(4)]
            pst = (kb.pbank(4), kb.pbank(5))
            xv = xview(src_x); dv = xview(dst_x); mv = mixT.rearrange("(k p) n -> p k n", p=128)
            it = 0
            for (ti, t0, n, d0) in tiles:
                which = 1 if ti == 0 else 0
                xt = xb[ti % 2]; mt = mb_[ti % 2]; zt = zt_[ti % 2]
                xt3 = xt.ap.rearrange("p (k n) -> p k n", k=8); mt3 = mt.ap.rearrange("p (k n) -> p k n", k=8); zt3 = zt.ap.rearrange("p (k n) -> p k n", k=8)
                kb.load(xt, xt3[:, :, :n], xv[:, :, t0:t0 + n])
                kb.load(mt, mt3[:, :, :n], mv[:, :, t0:t0 + n], q="act")
                kb.op("pool", E("tensor_scalar", out=xt3[:, :, :n], in0=xt3[:, :, :n], scalar1=ALPHA, scalar2=0.0, op0=ALU.mult, op1=ALU.add), [xt], [xt])
                for oc in range(8):
                    ps = pb[it % 4]; it += 1
                    kb.mm(ps, ps.ap[:, :n], [(w3[:, k, oc * 128:(oc + 1) * 128], mt3[:, k, :n]) for k in range(8)], [wt, mt])
                    kb.op("dve", E("scalar_tensor_tensor", out=zt3[:, oc, :n], in0=ps.ap[:, :n], scalar=MOD(L, 2, oc, which),
                                                                                in1=xt3[:, oc, :n], op0=ALU.mult, op1=ALU.add), [ps, xt, modv[L]], [zt])
                ln_tile(zt, zt3, n, p + "ln1_g", p + "ln1_b", dv[:, :, d0:d0 + n], scr, pst)
            kb.reset(m0)

        def stage_mlp(L, src_x, dst_x, tiles, TW=256):
            p = f"l{L}_"
            m0 = kb.mark()
            w1 = kb.alloc(8 * 4096, BF16, "w1"); w13 = w1.ap.rearrange("p (k n) -> p k n", k=8)
            w2 = kb.alloc(32 * 1024, BF16, "w2"); w23 = w2.ap.rearrange("p (k n) -> p k n", k=32)
            load_wbf(w1, w13, W[p + "mlp_w1"], 8, 4096)
            load_wbf(w2, w23, W[p + "mlp_w2"], 32, 1024)
            xts = [kb.alloc(8 * TW, F32, f"xt{i}") for i in range(2)]
            hts = [kb.alloc(8 * TW, BF16, f"ht{i}") for i in range(2)]
            hid = kb.alloc(32 * TW, BF16, "hid"); hid3 = hid.ap.rearrange("p (k n) -> p k n", k=32)
            rl = [kb.alloc(TW, F32, f"rl{i}") for i in range(3)]
            pb = [kb.pbank(i) for i in range(4)]
            pst = (kb.pbank(4), kb.pbank(5))
            xv = xview(src_x); dv = xview(dst_x)
            it = 0

            def prefetch(i):
                (ti, t0, n, d0) = tiles[i]
                xt = xts[i % 2]; ht = hts[i % 2]
                xt3 = xt.ap.rearrange("p (k n) -> p k n", k=8); ht3 = ht.ap.rearrange("p (k n) -> p k n", k=8)
                kb.load(xt, xt3[:, :, :n], xv[:, :, t0:t0 + n])
                return (xt, xt3, ht, ht3)

            def modul(i, bufs):
                (ti, t0, n, d0) = tiles[i]
                xt, xt3, ht, ht3 = bufs
                modulate_tile(L, 3, 4, xt3, ht3, n, 1 if ti == 0 else 0, xt, ht)
            cur = prefetch(0); modul(0, cur)
            for i, (ti, t0, n, d0) in enumerate(tiles):
                which = 1 if ti == 0 else 0
                xt, xt3, ht, ht3 = cur
                nxt = prefetch(i + 1) if i + 1 < len(tiles) else None
                for fc in range(32):
                    ps = pb[it % 4]; r = rl[it % 3]; it += 1
                    kb.mm(ps, ps.ap[:, :n], [(w13[:, k, fc * 128:(fc + 1) * 128], ht3[:, k, :n]) for k in range(8)], [w1, ht])
                    kb.op("act", E("activation", out=r.ap[:, :n], in_=ps.ap[:, :n], func=AF.Relu), [ps], [r])
                    eng = "dve" if fc % 2 == 0 else "pool"
                    kb.op(eng, E("tensor_tensor", out=hid3[:, fc, :n], in0=r.ap[:, :n], in1=r.ap[:, :n], op=ALU.mult), [r], [hid])
                if nxt is not None:
                    modul(i + 1, nxt)
                kb.op("pool", E("tensor_scalar", out=xt3[:, :, :n], in0=xt3[:, :, :n], scalar1=ALPHA, scalar2=0.0, op0=ALU.mult, op1=ALU.add), [xt, ht], [xt])
                for oc in range(8):
                    ps = pb[it % 4]; it += 1
                    kb.mm(ps, ps.ap[:, :n], [(w23[:, fc, oc * 128:(oc + 1) * 128], hid3[:, fc, :n]) for fc in range(32)], [w2, hid])
                    kb.op("dve", E("scalar_tensor_tensor", out=xt3[:, oc, :n], in0=ps.ap[:, :n], scalar=MOD(L, 5, oc, which),
                                   in1=xt3[:, oc, :n], op0=ALU.mult, op1=ALU.add), [ps, xt, modv[L]], [xt])
                zbT = T(hid.ap[:, 0:8 * TW], "zb_alias"); zbT.b = hid.b
                scr = {"zb": zbT, "mean": rl[0], "rstd": rl[1], "tmp": rl[2]}
                ln_tile(xt, xt3, n, p + "ln2_g", p + "ln2_b", dv[:, :, d0:d0 + n], scr, pst)
                cur = nxt
            kb.reset(m0)

        def stage_mla():
            m0 = kb.mark()
            qT = kb.alloc(8 * NTOK, BF16, "qT"); qT3 = qT.ap.rearrange("p (h n) -> p h n", h=8)
            kT = kb.alloc(8 * NTOK, BF16, "kT"); kT3 = kT.ap.rearrange("p (h n) -> p h n", h=8)
            Va = kb.alloc(18 * 8 * 65, BF16, "Vaug"); Va4 = Va.ap.rearrange("p (c h e) -> p c h e", c=18, h=8)
            negM = kb.alloc(8, F32, "negM")
            qmx = kb.alloc(1, F32, "qmx"); kmx = kb.alloc(1, F32, "kmx")
            m1 = kb.mark()
            wq = kb.alloc(3 * 768, BF16, "wq"); wq3 = wq.ap.rearrange("p (k n) -> p k n", k=3)
            wr = kb.alloc(3 * 768, BF16, "wr"); wr3 = wr.ap.rearrange("p (k n) -> p k n", k=3)
            wk = kb.alloc(2 * 512, BF16, "wk"); wk3 = wk.ap.rearrange("p (k n) -> p k n", k=2)
            wv = kb.alloc(2 * 512, BF16, "wv"); wv3 = wv.ap.rearrange("p (k n) -> p k n", k=2)
            load_wbf(wq, wq3, W["w_uq"], 3, 768); load_wbf(wr, wr3, W["w_uq_rot"], 3, 768)
            load_wbf(wk, wk3, W["w_uk"], 2, 512); load_wbf(wv, wv3, W["w_uv"], 2, 512)
            rope = kb.alloc(2 * NLAT, F32, "rope"); rope3 = rope.ap.rearrange("p (a n) -> p a n", a=2)
            kb.load(rope, rope.ap, rope_in.rearrange("p a n -> p (a n)"))
            ind = kb.alloc(64, BF16, "ind"); ind3 = ind.ap.rearrange("p (a b) -> p a b", a=8)
            kb.load(ind, ind.ap, ind_in.rearrange("p a b -> p (a b)"), q="pool")
            kb.op("pool", E("memset", Va4[:, :, :, 64:65], 1.0), [], [Va])
            kb.op("pool", E("memset", qmx.ap[0:8, :], 0.0), [], [qmx])
            kb.op("pool", E("memset", kmx.ap[0:8, :], 0.0), [], [kmx])
            fq = kb.alloc(3 * 512, F32, "fq"); fq3 = fq.ap.rearrange("p (k n) -> p k n", k=3)
            fkv = kb.alloc(2 * 512, F32, "fkv"); fkv3 = fkv.ap.rearrange("p (k n) -> p k n", k=2)
            krp = kb.alloc(512, F32, "krp"); krr = kb.alloc(512, F32, "krr")
            sq = kb.alloc(3 * 512, F32, "sq"); sq3 = sq.ap.rearrange("p (k n) -> p k n", k=3)
            sd = kb.alloc(512, F32, "sd"); rs = kb.alloc(512, F32, "rs")
            qn = kb.alloc(3 * 512, BF16, "qn"); qn3 = qn.ap.rearrange("p (k n) -> p k n", k=3)
            ckv = kb.alloc(2 * 512, BF16, "ckv"); ckv3 = ckv.ap.rearrange("p (k n) -> p k n", k=2)
            t1 = [kb.alloc(512, F32, f"t1_{i}") for i in range(2)]
            t2 = [kb.alloc(512, F32, f"t2_{i}") for i in range(2)]
            krf = kb.alloc(512, F32, "krf")
            sqq = kb.alloc(8 * 512, BF16, "sqq"); sqq3 = sqq.ap.rearrange("p (h n) -> p h n", h=8)
            nmx = kb.alloc(1, F32, "nmx")
            pA = [kb.pbank(i) for i in range(3)]
            pR = [kb.pbank(3), kb.pbank(4)]
            pK = [kb.pbank(5), kb.pbank(6)]
            pN = kb.pbank(7)
            ia = 0; ir = 0; ik = 0
            for ti, (t0, n) in enumerate(TILES):
                lat = ti > 0
                kb.load(fq, fq3[:, :, :n], F0T[0:3, :, t0:t0 + n].rearrange("k p n -> p k n"))
                kb.load(fkv, fkv3[:, :, :n], F0T[3:5, :, t0:t0 + n].rearrange("k p n -> p k n"))
                kb.load(krp, krp.ap[64:96, :n], F0T[5, 64:96, t0:t0 + n])
                if lat:
                    kb.load(krr, krr.ap[64:96, :n], F0T[6, 64:96, t0:t0 + n])
                kb.op("act", E("activation", out=sq3[:, :, :n], in_=fq3[:, :, :n], func=AF.Square), [fq], [sq])
                kb.mm(pN, pN.ap[:, :n], [(CMF(2), sq3[:, k, :n]) for k in range(3)], [sq, cmf])
                kb.op("act", E("activation", out=sd.ap[:, :n], in_=pN.ap[:, :n], func=AF.Sqrt, bias=1e-6), [pN], [sd])
                kb.op("dve", E("reciprocal", out=rs.ap[:, :n], in_=sd.ap[:, :n]), [sd], [rs])
                for k in range(3):
                    kb.op("dve", E("scalar_tensor_tensor", out=qn3[:, k, :n], in0=fq3[:, k, :n], scalar=V("q_norm", k), in1=rs.ap[:, :n],
                                                                      op0=ALU.mult, op1=ALU.mult), [fq, rs, vecs], [qn])
                if "DBGT" in kb.dbg and ti == 0:
                    dt_ = kb.dram_t("DBGT", [4, 128, 3 * 512], F32)
                    kb.store(dt_[0], sq, sq.ap); kb.store(dt_[1, :, 0:512], sd, sd.ap); kb.store(dt_[2, :, 0:512], rs, rs.ap)
                    kb.store(dt_[3], fq, fq.ap)
                    dt2 = kb.dram_t("DBGT2", [128, 3 * 512], BF16)
                    kb.store(dt2[:, :], qn, qn.ap)
                kb.op("act", E("activation", out=sq3[:, 0:2, :n], in_=fkv3[:, :, :n], func=AF.Square), [fkv, qn], [sq])
                kb.mm(pN, pN.ap[:, :n], [(CMF(3), sq3[:, k, :n]) for k in range(2)], [sq, cmf])
                kb.op("act", E("activation", out=sd.ap[:, :n], in_=pN.ap[:, :n], func=AF.Sqrt, bias=1e-6), [pN], [sd])
                kb.op("dve", E("reciprocal", out=rs.ap[:, :n], in_=sd.ap[:, :n]), [sd], [rs])
                for k in range(2):
                    kb.op("dve", E("scalar_tensor_tensor", out=ckv3[:, k, :n], in0=fkv3[:, k, :n], scalar=V("kv_norm", k), in1=rs.ap[:, :n],
                                                                      op0=ALU.mult, op1=ALU.mult), [fkv, rs, vecs], [ckv])
                if lat:
                    c0 = t0 - NCTX
                    kb.op("dve", E("tensor_tensor", out=krf.ap[64:96, :n], in0=krp.ap[64:96, :n], in1=rope3[64:96, 0, c0:c0 + n], op=ALU.mult), [krp, rope], [krf])
                    kb.op("pool", E("tensor_tensor", out=krr.ap[64:96, :n], in0=krr.ap[64:96, :n], in1=rope3[64:96, 1, c0:c0 + n], op=ALU.mult), [krr, rope], [krr])
                    kb.op("pool", E("tensor_tensor", out=krf.ap[64:96, :n], in0=krf.ap[64:96, :n], in1=krr.ap[64:96, :n], op=ALU.add), [krf, krr], [krf])
                    ksrc = krf
                else:
                    ksrc = krp
                kb.op("pool", E("tensor_copy", out=kT3[64:96, :, t0:t0 + n],
                                                                 in_=bc(ksrc.ap[64:96, :n].rearrange("p (o n) -> p o n", o=1), [32, 8, n])), [ksrc], [kT])
                for h in range(8):
                    pq = pA[ia % 3]; ia += 1
                    kb.mm(pq, pq.ap[0:96, :n], [(wq3[:, k, h * 96:(h + 1) * 96], qn3[:, k, :n]) for k in range(3)], [wq, qn])
                    kb.op("act", E("copy", out=qT3[0:64, h, t0:t0 + n], in_=pq.ap[0:64, :n]), [pq], [qT])
                    if lat:
                        pr = pR[ir % 2]; a1 = t1[ir % 2]; a2 = t2[ir % 2]; ir += 1
                        kb.mm(pr, pr.ap[0:96, :n], [(wr3[:, k, h * 96:(h + 1) * 96], qn3[:, k, :n]) for k in range(3)], [wr, qn])
                        kb.op("dve", E("tensor_tensor", out=a1.ap[64:96, :n], in0=pr.ap[64:96, :n], in1=rope3[64:96, 1, c0:c0 + n], op=ALU.mult), [pr, rope], [a1])
                        kb.op("dve", E("tensor_tensor", out=a2.ap[64:96, :n], in0=pq.ap[64:96, :n], in1=rope3[64:96, 0, c0:c0 + n], op=ALU.mult), [pq, rope], [a2])
                        kb.op("pool", E("tensor_tensor", out=qT3[64:96, h, t0:t0 + n], in0=a1.ap[64:96, :n], in1=a2.ap[64:96, :n], op=ALU.add), [a1, a2], [qT])
                    else:
                        kb.op("act", E("copy", out=qT3[64:96, h, t0:t0 + n], in_=pq.ap[64:96, :n]), [pq], [qT])
                    pk = pK[ik % 2]; ik += 1
                    kb.mm(pk, pk.ap[0:64, :n], [(wk3[:, k, h * 64:(h + 1) * 64], ckv3[:, k, :n]) for k in range(2)], [wk, ckv])
                    kb.op("dve", E("tensor_copy", out=kT3[0:64, h, t0:t0 + n], in_=pk.ap[0:64, :n]), [pk], [kT])
                for j in range(n // 128):
                    pk = pK[ik % 2]; ik += 1
                    kc = (t0 + j * 128) // 128
                    kb.mm(pk, pk.ap[:, :], [(ckv3[:, k, j * 128:(j + 1) * 128], wv3[:, k, :]) for k in range(2)], [wv, ckv])
                    kb.op("act", E("copy", out=Va4[:, kc, :, 0:64], in_=pk.ap.rearrange("p (h e) -> p h e", h=8)), [pk], [Va])
                for (src3, src, mx) in ((qT3, qT, qmx), (kT3, kT, kmx)):
                    kb.op("act", E("activation", out=sqq3[0:96, :, :n], in_=src3[0:96, :, t0:t0 + n], func=AF.Square), [src], [sqq])
                    pk = pK[ik % 2]; ik += 1
                    kb.mm(pk, pk.ap[0:8, :n], [(ind3[0:96, h, :], sqq3[0:96, h, :n]) for h in range(8)], [ind, sqq])
                    kb.op("dve", E("tensor_reduce", out=nmx.ap[0:8, :], in_=pk.ap[0:8, :n], axis=AX.X, op=ALU.max), [pk], [nmx])
                    kb.op("dve", E("tensor_tensor", out=mx.ap[0:8, :], in0=mx.ap[0:8, :], in1=nmx.ap[0:8, :], op=ALU.max), [mx, nmx], [mx])
            dg = kb.alloc(8, F32, "dg")
            kb.op("dve", E("tensor_tensor", out=nmx.ap[0:8, :], in0=qmx.ap[0:8, :], in1=kmx.ap[0:8, :], op=ALU.mult), [qmx, kmx], [nmx])
            kb.op("act", E("activation", out=nmx.ap[0:8, :], in_=nmx.ap[0:8, :], func=AF.Sqrt), [nmx], [nmx])
            kb.op("dve", E("tensor_scalar", out=dg.ap[0:8, :], in0=CMF(0)[0:8, 0:8], scalar1=nmx.ap[0:8, 0:1], scalar2=-1.03 * SCALE, op0=ALU.mult, op1=ALU.mult),
                  [nmx, cmf], [dg])
            pk = pK[ik % 2]; ik += 1
            kb.mm(pk, pk.ap[:, 0:8], [(CMF(7)[0:8, :], dg.ap[0:8, :])], [cmf, dg])
            kb.op("dve", E("tensor_copy", out=negM.ap, in_=pk.ap[:, 0:8]), [pk], [negM])
            if "DBGQK" in kb.dbg:
                dq = kb.dram_t("DBGQK", [2, 128, 8 * NTOK], BF16)
                kb.store(dq[0], qT, qT.ap); kb.store(dq[1], kT, kT.ap)
                dv_ = kb.dram_t("DBGV", [128, 18 * 8 * 65], BF16)
                kb.store(dv_[:, :], Va, Va.ap)
                dn = kb.dram_t("DBGNM", [128, 8], F32)
                kb.store(dn[:, :], negM, negM.ap)
            kb.reset(m1)
            PT = [kb.alloc(512, BF16, f"PT{i}") for i in range(4)]
            osb = [kb.alloc(512, F32, f"osb{i}") for i in range(2)]
            rec = [kb.alloc(512, F32, f"rec{i}") for i in range(2)]
            att = [kb.alloc(512, BF16, f"att{i}") for i in range(2)]
            pS = [kb.pbank(i) for i in range(4)]
            pO = [kb.pbank(4), kb.pbank(5)]
            pB = [kb.pbank(6), kb.pbank(7)]
            isc = 0; io = 0
            for h in range(8):
                for ti, (t0, n) in enumerate(TILES):
                    nk = 2 if ti == 0 else 18
                    po = pO[io % 2]; ob = osb[io % 2]; rc = rec[io % 2]; ao = att[io % 2]; pb_ = pB[io % 2]; io += 1
                    slots = {}
                    for i in range(nk + 2):
                        if i < nk:
                            ps = pS[isc % 4]; pt = PT[isc % 4]; isc += 1
                            slots[i] = (ps, pt)
                            kb.mm1(ps, ps.ap[:, :n], kT3[0:96, h, i * 128:(i + 1) * 128], qT3[0:96, h, t0:t0 + n], [kT, qT])
                            kb.op("act", E("activation", out=pt.ap[:, :n], in_=ps.ap[:, :n], func=AF.Exp, scale=SCALE, bias=negM.ap[:, h:h + 1]),
                                  [ps, negM], [pt])
                        if i >= 2:
                            j = i - 2
                            ps, pt = slots.pop(j)
                            kb.mm1(po, po.ap[0:65, :n], Va4[:, j, h, :], pt.ap[:, :n], [Va, pt], start=(j == 0), stop=(j == nk - 1), sig=(j == nk - 1))
                    kb.op("dve", E("tensor_copy", out=ob.ap[0:65, :n], in_=po.ap[0:65, :n]), [po], [ob])
                    kb.mm(pb_, pb_.ap[0:64, :n], [(CMF(6)[0:65, 0:64], ob.ap[0:65, :n])], [cmf, ob])
                    kb.op("dve", E("reciprocal", out=rc.ap[0:64, :n], in_=pb_.ap[0:64, :n]), [pb_], [rc])
                    kb.op("pool", E("tensor_tensor", out=ao.ap[0:64, :n], in0=ob.ap[0:64, :n], in1=rc.ap[0:64, :n], op=ALU.mult), [ob, rc], [ao])
                    kb.store(mixT[h * 64:(h + 1) * 64, t0:t0 + n], ao, ao.ap[0:64, :n])
            kb.reset(m0)


        def stage_lru():
            m0 = kb.mark()
            gaw = kb.alloc(2048, BF16, "gaw"); gaw4 = gaw.ap.rearrange("p (d k e) -> p d k e", d=2, k=8)
            gxw = kb.alloc(2048, BF16, "gxw"); gxw4 = gxw.ap.rearrange("p (d k e) -> p d k e", d=2, k=8)
            kb.load(gaw, gaw.ap, W["ga_w"].rearrange("p d k e -> p (d k e)"), q="pool")
            kb.load(gxw, gxw.ap, W["gx_w"].rearrange("p d k e -> p (d k e)"), q="pool")
            cl = kb.alloc(16, F32, "cl")
            kb.op("act", E("activation", out=cl.ap, in_=V("lam", 0, 16), func=AF.Exp, scale=-1.0), [vecs], [cl])
            kb.op("act", E("activation", out=cl.ap, in_=cl.ap, func=AF.Ln, bias=1.0), [cl], [cl])
            kb.op("dve", E("tensor_scalar", out=cl.ap, in0=cl.ap, scalar1=-8.0, scalar2=None, op0=ALU.mult), [cl], [cl])
            XW = 2310
            xh = kb.alloc(XW, F32, "xh")
            kb.op("pool", E("memset", xh.ap, 0.0), [], [xh])
            xc = kb.alloc(NTOK, F32, "xc"); xcb = kb.alloc(NTOK, BF16, "xcb")
            rr = kb.alloc(NTOK, F32, "rr"); ig = kb.alloc(NTOK, F32, "ig"); aa = kb.alloc(NTOK, F32, "aa")
            a2 = kb.alloc(NTOK, F32, "a2"); uu = kb.alloc(NTOK, F32, "uu")
            hd = [kb.alloc(NTOK, F32, f"hd{d}") for d in range(2)]
            gg = kb.alloc(NLAT, F32, "gg"); yy = kb.alloc(NLAT, BF16, "yy")
            pb = [kb.pbank(i) for i in range(4)]
            ip = 0
            segs = [(0, 0, 256), (256, 259, 2048)]
            for c in range(8):
                kb.load(xh, xh.ap[:, 2:258], G1T[8 + c, :, 0:256])
                kb.load(xh, xh.ap[:, 261:2309], G1T[8 + c, :, 256:NTOK])
                kb.load(gg, gg.ap, G1T[c, :, 256:NTOK])
                for (o0, b0, ln_) in segs:
                    kb.op("act", E("activation", out=xc.ap[:, o0:o0 + ln_], in_=xh.ap[:, b0:b0 + ln_], func=AF.Identity,
                                   scale=V("conv_w", 0 * 8 + c), bias=V("conv_b", c)), [xh, vecs], [xc])
                    for j in range(1, 4):
                        kb.op("dve", E("scalar_tensor_tensor", out=xc.ap[:, o0:o0 + ln_], in0=xh.ap[:, b0 + j:b0 + j + ln_], scalar=V("conv_w", j * 8 + c),
                                       in1=xc.ap[:, o0:o0 + ln_], op0=ALU.mult, op1=ALU.add), [xh, xc, vecs], [xc])
                kb.op("pool", E("tensor_copy", out=xcb.ap, in_=xc.ap), [xc], [xcb])
                if "DBGL1" in kb.dbg:
                    kb.store(kb.dram["DBGL1"][0, c], xc, xc.ap)
                for d in range(2):
                    for (t0, n) in TILES:
                        ps = pb[ip % 4]; ip += 1
                        kb.mm(ps, ps.ap[:, :n], [(gaw4[:, d, c, :], xcb.ap[:, t0:t0 + n])], [gaw, xcb])
                        kb.op("act", E("activation", out=rr.ap[:, t0:t0 + n], in_=ps.ap[:, :n], func=AF.Sigmoid, bias=V("ga_b", d * 8 + c)), [ps, vecs], [rr])
                        ps = pb[ip % 4]; ip += 1
                        kb.mm(ps, ps.ap[:, :n], [(gxw4[:, d, c, :], xcb.ap[:, t0:t0 + n])], [gxw, xcb])
                        kb.op("act", E("activation", out=ig.ap[:, t0:t0 + n], in_=ps.ap[:, :n], func=AF.Sigmoid, bias=V("gx_b", d * 8 + c)), [ps, vecs], [ig])
                    kb.op("act", E("activation", out=aa.ap, in_=rr.ap, func=AF.Exp, scale=cl.ap[:, d * 8 + c:d * 8 + c + 1]), [rr, cl], [aa])
                    kb.op("pool", E("tensor_tensor", out=a2.ap, in0=aa.ap, in1=aa.ap, op=ALU.mult), [aa], [a2])
                    kb.op("act", E("activation", out=a2.ap, in_=a2.ap, func=AF.Sqrt, scale=-1.0, bias=1.0), [a2], [a2])
                    kb.op("pool", E("tensor_tensor", out=uu.ap, in0=ig.ap, in1=xc.ap, op=ALU.mult), [ig, xc], [uu])
                    kb.op("dve", E("tensor_tensor", out=uu.ap, in0=uu.ap, in1=a2.ap, op=ALU.mult), [uu, a2], [uu])
                    h_ = hd[d]
                    if d == 0:
                        kb.op("dve", E("tensor_tensor_scan", out=h_.ap, data0=aa.ap, data1=uu.ap, initial=0.0, op0=ALU.mult, op1=ALU.add), [aa, uu], [h_])
                    else:
                        kb.op("dve", E("tensor_tensor_scan", out=h_.ap[:, 0:256][:, ::-1], data0=aa.ap[:, 0:256][:, ::-1], data1=uu.ap[:, 0:256][:, ::-1],
                                       initial=0.0, op0=ALU.mult, op1=ALU.add), [aa, uu], [h_])
                        kb.op("dve", E("tensor_tensor_scan", out=h_.ap[:, 256:NTOK][:, ::-1], data0=aa.ap[:, 256:NTOK][:, ::-1], data1=uu.ap[:, 256:NTOK][:, ::-1],
                                       initial=h_.ap[:, 0:1], op0=ALU.mult, op1=ALU.add), [aa, uu, h_], [h_])
                    if "DBGL1" in kb.dbg:
                        kb.store(kb.dram["DBGL1"][1 + d, c], h_, h_.ap)
                kb.op("pool", E("tensor_tensor", out=hd[0].ap[:, 256:NTOK], in0=hd[0].ap[:, 256:NTOK], in1=hd[1].ap[:, 256:NTOK], op=ALU.add), [hd[0], hd[1]], [hd[0]])
                kb.op("dve", E("tensor_tensor", out=yy.ap, in0=hd[0].ap[:, 256:NTOK], in1=gg.ap, op=ALU.mult), [hd[0], gg], [yy])
                kb.store(mixT[c * 128:(c + 1) * 128, 256:NTOK], yy, yy.ap)
            kb.reset(m0)


        def stage_rwkv():
            m0 = kb.mark()
            W_ = 2312
            ORDER = [list(range(NCH)), [1, 0] + list(range(NCH - 1, 1, -1))]
            import os
            SKIP_RWA = bool(os.environ.get("ONLY_RWB"))
            X = [kb.alloc(W_, F32, f"X{i}") for i in range(5)]
            tw = kb.alloc(NTOK, F32, "tw"); alr = kb.alloc(NTOK, F32, "alr"); sg = kb.alloc(NTOK, F32, "sg")
            rS = kb.alloc(NTOK, F32, "rS"); kS = kb.alloc(NTOK, F32, "kS"); vS = kb.alloc(NTOK, F32, "vS")
            kk = kb.alloc(NTOK, F32, "kk"); pbn = kb.alloc(NTOK, F32, "pbn")
            msk = kb.alloc(NTOK + 1, F32, "msk")
            ob = [kb.alloc(NTOK, F32, f"ob{i}") for i in range(3)]
            stg = [kb.alloc(512, F32, f"stg{i}") for i in range(2)]
            mud = kb.alloc(30, F32, "mud"); oka = kb.alloc(4, F32, "oka"); glt = kb.alloc(NCH, F32, "glt")
            w2p = kb.alloc(1024, F32, "w2p"); a2p = kb.alloc(1024, F32, "a2p"); g2t = kb.alloc(512, F32, "g2t")
            w2p3 = w2p.ap.rearrange("p (d n) -> p d n", d=2); a2p3 = a2p.ap.rearrange("p (d n) -> p d n", d=2)
            kb.load(w2p, w2p.ap, W["w2pad"].rearrange("p d n -> p (d n)"))
            kb.load(a2p, a2p.ap, W["a2pad"].rearrange("p d n -> p (d n)"))
            kb.load(g2t, g2t.ap, W["g2"][:, :])
            pb = [kb.pbank(i) for i in range(4)]
            ipb = [0]

            def nps():
                p_ = pb[ipb[0] % 4]; ipb[0] += 1
                return p_
            kb.op("dve", E("tensor_scalar", out=mud.ap[:, 0:15], in0=V("mu", 0, 15), scalar1=-1.0, scalar2=1.0, op0=ALU.mult, op1=ALU.add), [vecs], [mud])
            kb.op("dve", E("tensor_scalar", out=mud.ap[:, 15:30], in0=V("mu", 0, 15), scalar1=0.5, scalar2=None, op0=ALU.mult), [vecs, mud], [mud])
            kb.op("dve", E("tensor_scalar", out=oka.ap, in0=V("k_a", 0, 4), scalar1=-1.0, scalar2=1.0, op0=ALU.mult, op1=ALU.add), [vecs], [oka])
            kb.op("pool", E("memset", msk.ap, 1.0), [], [msk])
            kb.op("pool", E("memset", msk.ap[:, 0:NTOK + 1:128], 0.0), [msk], [msk])
            kb.op("pool", E("memset", X[1].ap, 0.0), [], [X[1]])
            maskf = msk.ap[:, 0:NTOK]; maskr = msk.ap[:, 1:NTOK + 1]

            def shift(j, dst, func=None):
                fc, fh, ss, uu = X[0], X[1], X[2], X[3]
                kb.load(fc, fc.ap[:, 0:NTOK], F0T[7 + j, :, :])
                kb.load(fh, fh.ap[:, 1:257], F0T[7 + j, :, 0:256])
                kb.load(fh, fh.ap[:, 258:2306], F0T[7 + j, :, 256:NTOK])
                kb.op("pool", E("tensor_tensor", out=ss.ap[:, 0:256], in0=fh.ap[:, 0:256], in1=fh.ap[:, 2:258], op=ALU.add), [fh], [ss])
                kb.op("pool", E("tensor_tensor", out=ss.ap[:, 256:NTOK], in0=fh.ap[:, 257:2305], in1=fh.ap[:, 259:2307], op=ALU.add), [fh, ss], [ss])
                kb.op("act", E("activation", out=uu.ap[:, 0:NTOK], in_=fc.ap[:, 0:NTOK], func=AF.Identity, scale=mud.ap[:, j:j + 1]), [fc, mud], [uu])
                kb.op("dve", E("scalar_tensor_tensor", out=dst.ap[:, 0:NTOK], in0=ss.ap[:, 0:NTOK], scalar=mud.ap[:, 15 + j:16 + j], in1=uu.ap[:, 0:NTOK],
                               op0=ALU.mult, op1=ALU.add), [ss, uu, mud], [dst])
                if func is not None:
                    kb.op("act", E("activation", out=dst.ap[:, 0:NTOK], in_=dst.ap[:, 0:NTOK], func=func), [dst], [dst])
            if not SKIP_RWA:
                shift(12, tw, AF.Tanh); shift(13, alr); shift(14, sg, AF.Sigmoid)
            RWU7 = [[RWU[hp, d].rearrange("c p (i t) -> p c i t", i=7) for d in range(2)] for hp in range(4)]
            for hp in range(0 if SKIP_RWA else 4):
                shift(hp, rS); shift(4 + hp, kS); shift(8 + hp, vS)
                for d in range(2):
                    kb.store(RWU7[hp][d][:, :, 6, :], vS, vS.ap.rearrange("p (c t) -> p c t", t=128))
                sq = X[4]
                kb.op("act", E("activation", out=kk.ap, in_=kS.ap, func=AF.Identity, scale=V("k_k", hp)), [kS, vecs], [kk])
                kb.op("pool", E("tensor_tensor", out=sq.ap[:, 0:NTOK], in0=kk.ap, in1=kk.ap, op=ALU.mult), [kk], [sq])
                for (t0, n) in TILES:
                    ps = nps()
                    kb.mm(ps, ps.ap[:, :n], [(CMF(4), sq.ap[:, t0:t0 + n])], [cmf, sq])
                    kb.op("act", E("activation", out=X[3].ap[:, t0:t0 + n], in_=ps.ap[:, :n], func=AF.Sqrt, bias=1e-12), [ps], [X[3]])
                kb.op("dve", E("reciprocal", out=X[3].ap[:, 0:NTOK], in_=X[3].ap[:, 0:NTOK]), [X[3]], [X[3]])
                kb.op("pool", E("tensor_tensor", out=kk.ap, in0=kk.ap, in1=X[3].ap[:, 0:NTOK], op=ALU.mult), [kk, X[3]], [kk])
                if "DBGRW" in kb.dbg:
                    kb.store(kb.dram["DBGRW"][0, hp], kk, kk.ap)
                iob = 0
                for d in range(2):
                    x1, x2, x3, x4, x5 = [X[i] for i in range(5)]
                    x1a = x1.ap[:, 0:NTOK]; x2a = x2.ap[:, 0:NTOK]; x3a = x3.ap[:, 0:NTOK]; x4a = x4.ap[:, 0:NTOK]; x5a = x5.ap[:, 0:NTOK]
                    for (t0, n) in TILES:
                        ps = nps()
                        kb.mm(ps, ps.ap[:, :n], [(w2p3[:, d, hp * 128:(hp + 1) * 128], tw.ap[:, t0:t0 + n])], [w2p, tw])
                        kb.op("act", E("activation", out=x1.ap[:, t0:t0 + n], in_=ps.ap[:, :n], func=AF.Sigmoid, bias=V("w0", d * 4 + hp)), [ps, vecs], [x1])
                        ps = nps()
                        kb.mm(ps, ps.ap[:, :n], [(a2p3[:, d, hp * 128:(hp + 1) * 128], alr.ap[:, t0:t0 + n])], [a2p, alr])
                        kb.op("act", E("activation", out=x2.ap[:, t0:t0 + n], in_=ps.ap[:, :n], func=AF.Sigmoid, bias=V("a0", d * 4 + hp)), [ps, vecs], [x2])
                    if "DBGRW" in kb.dbg:
                        kb.store(kb.dram["DBGRW"][1 + d, hp], x1, x1a)
                        kb.store(kb.dram["DBGRW"][3 + d, hp], x2, x2a)
                    kb.op("dve", E("tensor_scalar", out=x3a, in0=x2a, scalar1=V("k_a", hp), scalar2=oka.ap[:, hp:hp + 1], op0=ALU.mult, op1=ALU.add), [x2, vecs, oka], [x3])
                    kb.op("pool", E("tensor_tensor", out=x3a, in0=x3a, in1=kS.ap, op=ALU.mult), [x3, kS], [x3])
                    kb.op("pool", E("tensor_tensor", out=x2a, in0=x2a, in1=kk.ap, op=ALU.mult), [x2, kk], [x2])
                    if d == 0:
                        kb.op("dve", E("tensor_tensor_scan", out=x4a, data0=maskf, data1=x1a, initial=0.0, op0=ALU.mult, op1=ALU.add), [msk, x1], [x4])
                    else:
                        kb.op("dve", E("tensor_tensor_scan", out=x4a[:, ::-1], data0=maskr[:, ::-1], data1=x1a[:, ::-1], initial=0.0, op0=ALU.mult, op1=ALU.add), [msk, x1], [x4])
                    kb.op("pool", E("tensor_tensor", out=x1a, in0=x4a, in1=x1a, op=ALU.subtract), [x4, x1], [x1])
                    kb.op("act", E("activation", out=x5a, in_=x1a, func=AF.Exp, scale=-CDEC), [x1], [x5])
                    o = ob[iob % 3]; iob += 1
                    kb.op("pool", E("tensor_tensor", out=o.ap, in0=kk.ap, in1=x5a, op=ALU.mult), [kk, x5], [o])
                    kb.store(RWU7[hp][d][:, :, 0, :], o, o.ap.rearrange("p (c t) -> p c t", t=128))
                    kb.op("act", E("activation", out=x5a, in_=x4a, func=AF.Exp, scale=-CDEC), [x4, o], [x5])
                    o = ob[iob % 3]; iob += 1
                    kb.op("dve", E("tensor_tensor", out=o.ap, in0=rS.ap, in1=x5a, op=ALU.mult), [rS, x5], [o])
                    kb.store(RWU7[hp][d][:, :, 1, :], o, o.ap.rearrange("p (c t) -> p c t", t=128))
                    e1v = x5a.rearrange("p (c t) -> p c t", t=128)
                    kb.op("dve", E("tensor_copy", out=glt.ap, in_=(e1v[:, :, 127] if d == 0 else e1v[:, :, 0])), [x5], [glt])
                    kb.store(RWGL[hp, d], glt, glt.ap)
                    kb.op("act", E("activation", out=x1a, in_=x4a, func=AF.Exp, scale=CDEC), [x4, x1], [x1])
                    o = ob[iob % 3]; iob += 1
                    kb.op("pool", E("tensor_tensor", out=o.ap, in0=x3a, in1=x1a, op=ALU.mult), [x3, x1], [o])
                    kb.store(RWU7[hp][d][:, :, 2, :], o, o.ap.rearrange("p (c t) -> p c t", t=128))
                    o = ob[iob % 3]; iob += 1
                    kb.op("dve", E("tensor_tensor", out=o.ap, in0=x2a, in1=x1a, op=ALU.mult), [x2, x1], [o])
                    kb.store(RWU7[hp][d][:, :, 3, :], o, o.ap.rearrange("p (c t) -> p c t", t=128))
                    x1v = x1a.rearrange("p (c t) -> p c t", t=128)
                    kb.op("dve", E("tensor_tensor", out=x1v, in0=x1v, in1=bc(glt.ap.rearrange("p (c o) -> p c o", o=1), [128, NCH, 128]), op=ALU.mult), [x1, glt], [x1])
                    o = ob[iob % 3]; iob += 1
                    kb.op("pool", E("tensor_tensor", out=o.ap, in0=x3a, in1=x1a, op=ALU.mult), [x3, x1], [o])
                    kb.store(RWU7[hp][d][:, :, 4, :], o, o.ap.rearrange("p (c t) -> p c t", t=128))
                    o = ob[iob % 3]; iob += 1
                    kb.op("dve", E("tensor_tensor", out=o.ap, in0=x2a, in1=x1a, op=ALU.mult), [x2, x1], [o])
                    kb.store(RWU7[hp][d][:, :, 5, :], o, o.ap.rearrange("p (c t) -> p c t", t=128))
                    if d == 0:
                        kb.op("dve", E("scalar_tensor_tensor", out=pbn.ap, in0=rS.ap, scalar=V("r_k", hp), in1=x3a, op0=ALU.mult, op1=ALU.mult), [rS, x3, vecs], [pbn])
                    else:
                        kb.op("dve", E("scalar_tensor_tensor", out=x4a, in0=rS.ap, scalar=V("r_k", hp), in1=x3a, op0=ALU.mult, op1=ALU.mult), [rS, x3, vecs, x4], [x4])
                        kb.op("pool", E("tensor_tensor", out=pbn.ap, in0=pbn.ap, in1=x4a, op=ALU.add), [pbn, x4], [pbn])
                ist = 0
                for (t0, n) in TILES:
                    ps = nps(); sgb = stg[ist % 2]; ist += 1
                    kb.mm(ps, ps.ap[:, :n], [(CMF(4), pbn.ap[:, t0:t0 + n])], [cmf, pbn])
                    kb.op("dve", E("tensor_tensor", out=sgb.ap[:, :n], in0=ps.ap[:, :n], in1=vS.ap[:, t0:t0 + n], op=ALU.mult), [ps, vS], [sgb])
                    kb.store(RWBON[hp, :, t0:t0 + n], sgb, sgb.ap[:, :n])
                    ps = nps(); sgb = stg[ist % 2]; ist += 1
                    kb.mm(ps, ps.ap[:, :n], [(g2t.ap[:, hp * 128:(hp + 1) * 128], sg.ap[:, t0:t0 + n])], [g2t, sg])
                    kb.op("act", E("copy", out=sgb.ap[:, :n], in_=ps.ap[:, :n]), [ps], [sgb])
                    kb.store(RWG[hp, :, t0:t0 + n], sgb, sgb.ap[:, :n])
            kb.reset(m0)
            if stop_after == "rwa":
                return
            import os
            yT = [kb.alloc(NTOK, F32, f"yT{hp}") for hp in range(4)]
            m1 = kb.mark()
            NU = 4
            mk = kb.alloc(2 * 1280, F32, "mk"); mk3 = mk.ap.rearrange("p (d m) -> p d m", d=2)
            kb.load(mk, mk.ap, msk_in.rearrange("p d m -> p (d m)"))
            glall = kb.alloc(4 * 2 * NCH, F32, "glall"); gl4 = glall.ap.rearrange("p (a d c) -> p a d c", a=4, d=2)
            kb.load(glall, gl4, RWGL.rearrange("a d p c -> p a d c"))
            U7 = [[kb.alloc(896, F32, f"U7_{p}_{u}") for u in range(NU)] for p in range(2)]
            PD = [[kb.alloc(768, F32, f"PD_{p}_{u}") for u in range(NU)] for p in range(2)]
            for p in range(2):
                for u in range(NU):
                    kb.op("pool", E("memset", PD[p][u].ap, 0.0), [], [PD[p][u]])
            KBV = [kb.alloc(384, F32, f"KBV{u}") for u in range(NU)]
            VP = [kb.alloc(256, F32, f"VP{u}") for u in range(NU)]
            UP = [kb.alloc(256, F32, f"UP{u}") for u in range(NU)]
            BCt = [kb.alloc(512, F32, f"BC{u}") for u in range(NU)]
            ZDt = [kb.alloc(512, F32, f"ZD{u}") for u in range(NU)]
            X0T = [kb.alloc(256, F32, f"X0T{u}") for u in range(NU)]
            XX = [[kb.alloc(512, F32, f"XX{q}{u}") for u in range(NU)] for q in range(2)]
            RR = [[kb.alloc(256, F32, f"RR{q}{u}") for u in range(NU)] for q in range(2)]
            XIN = [kb.alloc(128, F32, f"XIN{u}") for u in range(NU)]
            UNt = [kb.alloc(128, F32, f"UN{u}") for u in range(NU)]
            Sst = [[kb.alloc(128, F32, f"S{p}{u}") for u in range(NU)] for p in range(2)]
            TMPS = [kb.alloc(128, F32, f"tmpS{u}") for u in range(NU)]
            for u in range(NU):
                kb.op("pool", E("memset", VP[u].ap, 0.0), [], [VP[u]])
                kb.op("pool", E("memset", UP[u].ap, 0.0), [], [UP[u]])
            bks = [kb.pbank(i) for i in range(8)]
            ib = [0]

            def nb():
                b_ = bks[ib[0] % 8]; ib[0] += 1
                return b_
            NST = int(os.environ.get('RWB_STEPS', NCH))
            ident2 = bc(CMF(0).rearrange("p (o n) -> p o n", o=1), [128, 2, 128])
            gstep = 0
            for d in range(2):
                for u in range(NU):
                    kb.op("pool", E("memset", Sst[gstep % 2][u].ap, 0.0), [], [Sst[gstep % 2][u]])

                def loads(s_, p):
                    for u in range(NU):
                        hp = u
                        c = ORDER[d][s_]
                        src = RWU[hp, d, c]
                        kb.load(U7[p][u], U7[p][u].ap, src[:, :])
                        pd4 = PD[p][u].ap.rearrange("p (w a t) -> p w a t", w=3, a=2)
                        for w, slot in enumerate((2, 3, 0)):
                            kb.load(PD[p][u], pd4[0:64, w, 0, :], src[0:64, slot * 128:(slot + 1) * 128])
                            kb.load(PD[p][u], pd4[64:128, w, 1, :], src[64:128, slot * 128:(slot + 1) * 128])
                loads(0, gstep % 2)
                for s_ in range(NST):
                    p = gstep % 2; po = 1 - p
                    if s_ + 1 < NST:
                        loads(s_ + 1, po)
                    c = ORDER[d][s_]
                    for u in range(NU):
                        u7 = U7[p][u]; pd4 = PD[p][u].ap.rearrange("p (w a t) -> p w a t", w=3, a=2)
                        bT = nb()
                        for j, slot in enumerate((4, 5, 6)):
                            kb.S.op("pe", E("transpose", out=bT.ap[:, j * 128:(j + 1) * 128], in_=u7.ap[:, slot * 128:(slot + 1) * 128], identity=CMF(0)),
                                    reads=[u7, cmf], writes=[bT], sig=(j == 2))
                        kb.op("act", E("copy", out=KBV[u].ap, in_=bT.ap[:, 0:384]), [bT], [KBV[u]])
                        vp64 = VP[u].ap.rearrange("p (a c) -> p a c", c=64)
                        kb.op("pool", E("tensor_copy", out=vp64[:, 0:4:3, :], in_=KBV[u].ap[:, 256:384].rearrange("p (a c) -> p a c", a=2)), [KBV[u]], [VP[u]])
                        rk_ = u7.ap[:, 0:256]
                        b1 = nb()
                        for a_ in range(2):
                            kb.mm1(b1, b1.ap[:, a_ * 256:(a_ + 1) * 256], pd4[:, 0, a_, :], rk_, [PD[p][u], u7])
                        kb.op("dve", E("tensor_tensor", out=BCt[u].ap, in0=b1.ap, in1=mk3[:, d, 0:512], op=ALU.mult), [b1, mk], [BCt[u]])
                        b1 = nb()
                        for a_ in range(2):
                            kb.mm1(b1, b1.ap[:, a_ * 256:(a_ + 1) * 256], pd4[:, 1, a_, :], rk_, [PD[p][u], u7])
                        kb.op("dve", E("tensor_tensor", out=ZDt[u].ap, in0=b1.ap, in1=mk3[:, d, 512:1024], op=ALU.mult), [b1, mk], [ZDt[u]])
                        zd3 = ZDt[u].ap.rearrange("p (a m) -> p a m", a=2)
                        kb.op("pool", E("tensor_tensor", out=RR[0][u].ap.rearrange("p (a m) -> p a m", a=2), in0=zd3[:, :, 0:128], in1=ident2, op=ALU.add), [ZDt[u], cmf], [RR[0][u]])
                        b1 = nb()
                        for a_ in range(2):
                            kb.mm1(b1, b1.ap[:, a_ * 128:(a_ + 1) * 128], pd4[:, 2, a_, :], u7.ap[:, 384:512], [PD[p][u], u7])
                        kb.op("dve", E("tensor_tensor", out=X0T[u].ap, in0=b1.ap[:, 0:256], in1=mk3[:, d, 1024:1280], op=ALU.mult), [b1, mk], [X0T[u]])
                    for lvl in range(6):
                        q0 = lvl % 2; q1 = 1 - q0
                        hs = {}
                        for u in range(NU):
                            b1 = nb(); hs[u] = b1
                            for a_ in range(2):
                                if lvl == 0:
                                    Xk = ZDt[u].ap[:, a_ * 256:a_ * 256 + 128]; XTk = X0T[u].ap[:, a_ * 128:(a_ + 1) * 128]; rd = [ZDt[u], X0T[u]]
                                else:
                                    Xk = XX[q0][u].ap[:, a_ * 256:a_ * 256 + 128]; XTk = XX[q0][u].ap[:, a_ * 256 + 128:a_ * 256 + 256]; rd = [XX[q0][u]]
                                if lvl < 5:
                                    kb.mm1(b1, b1.ap[:, a_ * 256:a_ * 256 + 128], XTk, Xk, rd)
                                kb.mm1(b1, b1.ap[:, a_ * 256 + 128:a_ * 256 + 256], Xk, XTk, rd)
                        for u in range(NU):
                            b1 = hs[u]
                            if lvl < 5:
                                kb.op("act", E("copy", out=XX[q1][u].ap, in_=b1.ap), [b1], [XX[q1][u]])
                            else:
                                kb.op("act", E("copy", out=XX[q1][u].ap.rearrange("p (a m) -> p a m", a=2)[:, :, 128:256],
                                               in_=b1.ap.rearrange("p (a m) -> p a m", a=2)[:, :, 128:256]), [b1], [XX[q1][u]])
                        for u in range(NU):
                            b1 = nb(); hs[u] = b1
                            for a_ in range(2):
                                kb.mm1(b1, b1.ap[:, a_ * 128:(a_ + 1) * 128], XX[q1][u].ap[:, a_ * 256 + 128:a_ * 256 + 256], RR[q0][u].ap[:, a_ * 128:(a_ + 1) * 128],
                                       [XX[q1][u], RR[q0][u]])
                        for u in range(NU):
                            b1 = hs[u]
                            kb.op("dve", E("tensor_tensor", out=RR[q1][u].ap, in0=b1.ap[:, 0:256], in1=RR[q0][u].ap, op=ALU.add), [b1, RR[q0][u]], [RR[q1][u]])
                    RF = RR[0]
                    hx = {}
                    for u in range(NU):
                        hp = u
                        kb.op("pool", E("tensor_scalar", out=TMPS[u].ap, in0=Sst[p][u].ap, scalar1=gl4[:, hp, d, c:c + 1], scalar2=0.0, op0=ALU.mult, op1=ALU.add), [Sst[p][u], glall], [TMPS[u]])
                        b1 = nb(); hx[u] = b1
                        u7 = U7[p][u]
                        kb.mm1(b1, b1.ap[:, 0:128], u7.ap[:, 0:128], Sst[p][u].ap, [u7, Sst[p][u]], start=True, stop=False, sig=False)
                        kb.mm1(b1, b1.ap[:, 0:64], BCt[u].ap[:, 0:128], KBV[u].ap[:, 256:320], [BCt[u], KBV[u]], start=False, stop=False, sig=False)
                        kb.mm1(b1, b1.ap[:, 64:128], BCt[u].ap[:, 256:384], KBV[u].ap[:, 320:384], [BCt[u], KBV[u]], start=False, stop=True, sig=True)
                    for u in range(NU):
                        kb.op("act", E("copy", out=XIN[u].ap, in_=hx[u].ap[:, 0:128]), [hx[u]], [XIN[u]])
                    for u in range(NU):
                        b1 = nb(); hx[u] = b1
                        for a_ in range(2):
                            kb.mm1(b1, b1.ap[:, a_ * 64:(a_ + 1) * 64], RF[u].ap[:, a_ * 128:(a_ + 1) * 128], XIN[u].ap[:, a_ * 64:(a_ + 1) * 64], [RF[u], XIN[u]])
                    for u in range(NU):
                        kb.op("dve", E("tensor_scalar", out=UNt[u].ap, in0=hx[u].ap[:, 0:128], scalar1=-1.0, scalar2=None, op0=ALU.mult), [hx[u]], [UNt[u]])
                        up64 = UP[u].ap.rearrange("p (a c) -> p a c", c=64)
                        kb.op("pool", E("tensor_copy", out=up64[:, 0:4:3, :], in_=UNt[u].ap.rearrange("p (a c) -> p a c", a=2)), [UNt[u]], [UP[u]])
                    for u in range(NU):
                        b1 = nb(); hx[u] = b1
                        kb.mm(b1, b1.ap[:, 0:128], [(KBV[u].ap[:, 0:128], KBV[u].ap[:, 256:384]), (KBV[u].ap[:, 128:256], UNt[u].ap)], [KBV[u], UNt[u]])
                    for u in range(NU):
                        kb.op("dve", E("tensor_tensor", out=Sst[po][u].ap, in0=hx[u].ap[:, 0:128], in1=CMF(8), op=ALU.mult), [hx[u], cmf], [Sst[po][u]])
                        kb.op("pool", E("tensor_tensor", out=Sst[po][u].ap, in0=Sst[po][u].ap, in1=TMPS[u].ap, op=ALU.add), [Sst[po][u], TMPS[u]], [Sst[po][u]])
                    for u in range(NU):
                        hp = u
                        b1 = nb(); u7 = U7[p][u]
                        vp3 = VP[u].ap.rearrange("p (a c) -> p a c", a=2); up3 = UP[u].ap.rearrange("p (a c) -> p a c", a=2)
                        kb.mm(b1, b1.ap[:, 0:128], [(Sst[p][u].ap, u7.ap[:, 128:256]),
                                                     (vp3[:, 0, :], BCt[u].ap[:, 128:256]), (vp3[:, 1, :], BCt[u].ap[:, 384:512]),
                                                     (up3[:, 0, :], ZDt[u].ap[:, 128:256]), (up3[:, 1, :], ZDt[u].ap[:, 384:512])],
                              [Sst[p][u], u7, VP[u], UP[u], BCt[u], ZDt[u]])
                        if "DBGYD" in kb.dbg:
                            kb.op("dve", E("tensor_copy", out=TMPS[u].ap, in_=b1.ap[:, 0:128]), [b1], [TMPS[u]])
                            kb.store(kb.dram["DBGYD"][d, hp, :, c * 128:(c + 1) * 128], TMPS[u], TMPS[u].ap)
                        ycol = yT[hp].ap[:, c * 128:(c + 1) * 128]
                        if d == 0:
                            kb.op("dve", E("tensor_copy", out=ycol, in_=b1.ap[:, 0:128]), [b1], [yT[hp]])
                        else:
                            kb.op("dve", E("tensor_tensor", out=ycol, in0=b1.ap[:, 0:128], in1=ycol, op=ALU.add), [b1, yT[hp]], [yT[hp]])
                    gstep += 1
            if stop_after == "rwb":
                return
            if "DBGY" in kb.dbg:
                for hp in range(4):
                    kb.store(kb.dram["DBGY"][hp], yT[hp], yT[hp].ap)
            kb.reset(m1)
            gb = [kb.alloc(512, F32, f"gb{i}") for i in range(2)]
            bb_ = [kb.alloc(512, F32, f"bb{i}") for i in range(2)]
            dv_ = [kb.alloc(512, F32, f"dv{i}") for i in range(2)]
            sq_ = [kb.alloc(512, F32, f"sq{i}") for i in range(2)]
            oo = [kb.alloc(512, BF16, f"oo{i}") for i in range(2)]
            pbk = [kb.pbank(i) for i in range(4)]
            it = 0
            for hp in range(4):
                for (t0, n) in TILES:
                    g_ = gb[it % 2]; b_ = bb_[it % 2]; dd = dv_[it % 2]; qq = sq_[it % 2]; o_ = oo[it % 2]
                    pm = pbk[(2 * it) % 4]; pv = pbk[(2 * it + 1) % 4]; it += 1
                    kb.load(g_, g_.ap[:, :n], RWG[hp, :, t0:t0 + n])
                    kb.load(b_, b_.ap[:, :n], RWBON[hp, :, t0:t0 + n])
                    ysl = yT[hp].ap[:, t0:t0 + n]
                    kb.mm(pm, pm.ap[:, :n], [(CMF(5), ysl)], [cmf, yT[hp]])
                    kb.op("dve", E("tensor_tensor", out=dd.ap[:, :n], in0=ysl, in1=pm.ap[:, :n], op=ALU.subtract), [yT[hp], pm], [dd])
                    kb.op("act", E("activation", out=qq.ap[:, :n], in_=dd.ap[:, :n], func=AF.Square), [dd], [qq])
                    kb.mm(pv, pv.ap[:, :n], [(CMF(5), qq.ap[:, :n])], [cmf, qq])
                    kb.op("act", E("activation", out=qq.ap[:, :n], in_=pv.ap[:, :n], func=AF.Sqrt, bias=64e-5), [pv, qq], [qq])
                    kb.op("dve", E("reciprocal", out=qq.ap[:, :n], in_=qq.ap[:, :n]), [qq], [qq])
                    kb.op("dve", E("scalar_tensor_tensor", out=dd.ap[:, :n], in0=dd.ap[:, :n], scalar=V("gn_w", hp), in1=qq.ap[:, :n], op0=ALU.mult, op1=ALU.mult), [dd, qq, vecs], [dd])
                    kb.op("dve", E("scalar_tensor_tensor", out=dd.ap[:, :n], in0=dd.ap[:, :n], scalar=V("gn_b", hp), in1=b_.ap[:, :n], op0=ALU.add, op1=ALU.add), [dd, b_, vecs], [dd])
                    kb.op("pool", E("tensor_tensor", out=o_.ap[:, :n], in0=dd.ap[:, :n], in1=g_.ap[:, :n], op=ALU.mult), [dd, g_], [o_])
                    kb.store(mixT[512 + hp * 128:512 + (hp + 1) * 128, t0:t0 + n], o_, o_.ap[:, :n])
            kb.reset(m0)

        ALLT = [(ti, t0, n, t0) for ti, (t0, n) in enumerate(TILES)]
        LATT = [(ti, t0, n, t0 - NCTX) for ti, (t0, n) in enumerate(TILES) if ti > 0]

        MLPA = [(0 if t0 == 0 else 1, t0, 256, t0) for t0 in range(0, NTOK, 256)]
        MLPL = [(1, t0, 256, t0 - NCTX) for t0 in range(NCTX, NTOK, 256)]
        LATI = [(ti, t0, n, t0) for ti, (t0, n) in enumerate(TILES) if ti > 0]
        chunks0 = [(i * 128, 128) for i in range(5)] + [(640, 96), (736, 96)] + [(832 + i * 128, 128) for i in range(15)]
        chunks1 = [(i * 128, 128) for i in range(16)]
        if "DBGL1" in kb.dbg:
            kb.dram_t("DBGL1", [3, 8, 128, NTOK], F32)
        if "DBGRW" in kb.dbg:
            kb.dram_t("DBGRW", [5, 4, 128, NTOK], F32)
        if "DBGY" in kb.dbg:
            kb.dram_t("DBGY", [4, 128, NTOK], F32)
        if "DBGYD" in kb.dbg:
            kb.dram_t("DBGYD", [2, 4, 128, NTOK], F32)

        def fin():
            S.run()
            return nc
        import os as _os
        if _os.environ.get("ONLY_RWB"):
            stage_rwkv()
            return fin()
        if start_layer == 0:
            stage_mod(0)
            if "DBGMOD" in kb.dbg:
                dm = kb.dram_t("DBGMOD", [128, 96], F32)
                kb.store(dm[:, :], modv[0], modv[0].ap)
            if stop_after == "mod":
                return fin()
            stage_win(0, xT_in, 2752, "w_in0", chunks0, F0T)
            if stop_after == "win0":
                return fin()
            stage_mla()
            if stop_after == "mla":
                return fin()
            stage_rwkv()
            if stop_after in ("rwkv", "rwa", "rwb"):
                return fin()
            stage_wout(0, xT_in, xT, ALLT)
            if stop_after == "wout0":
                return fin()
            stage_mlp(0, xT, xT, MLPA)
            if stop_after == "mlp0":
                return fin()
            x1src = xT
        else:
            x1src = xT_in
        stage_mod(1)
        stage_win(1, x1src, 2048, "w_in1", chunks1, G1T, gelu_chunks=set(range(8)))
        if stop_after == "win1":
            return fin()
        stage_lru()
        if stop_after == "lru":
            return fin()
        stage_wout(1, x1src, xT, LATI)
        if stop_after == "wout1":
            return fin()
        stage_mlp(1, xT, outT, MLPL)
        S.run()
    return nc


def _rope_tables():
    rows = np.repeat(np.arange(32, dtype=np.float32), 64)
    cols = np.tile(np.arange(64, dtype=np.float32), 32)
    inv = (10000.0 ** (-np.arange(0, 16, 2, dtype=np.float32) / 16)).astype(np.float32)
    ar = rows[:, None] * inv
    ac = cols[:, None] * inv
    ang = np.concatenate([ar, ar, ac, ac], -1)
    cos = np.cos(ang); sin = np.sin(ang)
    sgn = np.tile(np.concatenate([-np.ones(8), np.ones(8)]), 2).astype(np.float32)
    out = np.zeros((128, 2, NLAT), np.float32)
    out[64:96, 0, :] = cos.T
    out[64:96, 1, :] = (sin * sgn).T
    return out


def _rot_perm():
    perm = np.zeros(32, np.int64)
    for a in range(2):
        for h in range(2):
            for f in range(8):
                perm[a * 16 + h * 8 + f] = a * 16 + (1 - h) * 8 + f
    return perm


def prep_inputs(inputs):
    I = {k: np.asarray(v) for k, v in inputs.items()}
    shared = {}
    cmats = np.zeros((128, 9, 128), np.float32)
    cmats[:, 0] = np.eye(128)
    cmats[:, 1] = 1.0 / 1024
    cmats[:, 2] = 1.0 / 384
    cmats[:, 3] = 1.0 / 256
    bo = np.zeros((128, 128), np.float32); bo[:64, :64] = 1; bo[64:, 64:] = 1
    cmats[:, 4] = bo
    cmats[:, 5] = bo / 64
    cmats[64, 6, :64] = 1.0
    cmats[:, 7] = 1.0
    cmats[:, 8] = bo
    shared["cmats"] = cmats
    shared["rope"] = _rope_tables()
    ind = np.zeros((128, 8, 8), np.float32)
    for h in range(8):
        ind[:96, h, h] = 1.0
    shared["ind8"] = ind
    ii = np.arange(128)[:, None]; tt = np.arange(128)[None, :]
    msk = np.zeros((128, 2, 1280), np.float32)
    for d, (st_, inc_) in enumerate((((ii < tt), (ii <= tt)), ((ii > tt), (ii >= tt)))):
        st_ = st_.astype(np.float32); inc_ = inc_.astype(np.float32)
        msk[:, d, 0:512] = np.concatenate([st_, inc_, st_, inc_], 1)
        msk[:, d, 512:1024] = np.concatenate([-st_, inc_, -st_, inc_], 1)
        msk[:, d, 1024:1280] = np.concatenate([-st_.T, -st_.T], 1)
    shared["rwmask"] = msk
    for L in range(2):
        p = f"l{L}_"
        for nm in ("mod_w", "w_out", "mlp_w1", "mlp_w2"):
            shared[p + nm] = np.ascontiguousarray(I[p + nm], np.float32)
    w_in = I["l0_w_in"]
    perm = _rot_perm()
    w0e = np.zeros((1024, 2752), np.float32)
    w0e[:, 0:640] = w_in[:, 0:640]
    w0e[:, 640 + 64:640 + 96] = w_in[:, 640:672]
    w0e[:, 736 + 64:736 + 96] = w_in[:, 640:672][:, perm]
    w0e[:, 832:] = w_in[:, 672:]
    shared["w_in0"] = w0e
    wuq = I["l0_mla_w_uq"].reshape(384, 8, 96)
    wrot = np.zeros_like(wuq)
    wrot[:, :, 64:96] = wuq[:, :, 64:96][:, :, perm]
    shared["w_uq"] = np.ascontiguousarray(wuq.reshape(384, 768))
    shared["w_uq_rot"] = np.ascontiguousarray(wrot.reshape(384, 768))
    shared["w_uk"] = np.ascontiguousarray(I["l0_mla_w_uk"])
    shared["w_uv"] = np.ascontiguousarray(I["l0_mla_w_uv"])
    for nm, src in (("w2pad", "l0_rwkv_w2"), ("a2pad", "l0_rwkv_a2")):
        a = np.zeros((128, 2, 512), np.float32)
        a[0:64, 0] = I[src][0]; a[64:128, 1] = I[src][1]
        shared[nm] = a
    shared["g2"] = np.ascontiguousarray(I["l0_rwkv_g2"])
    shared["w_in1"] = np.ascontiguousarray(I["l1_w_in"])
    shared["ga_w"] = np.ascontiguousarray(np.transpose(I["l1_lru_ga_w"], (2, 0, 1, 3)))
    shared["gx_w"] = np.ascontiguousarray(np.transpose(I["l1_lru_gx_w"], (2, 0, 1, 3)))
    vbase = np.zeros((128, NCOL), np.float32)

    def put(name, arr, n):
        vbase[:, COLS[name]:COLS[name] + n] = _pcol(arr, n)
    put("cctxT", I["c_ctx"], 8)
    for L in range(2):
        p = f"l{L}_"
        put(p + "mod_b", I[p + "mod_b"], 48)
        for nm in ("ln1_g", "ln1_b", "ln2_g", "ln2_b"):
            put(p + nm, I[p + nm], 8)
    put("q_norm", I["l0_mla_q_norm"], 3); put("kv_norm", I["l0_mla_kv_norm"], 2); put("mu", I["l0_rwkv_mu"], 15)
    put("w0", I["l0_rwkv_w0"].reshape(-1), 8); put("a0", I["l0_rwkv_a0"].reshape(-1), 8)
    put("k_k", I["l0_rwkv_k_k"], 4); put("k_a", I["l0_rwkv_k_a"], 4); put("r_k", I["l0_rwkv_r_k"].reshape(-1), 4)
    put("gn_w", I["l0_rwkv_gn_w"], 4); put("gn_b", I["l0_rwkv_gn_b"], 4)
    put("conv_w", I["l1_conv_w"].reshape(-1), 32); put("conv_b", I["l1_conv_b"], 8)
    put("ga_b", I["l1_lru_ga_b"].reshape(-1), 16); put("gx_b", I["l1_lru_gx_b"].reshape(-1), 16)
    put("lam", I["l1_lru_lambda"].reshape(-1), 16)
    per_core = []
    for b in range(8):
        v = vbase.copy()
        v[:, COLS["cT"]:COLS["cT"] + 8] = _pcol(I["c"][b], 8)
        xTb = np.ascontiguousarray(np.concatenate([I["ctx"][b].T, I["x"][b].T], axis=1), np.float32)
        per_core.append({"xT": xTb, "vec": v})
    return shared, per_core


_NC_CACHE = {}


def kernel(**inputs):
    shared, per_core = prep_inputs(inputs)
    if "nc" not in _NC_CACHE:
        _NC_CACHE["nc"] = build_program()
    nc = _NC_CACHE["nc"]
    in_maps = [dict(shared, **pc) for pc in per_core]
    res = run_bass_kernel_spmd(nc, in_maps, core_ids=list(range(8)))
    out = np.stack([np.ascontiguousarray(r["outT"].T) for r in res.results], axis=0)
    return out.astype(np.float32)
```

```python
import contextlib
import numpy as np
import concourse.bass as bass
import concourse.mybir as mybir
from concourse.bass_utils import run_bass_kernel_spmd

F32 = mybir.dt.float32
BF16 = mybir.dt.bfloat16
AF = mybir.ActivationFunctionType
ALU = mybir.AluOpType
AX = mybir.AxisListType

NTOK = 2304
NCTX = 256
NLAT = 2048
TILES = [(0, 256), (256, 512), (768, 512), (1280, 512), (1792, 512)]
ALPHA = 4.0 ** 0.25
CDEC = float(np.exp(-0.5))
SCALE = 96.0 ** -0.5
NCH = 18

ENGS = ("pe", "act", "dve", "pool", "sp")
NDMASEM = 8


class Buf:
    __slots__ = ("name", "w", "r")

    def __init__(self, name=""):
        self.name = name
        self.w = None
        self.r = []


class T:
    __slots__ = ("ap", "b")

    def __init__(self, ap, name=""):
        self.ap = ap
        self.b = Buf(name)


def _b(x):
    return x.b if isinstance(x, T) else x


class Sched:
    def __init__(self, nc):
        self.nc = nc
        self.streams = {e: [] for e in ENGS}
        self.cnt = {e: 0 for e in ENGS}
        self.waited = {}
        self.sem = {}
        self.dma_n = {"sp": 0, "pool": 0, "act": 0}
        self.pending_nosig = {e: False for e in ENGS}
        self.ninst = 0

    def _semkeys(self):
        keys = list(ENGS)
        for q in ("sp", "pool", "act"):
            for i in range(NDMASEM):
                keys.append(("dma", q, i))
        return keys

    def _need(self, eng, tok, waits):
        if tok is None:
            return
        key, val = tok
        if self.waited.get((eng, key), 0) >= val:
            return
        if key == eng and eng == "pe":
            return
        self.waited[(eng, key)] = val
        waits[key] = max(waits.get(key, 0), val)

    def _deps(self, eng, reads, writes):
        waits = {}
        for b in reads:
            self._need(eng, _b(b).w, waits)
        for b in writes:
            b = _b(b)
            self._need(eng, b.w, waits)
            for t in b.r:
                self._need(eng, t, waits)
        return waits

    def _commit(self, tok, reads, writes):
        for b in reads:
            b = _b(b)
            b.r.append(tok)
            if len(b.r) > 48:
                d = {}
                for k, v in b.r:
                    d[k] = max(d.get(k, 0), v)
                b.r = list(d.items())
        for b in writes:
            b = _b(b)
            b.w = tok
            b.r = []

    def op(self, eng, fn, reads=(), writes=(), sig=True):
        waits = self._deps(eng, reads, writes)
        if sig:
            self.cnt[eng] += 1
            self.pending_nosig[eng] = False
        else:
            self.pending_nosig[eng] = True
        tok = (eng, self.cnt[eng] if sig else self.cnt[eng] + 1)
        self._commit(tok, reads, writes)
        self.streams[eng].append((waits, fn, eng if sig else None, 1))
        self.ninst += 1

    def dma(self, q, out, in_, reads=(), writes=()):
        n = self.dma_n[q]
        self.dma_n[q] += 1
        slot = n % NDMASEM
        key = ("dma", q, slot)
        val = 16 * (n // NDMASEM + 1)
        waits = self._deps(q, reads, writes)
        if val > 16:
            self._need(q, (key, val - 16), waits)
        tok = (key, val)
        self._commit(tok, reads, writes)
        self.streams[q].append((waits, E("dma_start", out=out, in_=in_), key, 16))
        self.ninst += 1
        return tok

    def all_tokens(self):
        toks = [(e, self.cnt[e]) for e in ENGS if self.cnt[e] > 0]
        for q, n in self.dma_n.items():
            for slot in range(min(n, NDMASEM)):
                last = ((n - 1 - slot) // NDMASEM) * NDMASEM + slot
                toks.append((("dma", q, slot), 16 * (last // NDMASEM + 1)))
        return toks

    def barrier(self):
        for e in ENGS:
            assert not self.pending_nosig[e], e
        toks = self.all_tokens()
        for e in ENGS:
            waits = {}
            for t in toks:
                self._need(e, t, waits)
            if waits:
                self.streams[e].append((waits, None, None, 0))

    def run(self):
        nc = self.nc
        self.barrier()
        with contextlib.ExitStack() as st:
            for k in self._semkeys():
                nm = k if isinstance(k, str) else f"d_{k[1]}_{k[2]}"
                self.sem[k] = st.enter_context(nc.semaphore("s_" + nm))
            block = st.enter_context(nc.Block())
            sem = self.sem

            def mk(ename):
                stream = self.streams[ename]

                def body(e):
                    for waits, fn, sigkey, inc in stream:
                        for k, v in waits.items():
                            e.wait_ge(sem[k], v)
                        if fn is not None:
                            ins = fn(e)
                            if sigkey is not None:
                                ins.then_inc(sem[sigkey], inc)
                return body

            block.tensor(mk("pe"))
            block.scalar(mk("act"))
            block.vector(mk("dve"))
            block.gpsimd(mk("pool"))
            block.sync(mk("sp"))


ARENA_F32 = 47104


class KB:
    def __init__(self, nc, arena, banks, dbg):
        self.nc = nc
        self.S = Sched(nc)
        self.arena = arena
        self.banks = banks
        self.off = 0
        self.dbg = dbg
        self.dram = {}
        self.eng_rr = 0

    def alloc(self, n, dt=F32, name=""):
        words = (n + 1) // 2 if dt == BF16 else n
        words = (words + 7) // 8 * 8
        assert self.off + words <= ARENA_F32, (name, self.off, words)
        ap = self.arena[:, self.off:self.off + words]
        self.off += words
        if dt == BF16:
            ap = ap.bitcast(BF16)[:, 0:n]
        else:
            ap = ap[:, 0:n]
        return T(ap, name)

    def mark(self):
        return self.off

    def reset(self, mark):
        self.S.barrier()
        self.off = mark

    def pbank(self, i):
        return T(self.banks[i][:, :], f"bank{i}")

    def phalf(self, i):
        return T(self.banks[i // 2][:, (i % 2) * 256:(i % 2 + 1) * 256], f"half{i}")

    def dram_t(self, name, shape, dt):
        kind = "ExternalOutput" if name in self.dbg else "Internal"
        t = self.nc.dram_tensor(name, list(shape), dt, kind=kind).ap()
        self.dram[name] = t
        return t

    def mm(self, out, out_ap, pairs, reads):
        n = len(pairs)
        for j, (l, r) in enumerate(pairs):
            self.S.op("pe", E("matmul", out_ap, lhsT=l, rhs=r, start=(j == 0), stop=(j == n - 1)),
                      reads=reads, writes=[out], sig=(j == n - 1))

    def mm1(self, out, out_ap, l, r, reads, start=True, stop=True, sig=True):
        self.S.op("pe", E("matmul", out_ap, lhsT=l, rhs=r, start=start, stop=stop), reads=reads, writes=[out], sig=sig)

    def op(self, eng, fn, reads, writes):
        self.S.op(eng, fn, reads=reads, writes=writes)

    def load(self, dst, dst_ap, src_ap, q="sp"):
        return self.S.dma(q, dst_ap, src_ap, writes=[dst])

    def store(self, dst_ap, src, src_ap, q="sp"):
        return self.S.dma(q, dst_ap, src_ap, reads=[src])


def E(name, *a, **kw):
    return lambda e: getattr(e, name)(*a, **kw)


def bc(ap, shape):
    return ap.to_broadcast(list(shape))


def _colmap():
    cols = {}
    off = 0

    def add(name, n):
        nonlocal off
        cols[name] = off
        off += n
    add("cT", 8); add("cctxT", 8)
    for L in range(2):
        p = f"l{L}_"
        add(p + "mod_b", 48)
        for nm in ("ln1_g", "ln1_b", "ln2_g", "ln2_b"):
            add(p + nm, 8)
    add("q_norm", 3); add("kv_norm", 2); add("mu", 15)
    add("w0", 8); add("a0", 8)
    for nm in ("k_k", "k_a", "r_k", "gn_w", "gn_b"):
        add(nm, 4)
    add("conv_w", 32)
    add("conv_b", 8)
    add("ga_b", 16); add("gx_b", 16); add("lam", 16)
    return cols, off


COLS, NCOL = _colmap()


def _pcol(v, n):
    return np.ascontiguousarray(np.asarray(v, np.float32).reshape(n, 128).T)


def build_program(dbg=(), stop_after=None, start_layer=0):
    nc = bass.Bass("TRN2", target_bir_lowering=False)
    IN = {}

    def inp(name, shape, dt=F32):
        IN[name] = nc.dram_tensor(name, list(shape), dt, kind="ExternalInput").ap()
        return IN[name]

    xT_in = inp("xT", [1024, NTOK])
    vec = inp("vec", [128, NCOL])
    cm = inp("cmats", [128, 9, 128])
    rope_in = inp("rope", [128, 2, NLAT])
    ind_in = inp("ind8", [128, 8, 8])
    msk_in = inp("rwmask", [128, 2, 1280])
    W = {}
    for L in range(2):
        p = f"l{L}_"
        W[p + "mod_w"] = inp(p + "mod_w", [1024, 6144])
        W[p + "w_out"] = inp(p + "w_out", [1024, 1024])
        W[p + "mlp_w1"] = inp(p + "mlp_w1", [1024, 4096])
        W[p + "mlp_w2"] = inp(p + "mlp_w2", [4096, 1024])
    W["w_in0"] = inp("w_in0", [1024, 2752])
    W["w_uq"] = inp("w_uq", [384, 768]); W["w_uq_rot"] = inp("w_uq_rot", [384, 768])
    W["w_uk"] = inp("w_uk", [256, 512]); W["w_uv"] = inp("w_uv", [256, 512])
    W["w2pad"] = inp("w2pad", [128, 2, 512]); W["a2pad"] = inp("a2pad", [128, 2, 512]); W["g2"] = inp("g2", [128, 512])
    W["w_in1"] = inp("w_in1", [1024, 2048])
    W["ga_w"] = inp("ga_w", [128, 2, 8, 128]); W["gx_w"] = inp("gx_w", [128, 2, 8, 128])
    outT = nc.dram_tensor("outT", [1024, NLAT], F32, kind="ExternalOutput").ap()

    with contextlib.ExitStack() as st:
        arena = st.enter_context(nc.sbuf_tensor("arena", [128, ARENA_F32], F32))
        banks = [st.enter_context(nc.psum_tensor(f"bank{i}", [128, 512], F32)) for i in range(8)]
        kb = KB(nc, arena, banks, set(dbg))
        S = kb.S
        xT = kb.dram_t("xTs", [1024, NTOK], F32)
        F0T = kb.dram_t("F0T", [22, 128, NTOK], F32)
        mixT = kb.dram_t("mixT", [1024, NTOK], BF16)
        RWU = kb.dram_t("RWU", [4, 2, NCH, 128, 7 * 128], F32)
        RWGL = kb.dram_t("RWGL", [4, 2, 128, NCH], F32)
        RWG = kb.dram_t("RWG", [4, 128, NTOK], F32)
        RWBON = kb.dram_t("RWBON", [4, 128, NTOK], F32)
        G1T = kb.dram_t("G1T", [16, 128, NTOK], F32)
        HID = None

        vecs = kb.alloc(NCOL, F32, "vecs")
        cmf = kb.alloc(9 * 128, F32, "cmf")
        cmb = kb.alloc(9 * 128, BF16, "cmb")
        modv = [kb.alloc(96, F32, f"modv{L}") for L in range(2)]
        mod1p = [kb.alloc(96, F32, f"mod1p{L}") for L in range(2)]
        kb.load(vecs, vecs.ap, vec[:, :])
        kb.load(cmf, cmf.ap, cm.rearrange("p a b -> p (a b)"))
        kb.load(cmb, cmb.ap, cm.rearrange("p a b -> p (a b)"), q="pool")
        PERSIST = kb.mark()

        def V(name, i=0, n=1):
            o = COLS[name] + i
            return vecs.ap[:, o:o + n]

        def CMF(i):
            return cmf.ap[:, i * 128:(i + 1) * 128]

        def CMB(i):
            return cmb.ap[:, i * 128:(i + 1) * 128]

        def MOD(L, j, k, which):
            o = (j * 8 + k) * 2 + which
            return modv[L].ap[:, o:o + 1]

        def MOD1P(L, j, k, which):
            o = (j * 8 + k) * 2 + which
            return mod1p[L].ap[:, o:o + 1]

        def xview(t):
            return t.rearrange("(k p) n -> p k n", p=128)


        def stage_mod(L):
            p = f"l{L}_"
            m0 = kb.mark()
            cs = kb.alloc(16, F32, "cs")
            mwb = [kb.alloc(8 * 512, F32, f"mw{i}") for i in range(2)]
            ps = kb.pbank(0)
            csv = cs.ap.rearrange("p (k w) -> p k w", w=2)
            kb.op("act", E("activation", out=csv[:, :, 0], in_=V("cT", 0, 8), func=AF.Silu), [vecs], [cs])
            kb.op("act", E("activation", out=csv[:, :, 1], in_=V("cctxT", 0, 8), func=AF.Silu), [vecs], [cs])
            mwv = W[p + "mod_w"].rearrange("(k p) n -> p k n", p=128)
            for piece in range(12):
                mw = mwb[piece % 2]
                mw3 = mw.ap.rearrange("p (k n) -> p k n", k=8)
                kb.load(mw, mw3, mwv[:, :, piece * 512:(piece + 1) * 512], q="sp" if piece % 2 == 0 else "pool")
                for j in range(4):
                    oc = piece * 4 + j
                    kb.mm(ps, ps.ap[:, oc * 2:oc * 2 + 2],
                          [(mw3[:, k, j * 128:(j + 1) * 128], csv[:, k, :]) for k in range(8)], [mw, cs])
            mv3 = modv[L].ap.rearrange("p (a w) -> p a w", w=2)
            kb.op("dve", E("tensor_tensor", out=mv3, in0=ps.ap[:, 0:96].rearrange("p (a w) -> p a w", w=2),
                                                    in1=bc(V(p + "mod_b", 0, 48).rearrange("p (a o) -> p a o", o=1), [128, 48, 2]), op=ALU.add),
                  [ps, vecs], [modv[L]])
            kb.op("dve", E("tensor_scalar", out=mod1p[L].ap, in0=modv[L].ap, scalar1=1.0, scalar2=None, op0=ALU.add),
                  [modv[L]], [mod1p[L]])
            kb.reset(m0)

        def modulate_tile(L, jsh, jsc, xt3, ht3, n, which, xT_T, hT_T):
            for k in range(8):
                if k % 2 == 0:
                    kb.op("act", E("activation", out=ht3[:, k, :n], in_=xt3[:, k, :n], func=AF.Identity,
                                                             scale=MOD1P(L, jsc, k, which), bias=MOD(L, jsh, k, which)),
                          [xT_T, modv[L], mod1p[L]], [hT_T])
                else:
                    kb.op("dve", E("tensor_scalar", out=ht3[:, k, :n], in0=xt3[:, k, :n], scalar1=MOD1P(L, jsc, k, which),
                                                                scalar2=MOD(L, jsh, k, which), op0=ALU.mult, op1=ALU.add),
                          [xT_T, modv[L], mod1p[L]], [hT_T])

        def load_wbf(dst, dst3, src, nk, ncols, cpp=None):
            sv = src.rearrange("(k p) n -> p k n", p=128)
            for k in range(nk):
                kb.load(dst, dst3[:, k, :], sv[:, k, :], q="pool")

        def stage_win(L, src_x, ncolsW, wname, chunks, dstT, gelu_chunks=()):
            m0 = kb.mark()
            wt = kb.alloc(8 * ncolsW, BF16, "w_in")
            w3 = wt.ap.rearrange("p (k n) -> p k n", k=8)
            load_wbf(wt, w3, W[wname], 8, ncolsW)
            xb = [kb.alloc(8 * 512, F32, f"xb{i}") for i in range(2)]
            hb = [kb.alloc(8 * 512, BF16, f"hb{i}") for i in range(2)]
            stg = [kb.alloc(512, F32, f"stg{i}") for i in range(4)]
            pb = [kb.pbank(i) for i in range(4)]
            xv = xview(src_x)
            it = 0
            for ti, (t0, n) in enumerate(TILES):
                which = 1 if ti == 0 else 0
                xt = xb[ti % 2]; ht = hb[ti % 2]
                xt3 = xt.ap.rearrange("p (k n) -> p k n", k=8); ht3 = ht.ap.rearrange("p (k n) -> p k n", k=8)
                kb.load(xt, xt3[:, :, :n], xv[:, :, t0:t0 + n])
                modulate_tile(L, 0, 1, xt3, ht3, n, which, xt, ht)
                for ci, (c0, M) in enumerate(chunks):
                    ps = pb[it % 4]; sg = stg[it % 4]
                    kb.mm(ps, ps.ap[:M, :n], [(w3[:, k, c0:c0 + M], ht3[:, k, :n]) for k in range(8)], [wt, ht])
                    if ci in gelu_chunks:
                        kb.op("act", E("activation", out=sg.ap[:M, :n], in_=ps.ap[:M, :n], func=AF.Gelu), [ps], [sg])
                    elif it % 2 == 0:
                        kb.op("act", E("copy", out=sg.ap[:M, :n], in_=ps.ap[:M, :n]), [ps], [sg])
                    else:
                        kb.op("dve", E("tensor_copy", out=sg.ap[:M, :n], in_=ps.ap[:M, :n]), [ps], [sg])
                    kb.store(dstT[ci, 0:M, t0:t0 + n], sg, sg.ap[:M, :n])
                    it += 1
            kb.reset(m0)

        def ln_tile(zt, zt3, n, gname, bname, dst3_dram, scr, pbs):
            ps_m, ps_q = pbs
            zb = scr["zb"]; zb3 = zb.ap.rearrange("p (k n) -> p k n", k=8)
            kb.op("act", E("activation", out=zb3[:, :, :n], in_=zt3[:, :, :n], func=AF.Square), [zt], [zb])
            kb.mm(ps_m, ps_m.ap[:, :n], [(CMF(1), zt3[:, k, :n]) for k in range(8)], [zt, cmf])
            kb.mm(ps_q, ps_q.ap[:, :n], [(CMB(1), zb3[:, k, :n]) for k in range(8)], [zb, cmb])
            mean = scr["mean"]; rstd = scr["rstd"]; tmp = scr["tmp"]
            kb.op("act", E("copy", out=mean.ap[:, :n], in_=ps_m.ap[:, :n]), [ps_m], [mean])
            kb.op("act", E("activation", out=tmp.ap[:, :n], in_=ps_m.ap[:, :n], func=AF.Square), [ps_m], [tmp])
            kb.op("dve", E("tensor_tensor", out=tmp.ap[:, :n], in0=ps_q.ap[:, :n], in1=tmp.ap[:, :n], op=ALU.subtract), [ps_q, tmp], [tmp])
            kb.op("dve", E("tensor_scalar", out=tmp.ap[:, :n], in0=tmp.ap[:, :n], scalar1=0.0, scalar2=None, op0=ALU.max), [tmp], [tmp])
            kb.op("act", E("activation", out=tmp.ap[:, :n], in_=tmp.ap[:, :n], func=AF.Sqrt, bias=1e-5), [tmp], [tmp])
            kb.op("dve", E("reciprocal", out=rstd.ap[:, :n], in_=tmp.ap[:, :n]), [tmp], [rstd])
            for k in range(8):
                e1, e2 = ("dve", "dve")
                kb.op(e1, E("tensor_tensor", out=zt3[:, k, :n], in0=zt3[:, k, :n], in1=mean.ap[:, :n], op=ALU.subtract), [zt, mean], [zt])
                kb.op(e2, E("tensor_tensor", out=zt3[:, k, :n], in0=zt3[:, k, :n], in1=rstd.ap[:, :n], op=ALU.mult), [zt, rstd], [zt])
                kb.op("act", E("activation", out=zt3[:, k, :n], in_=zt3[:, k, :n], func=AF.Identity, scale=V(gname, k), bias=V(bname, k)), [zt, vecs], [zt])
            kb.store(dst3_dram, zt, zt3[:, :, :n])

        def stage_wout(L, src_x, dst_x, tiles):
            p = f"l{L}_"
            m0 = kb.mark()
            wt = kb.alloc(8 * 1024, BF16, "w_out")
            w3 = wt.ap.rearrange("p (k n) -> p k n", k=8)
            load_wbf(wt, w3, W[p + "w_out"], 8, 1024)
            xb = [kb.alloc(8 * 512, F32, f"xb{i}") for i in range(2)]
            mb_ = [kb.alloc(8 * 512, BF16, f"mb{i}") for i in range(2)]
            zt_ = [kb.alloc(8 * 512, F32, f"z{i}") for i in range(2)]
            scr = {"zb": kb.alloc(8 * 512, BF16, "zb"), "mean": kb.alloc(512, F32, "mean"), "rstd": kb.alloc(512, F32, "rstd"), "tmp": kb.alloc(512, F32, "tmp")}
            pb = [kb.pbank(i) for i in range(4)]
            pst = (kb.pbank(4), kb.pbank(5))
            xv = xview(src_x); dv = xview(dst_x); mv = mixT.rearrange("(k p) n -> p k n", p=128)
            it = 0
            for (ti, t0, n, d0) in tiles:
                which = 1 if ti == 0 else 0
                xt = xb[ti % 2]; mt = mb_[ti % 2]; zt = zt_[ti % 2]
                xt3 = xt.ap.rearrange("p (k n) -> p k n", k=8); mt3 = mt.ap.rearrange("p (k n) -> p k n", k=8); zt3 = zt.ap.rearrange("p (k n) -> p k n", k=8)
                kb.load(xt, xt3[:, :, :n], xv[:, :, t0:t0 + n])
                kb.load(mt, mt3[:, :, :n], mv[:, :, t0:t0 + n], q="act")
                kb.op("act", E("activation", out=xt3[:, :, :n], in_=xt3[:, :, :n], func=AF.Identity, scale=ALPHA), [xt], [xt])
                for oc in range(8):
                    ps = pb[it % 4]; it += 1
                    kb.mm(ps, ps.ap[:, :n], [(w3[:, k, oc * 128:(oc + 1) * 128], mt3[:, k, :n]) for k in range(8)], [wt, mt])
                    kb.op("dve", E("scalar_tensor_tensor", out=zt3[:, oc, :n], in0=ps.ap[:, :n], scalar=MOD(L, 2, oc, which),
                                                                                in1=xt3[:, oc, :n], op0=ALU.mult, op1=ALU.add), [ps, xt, modv[L]], [zt])
                ln_tile(zt, zt3, n, p + "ln1_g", p + "ln1_b", dv[:, :, d0:d0 + n], scr, pst)
            kb.reset(m0)

        def stage_mlp(L, src_x, dst_x, tiles, TW=256):
            p = f"l{L}_"
            m0 = kb.mark()
            w1 = kb.alloc(8 * 4096, BF16, "w1"); w13 = w1.ap.rearrange("p (k n) -> p k n", k=8)
            w2 = kb.alloc(32 * 1024, BF16, "w2"); w23 = w2.ap.rearrange("p (k n) -> p k n", k=32)
            load_wbf(w1, w13, W[p + "mlp_w1"], 8, 4096)
            load_wbf(w2, w23, W[p + "mlp_w2"], 32, 1024)
            xts = [kb.alloc(8 * TW, F32, f"xt{i}") for i in range(2)]
            hts = [kb.alloc(8 * TW, BF16, f"ht{i}") for i in range(2)]
            hid = kb.alloc(32 * TW, BF16, "hid"); hid3 = hid.ap.rearrange("p (k n) -> p k n", k=32)
            rl = [kb.alloc(TW, F32, f"rl{i}") for i in range(3)]
            pb = [kb.pbank(i) for i in range(4)]
            pst = (kb.pbank(4), kb.pbank(5))
            xv = xview(src_x); dv = xview(dst_x)
            it = 0

            def prefetch(i):
                (ti, t0, n, d0) = tiles[i]
                xt = xts[i % 2]; ht = hts[i % 2]
                xt3 = xt.ap.rearrange("p (k n) -> p k n", k=8); ht3 = ht.ap.rearrange("p (k n) -> p k n", k=8)
                kb.load(xt, xt3[:, :, :n], xv[:, :, t0:t0 + n])
                return (xt, xt3, ht, ht3)

            def modul(i, bufs):
                (ti, t0, n, d0) = tiles[i]
                xt, xt3, ht, ht3 = bufs
                modulate_tile(L, 3, 4, xt3, ht3, n, 1 if ti == 0 else 0, xt, ht)
            cur = prefetch(0); modul(0, cur)
            for i, (ti, t0, n, d0) in enumerate(tiles):
                which = 1 if ti == 0 else 0
                xt, xt3, ht, ht3 = cur
                nxt = prefetch(i + 1) if i + 1 < len(tiles) else None
                for fc in range(32):
                    ps = pb[it % 4]; r = rl[it % 3]; it += 1
                    kb.mm(ps, ps.ap[:, :n], [(w13[:, k, fc * 128:(fc + 1) * 128], ht3[:, k, :n]) for k in range(8)], [w1, ht])
                    kb.op("act", E("activation", out=r.ap[:, :n], in_=ps.ap[:, :n], func=AF.Relu), [ps], [r])
                    kb.op("dve", E("tensor_tensor", out=hid3[:, fc, :n], in0=r.ap[:, :n], in1=r.ap[:, :n], op=ALU.mult), [r], [hid])
                if nxt is not None:
                    modul(i + 1, nxt)
                kb.op("act", E("activation", out=xt3[:, :, :n], in_=xt3[:, :, :n], func=AF.Identity, scale=ALPHA), [xt, ht], [xt])
                for oc in range(8):
                    ps = pb[it % 4]; it += 1
                    kb.mm(ps, ps.ap[:, :n], [(w23[:, fc, oc * 128:(oc + 1) * 128], hid3[:, fc, :n]) for fc in range(32)], [w2, hid])
                    kb.op("dve", E("scalar_tensor_tensor", out=xt3[:, oc, :n], in0=ps.ap[:, :n], scalar=MOD(L, 5, oc, which),
                                   in1=xt3[:, oc, :n], op0=ALU.mult, op1=ALU.add), [ps, xt, modv[L]], [xt])
                zbT = T(hid.ap[:, 0:8 * TW], "zb_alias"); zbT.b = hid.b
                scr = {"zb": zbT, "mean": rl[0], "rstd": rl[1], "tmp": rl[2]}
                ln_tile(xt, xt3, n, p + "ln2_g", p + "ln2_b", dv[:, :, d0:d0 + n], scr, pst)
                cur = nxt
            kb.reset(m0)

        def stage_mla():
            m0 = kb.mark()
            qT = kb.alloc(8 * NTOK, BF16, "qT"); qT3 = qT.ap.rearrange("p (h n) -> p h n", h=8)
            kT = kb.alloc(8 * NTOK, BF16, "kT"); kT3 = kT.ap.rearrange("p (h n) -> p h n", h=8)
            Va = kb.alloc(18 * 8 * 65, BF16, "Vaug"); Va4 = Va.ap.rearrange("p (c h e) -> p c h e", c=18, h=8)
            negM = kb.alloc(8, F32, "negM")
            qmx = kb.alloc(1, F32, "qmx"); kmx = kb.alloc(1, F32, "kmx")
            m1 = kb.mark()
            wq = kb.alloc(3 * 768, BF16, "wq"); wq3 = wq.ap.rearrange("p (k n) -> p k n", k=3)
            wr = kb.alloc(3 * 768, BF16, "wr"); wr3 = wr.ap.rearrange("p (k n) -> p k n", k=3)
            wk = kb.alloc(2 * 512, BF16, "wk"); wk3 = wk.ap.rearrange("p (k n) -> p k n", k=2)
            wv = kb.alloc(2 * 512, BF16, "wv"); wv3 = wv.ap.rearrange("p (k n) -> p k n", k=2)
            load_wbf(wq, wq3, W["w_uq"], 3, 768); load_wbf(wr, wr3, W["w_uq_rot"], 3, 768)
            load_wbf(wk, wk3, W["w_uk"], 2, 512); load_wbf(wv, wv3, W["w_uv"], 2, 512)
            rope = kb.alloc(2 * NLAT, F32, "rope"); rope3 = rope.ap.rearrange("p (a n) -> p a n", a=2)
            kb.load(rope, rope.ap, rope_in.rearrange("p a n -> p (a n)"))
            ind = kb.alloc(64, BF16, "ind"); ind3 = ind.ap.rearrange("p (a b) -> p a b", a=8)
            kb.load(ind, ind.ap, ind_in.rearrange("p a b -> p (a b)"), q="pool")
            kb.op("pool", E("memset", Va4[:, :, :, 64:65], 1.0), [], [Va])
            kb.op("pool", E("memset", qmx.ap[0:8, :], 0.0), [], [qmx])
            kb.op("pool", E("memset", kmx.ap[0:8, :], 0.0), [], [kmx])
            fq = kb.alloc(3 * 512, F32, "fq"); fq3 = fq.ap.rearrange("p (k n) -> p k n", k=3)
            fkv = kb.alloc(2 * 512, F32, "fkv"); fkv3 = fkv.ap.rearrange("p (k n) -> p k n", k=2)
            krp = kb.alloc(512, F32, "krp"); krr = kb.alloc(512, F32, "krr")
            sq = kb.alloc(3 * 512, F32, "sq"); sq3 = sq.ap.rearrange("p (k n) -> p k n", k=3)
            sd = kb.alloc(512, F32, "sd"); rs = kb.alloc(512, F32, "rs")
            qn = kb.alloc(3 * 512, BF16, "qn"); qn3 = qn.ap.rearrange("p (k n) -> p k n", k=3)
            ckv = kb.alloc(2 * 512, BF16, "ckv"); ckv3 = ckv.ap.rearrange("p (k n) -> p k n", k=2)
            t1 = [kb.alloc(512, F32, f"t1_{i}") for i in range(2)]
            t2 = [kb.alloc(512, F32, f"t2_{i}") for i in range(2)]
            krf = kb.alloc(512, F32, "krf")
            sqq = kb.alloc(8 * 512, BF16, "sqq"); sqq3 = sqq.ap.rearrange("p (h n) -> p h n", h=8)
            nmx = kb.alloc(1, F32, "nmx")
            pA = [kb.pbank(i) for i in range(3)]
            pR = [kb.pbank(3), kb.pbank(4)]
            pK = [kb.pbank(5), kb.pbank(6)]
            pN = kb.pbank(7)
            ia = 0; ir = 0; ik = 0
            for ti, (t0, n) in enumerate(TILES):
                lat = ti > 0
                kb.load(fq, fq3[:, :, :n], F0T[0:3, :, t0:t0 + n].rearrange("k p n -> p k n"))
                kb.load(fkv, fkv3[:, :, :n], F0T[3:5, :, t0:t0 + n].rearrange("k p n -> p k n"))
                kb.load(krp, krp.ap[64:96, :n], F0T[5, 64:96, t0:t0 + n])
                if lat:
                    kb.load(krr, krr.ap[64:96, :n], F0T[6, 64:96, t0:t0 + n])
                kb.op("act", E("activation", out=sq3[:, :, :n], in_=fq3[:, :, :n], func=AF.Square), [fq], [sq])
                kb.mm(pN, pN.ap[:, :n], [(CMF(2), sq3[:, k, :n]) for k in range(3)], [sq, cmf])
                kb.op("act", E("activation", out=sd.ap[:, :n], in_=pN.ap[:, :n], func=AF.Sqrt, bias=1e-6), [pN], [sd])
                kb.op("dve", E("reciprocal", out=rs.ap[:, :n], in_=sd.ap[:, :n]), [sd], [rs])
                for k in range(3):
                    kb.op("dve", E("scalar_tensor_tensor", out=qn3[:, k, :n], in0=fq3[:, k, :n], scalar=V("q_norm", k), in1=rs.ap[:, :n],
                                                                      op0=ALU.mult, op1=ALU.mult), [fq, rs, vecs], [qn])
                if "DBGT" in kb.dbg and ti == 0:
                    dt_ = kb.dram_t("DBGT", [4, 128, 3 * 512], F32)
                    kb.store(dt_[0], sq, sq.ap); kb.store(dt_[1, :, 0:512], sd, sd.ap); kb.store(dt_[2, :, 0:512], rs, rs.ap)
                    kb.store(dt_[3], fq, fq.ap)
                    dt2 = kb.dram_t("DBGT2", [128, 3 * 512], BF16)
                    kb.store(dt2[:, :], qn, qn.ap)
                kb.op("act", E("activation", out=sq3[:, 0:2, :n], in_=fkv3[:, :, :n], func=AF.Square), [fkv, qn], [sq])
                kb.mm(pN, pN.ap[:, :n], [(CMF(3), sq3[:, k, :n]) for k in range(2)], [sq, cmf])
                kb.op("act", E("activation", out=sd.ap[:, :n], in_=pN.ap[:, :n], func=AF.Sqrt, bias=1e-6), [pN], [sd])
                kb.op("dve", E("reciprocal", out=rs.ap[:, :n], in_=sd.ap[:, :n]), [sd], [rs])
                for k in range(2):
                    kb.op("dve", E("scalar_tensor_tensor", out=ckv3[:, k, :n], in0=fkv3[:, k, :n], scalar=V("kv_norm", k), in1=rs.ap[:, :n],
                                                                      op0=ALU.mult, op1=ALU.mult), [fkv, rs, vecs], [ckv])
                if lat:
                    c0 = t0 - NCTX
                    kb.op("dve", E("tensor_tensor", out=krf.ap[64:96, :n], in0=krp.ap[64:96, :n], in1=rope3[64:96, 0, c0:c0 + n], op=ALU.mult), [krp, rope], [krf])
                    kb.op("pool", E("tensor_tensor", out=krr.ap[64:96, :n], in0=krr.ap[64:96, :n], in1=rope3[64:96, 1, c0:c0 + n], op=ALU.mult), [krr, rope], [krr])
                    kb.op("pool", E("tensor_tensor", out=krf.ap[64:96, :n], in0=krf.ap[64:96, :n], in1=krr.ap[64:96, :n], op=ALU.add), [krf, krr], [krf])
                    ksrc = krf
                else:
                    ksrc = krp
                kb.op("act", E("copy", out=kT3[64:96, :, t0:t0 + n], in_=bc(ksrc.ap[64:96, :n].rearrange("p (o n) -> p o n", o=1), [32, 8, n])), [ksrc], [kT])
                for h in range(8):
                    pq = pA[ia % 3]; ia += 1
                    kb.mm(pq, pq.ap[0:96, :n], [(wq3[:, k, h * 96:(h + 1) * 96], qn3[:, k, :n]) for k in range(3)], [wq, qn])
                    kb.op("act", E("copy", out=qT3[0:64, h, t0:t0 + n], in_=pq.ap[0:64, :n]), [pq], [qT])
                    if lat:
                        pr = pR[ir % 2]; a1 = t1[ir % 2]; a2 = t2[ir % 2]; ir += 1
                        kb.mm(pr, pr.ap[0:96, :n], [(wr3[:, k, h * 96:(h + 1) * 96], qn3[:, k, :n]) for k in range(3)], [wr, qn])
                        kb.op("dve", E("tensor_tensor", out=a1.ap[64:96, :n], in0=pr.ap[64:96, :n], in1=rope3[64:96, 1, c0:c0 + n], op=ALU.mult), [pr, rope], [a1])
                        kb.op("dve", E("tensor_tensor", out=a2.ap[64:96, :n], in0=pq.ap[64:96, :n], in1=rope3[64:96, 0, c0:c0 + n], op=ALU.mult), [pq, rope], [a2])
                        kb.op("pool", E("tensor_tensor", out=qT3[64:96, h, t0:t0 + n], in0=a1.ap[64:96, :n], in1=a2.ap[64:96, :n], op=ALU.add), [a1, a2], [qT])
                    else:
                        kb.op("act", E("copy", out=qT3[64:96, h, t0:t0 + n], in_=pq.ap[64:96, :n]), [pq], [qT])
                    pk = pK[ik % 2]; ik += 1
                    kb.mm(pk, pk.ap[0:64, :n], [(wk3[:, k, h * 64:(h + 1) * 64], ckv3[:, k, :n]) for k in range(2)], [wk, ckv])
                    kb.op("dve", E("tensor_copy", out=kT3[0:64, h, t0:t0 + n], in_=pk.ap[0:64, :n]), [pk], [kT])
                for j in range(n // 128):
                    pk = pK[ik % 2]; ik += 1
                    kc = (t0 + j * 128) // 128
                    kb.mm(pk, pk.ap[:, :], [(ckv3[:, k, j * 128:(j + 1) * 128], wv3[:, k, :]) for k in range(2)], [wv, ckv])
                    kb.op("act", E("copy", out=Va4[:, kc, :, 0:64], in_=pk.ap.rearrange("p (h e) -> p h e", h=8)), [pk], [Va])
                for (src3, src, mx) in ((qT3, qT, qmx), (kT3, kT, kmx)):
                    kb.op("act", E("activation", out=sqq3[0:96, :, :n], in_=src3[0:96, :, t0:t0 + n], func=AF.Square), [src], [sqq])
                    pk = pK[ik % 2]; ik += 1
                    kb.mm(pk, pk.ap[0:8, :n], [(ind3[0:96, h, :], sqq3[0:96, h, :n]) for h in range(8)], [ind, sqq])
                    kb.op("dve", E("tensor_reduce", out=nmx.ap[0:8, :], in_=pk.ap[0:8, :n], axis=AX.X, op=ALU.max), [pk], [nmx])
                    kb.op("dve", E("tensor_tensor", out=mx.ap[0:8, :], in0=mx.ap[0:8, :], in1=nmx.ap[0:8, :], op=ALU.max), [mx, nmx], [mx])
            dg = kb.alloc(8, F32, "dg")
            kb.op("dve", E("tensor_tensor", out=nmx.ap[0:8, :], in0=qmx.ap[0:8, :], in1=kmx.ap[0:8, :], op=ALU.mult), [qmx, kmx], [nmx])
            kb.op("act", E("activation", out=nmx.ap[0:8, :], in_=nmx.ap[0:8, :], func=AF.Sqrt), [nmx], [nmx])
            kb.op("dve", E("tensor_scalar", out=dg.ap[0:8, :], in0=CMF(0)[0:8, 0:8], scalar1=nmx.ap[0:8, 0:1], scalar2=-1.03 * SCALE, op0=ALU.mult, op1=ALU.mult),
                  [nmx, cmf], [dg])
            pk = pK[ik % 2]; ik += 1
            kb.mm(pk, pk.ap[:, 0:8], [(CMF(7)[0:8, :], dg.ap[0:8, :])], [cmf, dg])
            kb.op("dve", E("tensor_copy", out=negM.ap, in_=pk.ap[:, 0:8]), [pk], [negM])
            if "DBGQK" in kb.dbg:
                dq = kb.dram_t("DBGQK", [2, 128, 8 * NTOK], BF16)
                kb.store(dq[0], qT, qT.ap); kb.store(dq[1], kT, kT.ap)
                dv_ = kb.dram_t("DBGV", [128, 18 * 8 * 65], BF16)
                kb.store(dv_[:, :], Va, Va.ap)
                dn = kb.dram_t("DBGNM", [128, 8], F32)
                kb.store(dn[:, :], negM, negM.ap)
            kb.reset(m1)
            PT = [kb.alloc(512, BF16, f"PT{i}") for i in range(4)]
            osb = [kb.alloc(512, F32, f"osb{i}") for i in range(2)]
            rec = [kb.alloc(512, F32, f"rec{i}") for i in range(2)]
            att = [kb.alloc(512, BF16, f"att{i}") for i in range(2)]
            pS = [kb.pbank(i) for i in range(4)]
            pO = [kb.pbank(4), kb.pbank(5)]
            pB = [kb.pbank(6), kb.pbank(7)]
            isc = 0; io = 0
            for h in range(8):
                for ti, (t0, n) in enumerate(TILES):
                    nk = 2 if ti == 0 else 18
                    po = pO[io % 2]; ob = osb[io % 2]; rc = rec[io % 2]; ao = att[io % 2]; pb_ = pB[io % 2]; io += 1
                    slots = {}
                    for i in range(nk + 2):
                        if i < nk:
                            ps = pS[isc % 4]; pt = PT[isc % 4]; isc += 1
                            slots[i] = (ps, pt)
                            kb.mm1(ps, ps.ap[:, :n], kT3[0:96, h, i * 128:(i + 1) * 128], qT3[0:96, h, t0:t0 + n], [kT, qT])
                            kb.op("act", E("activation", out=pt.ap[:, :n], in_=ps.ap[:, :n], func=AF.Exp, scale=SCALE, bias=negM.ap[:, h:h + 1]),
                                  [ps, negM], [pt])
                        if i >= 2:
                            j = i - 2
                            ps, pt = slots.pop(j)
                            kb.mm1(po, po.ap[0:65, :n], Va4[:, j, h, :], pt.ap[:, :n], [Va, pt], start=(j == 0), stop=(j == nk - 1), sig=(j == nk - 1))
                    kb.op("dve", E("tensor_copy", out=ob.ap[0:65, :n], in_=po.ap[0:65, :n]), [po], [ob])
                    kb.mm(pb_, pb_.ap[0:64, :n], [(CMF(6)[0:65, 0:64], ob.ap[0:65, :n])], [cmf, ob])
                    kb.op("dve", E("reciprocal", out=rc.ap[0:64, :n], in_=pb_.ap[0:64, :n]), [pb_], [rc])
                    kb.op("pool", E("tensor_tensor", out=ao.ap[0:64, :n], in0=ob.ap[0:64, :n], in1=rc.ap[0:64, :n], op=ALU.mult), [ob, rc], [ao])
                    kb.store(mixT[h * 64:(h + 1) * 64, t0:t0 + n], ao, ao.ap[0:64, :n])
            kb.reset(m0)


        def stage_lru():
            m0 = kb.mark()
            gaw = kb.alloc(2048, BF16, "gaw"); gaw4 = gaw.ap.rearrange("p (d k e) -> p d k e", d=2, k=8)
            gxw = kb.alloc(2048, BF16, "gxw"); gxw4 = gxw.ap.rearrange("p (d k e) -> p d k e", d=2, k=8)
            kb.load(gaw, gaw.ap, W["ga_w"].rearrange("p d k e -> p (d k e)"), q="pool")
            kb.load(gxw, gxw.ap, W["gx_w"].rearrange("p d k e -> p (d k e)"), q="pool")
            cl = kb.alloc(16, F32, "cl")
            kb.op("act", E("activation", out=cl.ap, in_=V("lam", 0, 16), func=AF.Exp, scale=-1.0), [vecs], [cl])
            kb.op("act", E("activation", out=cl.ap, in_=cl.ap, func=AF.Ln, bias=1.0), [cl], [cl])
            kb.op("dve", E("tensor_scalar", out=cl.ap, in0=cl.ap, scalar1=-8.0, scalar2=None, op0=ALU.mult), [cl], [cl])
            XW = 2310
            xh = kb.alloc(XW, F32, "xh")
            kb.op("pool", E("memset", xh.ap, 0.0), [], [xh])
            xc = kb.alloc(NTOK, F32, "xc"); xcb = kb.alloc(NTOK, BF16, "xcb")
            rr_ = [kb.alloc(NTOK, F32, f"rr{i}") for i in range(2)]; ig_ = [kb.alloc(NTOK, F32, f"ig{i}") for i in range(2)]
            aa_ = [kb.alloc(NTOK, F32, "aa0")] * 2; a2_ = [kb.alloc(NTOK, F32, "a20")] * 2
            uu_ = [kb.alloc(NTOK, F32, "uu0")] * 2
            xh2 = kb.alloc(XW, F32, "xh2")
            kb.op("pool", E("memset", xh2.ap, 0.0), [], [xh2])
            xhs = [xh, xh2]
            dgs = [kb.alloc(512, F32, f"dg{i}") for i in range(2)]
            xc2 = kb.alloc(NTOK, F32, "xc2"); xcb2 = kb.alloc(NTOK, BF16, "xcb2")
            xcs = [xc, xc2]; xcbs = [xcb, xcb2]
            hd = [kb.alloc(NTOK, F32, f"hd{d}") for d in range(2)]
            gg = kb.alloc(NLAT, F32, "gg"); yy = kb.alloc(NLAT, BF16, "yy")
            pb = [kb.pbank(i) for i in range(4)]
            ip = 0
            segs = [(0, 0, 256), (256, 259, 2048)]
            for c in range(8):
                xh = xhs[c % 2]; xc = xcs[c % 2]; xcb = xcbs[c % 2]
                kb.load(xh, xh.ap[:, 2:258], G1T[8 + c, :, 0:256])
                kb.load(xh, xh.ap[:, 261:2309], G1T[8 + c, :, 256:NTOK])
                kb.load(gg, gg.ap, G1T[c, :, 256:NTOK])
                dg = dgs[c % 2]; dg3 = dg.ap.rearrange("p (j m) -> p j m", j=4)
                for j in range(4):
                    kb.op("dve", E("tensor_scalar", out=dg3[:, j, :], in0=CMF(0), scalar1=V("conv_w", j * 8 + c), scalar2=None, op0=ALU.mult), [cmf, vecs], [dg])
                for (t0, n) in TILES:
                    b0 = 0 if t0 == 0 else 259 + (t0 - 256)
                    ps = pb[ip % 4]; ip += 1
                    kb.mm(ps, ps.ap[:, :n], [(dg3[:, j, :], xh.ap[:, b0 + j:b0 + j + n]) for j in range(4)], [dg, xh])
                    kb.op("act", E("activation", out=xc.ap[:, t0:t0 + n], in_=ps.ap[:, :n], func=AF.Identity, bias=V("conv_b", c)), [ps, vecs], [xc])
                kb.op("act", E("copy", out=xcb.ap, in_=xc.ap), [xc], [xcb])
                if "DBGL1" in kb.dbg:
                    kb.store(kb.dram["DBGL1"][0, c], xc, xc.ap)
                for d in range(2):
                    rr = rr_[d]; ig = ig_[d]; aa = aa_[d]; a2 = a2_[d]; uu = uu_[d]
                    for (t0, n) in TILES:
                        ps = pb[ip % 4]; ip += 1
                        kb.mm(ps, ps.ap[:, :n], [(gaw4[:, d, c, :], xcb.ap[:, t0:t0 + n])], [gaw, xcb])
                        kb.op("act", E("activation", out=rr.ap[:, t0:t0 + n], in_=ps.ap[:, :n], func=AF.Sigmoid, bias=V("ga_b", d * 8 + c)), [ps, vecs], [rr])
                        ps = pb[ip % 4]; ip += 1
                        kb.mm(ps, ps.ap[:, :n], [(gxw4[:, d, c, :], xcb.ap[:, t0:t0 + n])], [gxw, xcb])
                        kb.op("act", E("activation", out=ig.ap[:, t0:t0 + n], in_=ps.ap[:, :n], func=AF.Sigmoid, bias=V("gx_b", d * 8 + c)), [ps, vecs], [ig])
                    kb.op("act", E("activation", out=aa.ap, in_=rr.ap, func=AF.Exp, scale=cl.ap[:, d * 8 + c:d * 8 + c + 1]), [rr, cl], [aa])
                    kb.op("act", E("activation", out=a2.ap, in_=aa.ap, func=AF.Square), [aa], [a2])
                    kb.op("act", E("activation", out=a2.ap, in_=a2.ap, func=AF.Sqrt, scale=-1.0, bias=1.0), [a2], [a2])
                    kb.op("dve", E("tensor_tensor", out=uu.ap, in0=ig.ap, in1=xc.ap, op=ALU.mult), [ig, xc], [uu])
                    kb.op("dve", E("tensor_tensor", out=uu.ap, in0=uu.ap, in1=a2.ap, op=ALU.mult), [uu, a2], [uu])
                    h_ = hd[d]
                    if d == 0:
                        kb.op("dve", E("tensor_tensor_scan", out=h_.ap, data0=aa.ap, data1=uu.ap, initial=0.0, op0=ALU.mult, op1=ALU.add), [aa, uu], [h_])
                    else:
                        kb.op("dve", E("tensor_tensor_scan", out=h_.ap[:, 0:256][:, ::-1], data0=aa.ap[:, 0:256][:, ::-1], data1=uu.ap[:, 0:256][:, ::-1],
                                       initial=0.0, op0=ALU.mult, op1=ALU.add), [aa, uu], [h_])
                        kb.op("dve", E("tensor_tensor_scan", out=h_.ap[:, 256:NTOK][:, ::-1], data0=aa.ap[:, 256:NTOK][:, ::-1], data1=uu.ap[:, 256:NTOK][:, ::-1],
                                       initial=h_.ap[:, 0:1], op0=ALU.mult, op1=ALU.add), [aa, uu, h_], [h_])
                    if "DBGL1" in kb.dbg:
                        kb.store(kb.dram["DBGL1"][1 + d, c], h_, h_.ap)
                kb.op("dve", E("tensor_tensor", out=hd[0].ap[:, 256:NTOK], in0=hd[0].ap[:, 256:NTOK], in1=hd[1].ap[:, 256:NTOK], op=ALU.add), [hd[0], hd[1]], [hd[0]])
                kb.op("dve", E("tensor_tensor", out=yy.ap, in0=hd[0].ap[:, 256:NTOK], in1=gg.ap, op=ALU.mult), [hd[0], gg], [yy])
                kb.store(mixT[c * 128:(c + 1) * 128, 256:NTOK], yy, yy.ap)
            kb.reset(m0)


        def stage_rwkv():
            m0 = kb.mark()
            W_ = 2312
            ORDER = [list(range(NCH)), [1, 0] + list(range(NCH - 1, 1, -1))]
            import os
            SKIP_RWA = bool(os.environ.get("ONLY_RWB"))
            X = [kb.alloc(W_, F32, f"X{i}") for i in range(5)]
            tw = kb.alloc(NTOK, F32, "tw"); alr = kb.alloc(NTOK, F32, "alr"); sg = kb.alloc(NTOK, F32, "sg")
            rS = kb.alloc(NTOK, F32, "rS"); kS = kb.alloc(NTOK, F32, "kS"); vS = kb.alloc(NTOK, F32, "vS")
            kk = kb.alloc(NTOK, F32, "kk"); pbn = kb.alloc(NTOK, F32, "pbn")
            msk = kb.alloc(NTOK + 1, F32, "msk")
            ob = [kb.alloc(NTOK, F32, f"ob{i}") for i in range(3)]
            stg = [kb.alloc(512, F32, f"stg{i}") for i in range(2)]
            mud = kb.alloc(30, F32, "mud"); oka = kb.alloc(4, F32, "oka"); glt = kb.alloc(NCH, F32, "glt")
            w2p = kb.alloc(1024, F32, "w2p"); a2p = kb.alloc(1024, F32, "a2p"); g2t = kb.alloc(512, F32, "g2t")
            w2p3 = w2p.ap.rearrange("p (d n) -> p d n", d=2); a2p3 = a2p.ap.rearrange("p (d n) -> p d n", d=2)
            kb.load(w2p, w2p.ap, W["w2pad"].rearrange("p d n -> p (d n)"))
            kb.load(a2p, a2p.ap, W["a2pad"].rearrange("p d n -> p (d n)"))
            kb.load(g2t, g2t.ap, W["g2"][:, :])
            pb = [kb.pbank(i) for i in range(4)]
            ipb = [0]

            def nps():
                p_ = pb[ipb[0] % 4]; ipb[0] += 1
                return p_
            kb.op("dve", E("tensor_scalar", out=mud.ap[:, 0:15], in0=V("mu", 0, 15), scalar1=-1.0, scalar2=1.0, op0=ALU.mult, op1=ALU.add), [vecs], [mud])
            kb.op("dve", E("tensor_scalar", out=mud.ap[:, 15:30], in0=V("mu", 0, 15), scalar1=0.5, scalar2=None, op0=ALU.mult), [vecs, mud], [mud])
            kb.op("dve", E("tensor_scalar", out=oka.ap, in0=V("k_a", 0, 4), scalar1=-1.0, scalar2=1.0, op0=ALU.mult, op1=ALU.add), [vecs], [oka])
            kb.op("pool", E("memset", msk.ap, 1.0), [], [msk])
            kb.op("pool", E("memset", msk.ap[:, 0:NTOK + 1:128], 0.0), [msk], [msk])
            kb.op("pool", E("memset", X[1].ap, 0.0), [], [X[1]])
            maskf = msk.ap[:, 0:NTOK]; maskr = msk.ap[:, 1:NTOK + 1]

            def shift(j, dst, func=None):
                fc, fh, ss, uu = X[0], X[1], X[2], X[3]
                kb.load(fc, fc.ap[:, 0:NTOK], F0T[7 + j, :, :])
                kb.load(fh, fh.ap[:, 1:257], F0T[7 + j, :, 0:256])
                kb.load(fh, fh.ap[:, 258:2306], F0T[7 + j, :, 256:NTOK])
                kb.op("dve", E("tensor_tensor", out=ss.ap[:, 0:256], in0=fh.ap[:, 0:256], in1=fh.ap[:, 2:258], op=ALU.add), [fh], [ss])
                kb.op("dve", E("tensor_tensor", out=ss.ap[:, 256:NTOK], in0=fh.ap[:, 257:2305], in1=fh.ap[:, 259:2307], op=ALU.add), [fh, ss], [ss])
                kb.op("act", E("activation", out=uu.ap[:, 0:NTOK], in_=fc.ap[:, 0:NTOK], func=AF.Identity, scale=mud.ap[:, j:j + 1]), [fc, mud], [uu])
                kb.op("dve", E("scalar_tensor_tensor", out=dst.ap[:, 0:NTOK], in0=ss.ap[:, 0:NTOK], scalar=mud.ap[:, 15 + j:16 + j], in1=uu.ap[:, 0:NTOK],
                               op0=ALU.mult, op1=ALU.add), [ss, uu, mud], [dst])
                if func is not None:
                    kb.op("act", E("activation", out=dst.ap[:, 0:NTOK], in_=dst.ap[:, 0:NTOK], func=func), [dst], [dst])
            if not SKIP_RWA:
                shift(12, tw, AF.Tanh); shift(13, alr); shift(14, sg, AF.Sigmoid)
            RWU7 = [[RWU[hp, d].rearrange("c p (i t) -> p c i t", i=7) for d in range(2)] for hp in range(4)]
            for hp in range(0 if SKIP_RWA else 4):
                shift(hp, rS); shift(4 + hp, kS); shift(8 + hp, vS)
                for d in range(2):
                    kb.store(RWU7[hp][d][:, :, 6, :], vS, vS.ap.rearrange("p (c t) -> p c t", t=128))
                sq = X[4]
                kb.op("act", E("activation", out=kk.ap, in_=kS.ap, func=AF.Identity, scale=V("k_k", hp)), [kS, vecs], [kk])
                kb.op("act", E("activation", out=sq.ap[:, 0:NTOK], in_=kk.ap, func=AF.Square), [kk], [sq])
                for (t0, n) in TILES:
                    ps = nps()
                    kb.mm(ps, ps.ap[:, :n], [(CMF(4), sq.ap[:, t0:t0 + n])], [cmf, sq])
                    kb.op("act", E("activation", out=X[3].ap[:, t0:t0 + n], in_=ps.ap[:, :n], func=AF.Sqrt, bias=1e-12), [ps], [X[3]])
                kb.op("dve", E("reciprocal", out=X[3].ap[:, 0:NTOK], in_=X[3].ap[:, 0:NTOK]), [X[3]], [X[3]])
                kb.op("dve", E("tensor_tensor", out=kk.ap, in0=kk.ap, in1=X[3].ap[:, 0:NTOK], op=ALU.mult), [kk, X[3]], [kk])
                if "DBGRW" in kb.dbg:
                    kb.store(kb.dram["DBGRW"][0, hp], kk, kk.ap)
                iob = 0
                for d in range(2):
                    x1, x2, x3, x4, x5 = [X[i] for i in range(5)]
                    x1a = x1.ap[:, 0:NTOK]; x2a = x2.ap[:, 0:NTOK]; x3a = x3.ap[:, 0:NTOK]; x4a = x4.ap[:, 0:NTOK]; x5a = x5.ap[:, 0:NTOK]
                    for (t0, n) in TILES:
                        ps = nps()
                        kb.mm(ps, ps.ap[:, :n], [(w2p3[:, d, hp * 128:(hp + 1) * 128], tw.ap[:, t0:t0 + n])], [w2p, tw])
                        kb.op("act", E("activation", out=x1.ap[:, t0:t0 + n], in_=ps.ap[:, :n], func=AF.Sigmoid, bias=V("w0", d * 4 + hp)), [ps, vecs], [x1])
                        ps = nps()
                        kb.mm(ps, ps.ap[:, :n], [(a2p3[:, d, hp * 128:(hp + 1) * 128], alr.ap[:, t0:t0 + n])], [a2p, alr])
                        kb.op("act", E("activation", out=x2.ap[:, t0:t0 + n], in_=ps.ap[:, :n], func=AF.Sigmoid, bias=V("a0", d * 4 + hp)), [ps, vecs], [x2])
                    if "DBGRW" in kb.dbg:
                        kb.store(kb.dram["DBGRW"][1 + d, hp], x1, x1a)
                        kb.store(kb.dram["DBGRW"][3 + d, hp], x2, x2a)
                    kb.op("dve", E("tensor_scalar", out=x3a, in0=x2a, scalar1=V("k_a", hp), scalar2=oka.ap[:, hp:hp + 1], op0=ALU.mult, op1=ALU.add), [x2, vecs, oka], [x3])
                    kb.op("dve", E("tensor_tensor", out=x3a, in0=x3a, in1=kS.ap, op=ALU.mult), [x3, kS], [x3])
                    kb.op("pool", E("tensor_tensor", out=x2a, in0=x2a, in1=kk.ap, op=ALU.mult), [x2, kk], [x2])
                    if d == 0:
                        kb.op("dve", E("tensor_tensor_scan", out=x4a, data0=maskf, data1=x1a, initial=0.0, op0=ALU.mult, op1=ALU.add), [msk, x1], [x4])
                    else:
                        kb.op("dve", E("tensor_tensor_scan", out=x4a[:, ::-1], data0=maskr[:, ::-1], data1=x1a[:, ::-1], initial=0.0, op0=ALU.mult, op1=ALU.add), [msk, x1], [x4])
                    kb.op("dve", E("tensor_tensor", out=x1a, in0=x4a, in1=x1a, op=ALU.subtract), [x4, x1], [x1])
                    kb.op("act", E("activation", out=x5a, in_=x1a, func=AF.Exp, scale=-CDEC), [x1], [x5])
                    o = ob[iob % 3]; iob += 1
                    kb.op("pool", E("tensor_tensor", out=o.ap, in0=kk.ap, in1=x5a, op=ALU.mult), [kk, x5], [o])
                    kb.store(RWU7[hp][d][:, :, 0, :], o, o.ap.rearrange("p (c t) -> p c t", t=128), q="pool")
                    kb.op("act", E("activation", out=x5a, in_=x4a, func=AF.Exp, scale=-CDEC), [x4, o], [x5])
                    o = ob[iob % 3]; iob += 1
                    kb.op("dve", E("tensor_tensor", out=o.ap, in0=rS.ap, in1=x5a, op=ALU.mult), [rS, x5], [o])
                    kb.store(RWU7[hp][d][:, :, 1, :], o, o.ap.rearrange("p (c t) -> p c t", t=128))
                    e1v = x5a.rearrange("p (c t) -> p c t", t=128)
                    kb.op("dve", E("tensor_copy", out=glt.ap, in_=(e1v[:, :, 127] if d == 0 else e1v[:, :, 0])), [x5], [glt])
                    kb.store(RWGL[hp, d], glt, glt.ap)
                    kb.op("act", E("activation", out=x1a, in_=x4a, func=AF.Exp, scale=CDEC), [x4, x1], [x1])
                    o = ob[iob % 3]; iob += 1
                    kb.op("dve", E("tensor_tensor", out=o.ap, in0=x3a, in1=x1a, op=ALU.mult), [x3, x1], [o])
                    kb.store(RWU7[hp][d][:, :, 2, :], o, o.ap.rearrange("p (c t) -> p c t", t=128), q="pool")
                    o = ob[iob % 3]; iob += 1
                    kb.op("dve", E("tensor_tensor", out=o.ap, in0=x2a, in1=x1a, op=ALU.mult), [x2, x1], [o])
                    kb.store(RWU7[hp][d][:, :, 3, :], o, o.ap.rearrange("p (c t) -> p c t", t=128))
                    x1v = x1a.rearrange("p (c t) -> p c t", t=128)
                    kb.op("dve", E("tensor_tensor", out=x1v, in0=x1v, in1=bc(glt.ap.rearrange("p (c o) -> p c o", o=1), [128, NCH, 128]), op=ALU.mult), [x1, glt], [x1])
                    o = ob[iob % 3]; iob += 1
                    kb.op("pool", E("tensor_tensor", out=o.ap, in0=x3a, in1=x1a, op=ALU.mult), [x3, x1], [o])
                    kb.store(RWU7[hp][d][:, :, 4, :], o, o.ap.rearrange("p (c t) -> p c t", t=128), q="pool")
                    o = ob[iob % 3]; iob += 1
                    kb.op("dve", E("tensor_tensor", out=o.ap, in0=x2a, in1=x1a, op=ALU.mult), [x2, x1], [o])
                    kb.store(RWU7[hp][d][:, :, 5, :], o, o.ap.rearrange("p (c t) -> p c t", t=128))
                    if d == 0:
                        kb.op("dve", E("scalar_tensor_tensor", out=pbn.ap, in0=rS.ap, scalar=V("r_k", hp), in1=x3a, op0=ALU.mult, op1=ALU.mult), [rS, x3, vecs], [pbn])
                    else:
                        kb.op("dve", E("scalar_tensor_tensor", out=x4a, in0=rS.ap, scalar=V("r_k", hp), in1=x3a, op0=ALU.mult, op1=ALU.mult), [rS, x3, vecs, x4], [x4])
                        kb.op("dve", E("tensor_tensor", out=pbn.ap, in0=pbn.ap, in1=x4a, op=ALU.add), [pbn, x4], [pbn])
                ist = 0
                for (t0, n) in TILES:
                    ps = nps(); sgb = stg[ist % 2]; ist += 1
                    kb.mm(ps, ps.ap[:, :n], [(CMF(4), pbn.ap[:, t0:t0 + n])], [cmf, pbn])
                    kb.op("dve", E("tensor_tensor", out=sgb.ap[:, :n], in0=ps.ap[:, :n], in1=vS.ap[:, t0:t0 + n], op=ALU.mult), [ps, vS], [sgb])
                    kb.store(RWBON[hp, :, t0:t0 + n], sgb, sgb.ap[:, :n])
                    ps = nps(); sgb = stg[ist % 2]; ist += 1
                    kb.mm(ps, ps.ap[:, :n], [(g2t.ap[:, hp * 128:(hp + 1) * 128], sg.ap[:, t0:t0 + n])], [g2t, sg])
                    kb.op("act", E("copy", out=sgb.ap[:, :n], in_=ps.ap[:, :n]), [ps], [sgb])
                    kb.store(RWG[hp, :, t0:t0 + n], sgb, sgb.ap[:, :n])
            kb.reset(m0)
            if stop_after == "rwa":
                return
            import os
            yT = [kb.alloc(NTOK, F32, f"yT{hp}") for hp in range(4)]
            m1 = kb.mark()
            NU = 4
            mk = kb.alloc(2 * 1280, F32, "mk"); mk3 = mk.ap.rearrange("p (d m) -> p d m", d=2)
            kb.load(mk, mk.ap, msk_in.rearrange("p d m -> p (d m)"))
            glall = kb.alloc(4 * 2 * NCH, F32, "glall"); gl4 = glall.ap.rearrange("p (a d c) -> p a d c", a=4, d=2)
            kb.load(glall, gl4, RWGL.rearrange("a d p c -> p a d c"))
            U7 = [[kb.alloc(896, F32, f"U7_{p}_{u}") for u in range(NU)] for p in range(2)]
            PD = [[kb.alloc(768, F32, f"PD_{p}_{u}") for u in range(NU)] for p in range(2)]
            for p in range(2):
                for u in range(NU):
                    kb.op("pool", E("memset", PD[p][u].ap, 0.0), [], [PD[p][u]])
            KBV = [kb.alloc(384, F32, f"KBV{u}") for u in range(NU)]
            VP = [kb.alloc(256, F32, f"VP{u}") for u in range(NU)]
            UP = [kb.alloc(256, F32, f"UP{u}") for u in range(NU)]
            BCt = [kb.alloc(512, F32, f"BC{u}") for u in range(NU)]
            ZDt = [kb.alloc(512, F32, f"ZD{u}") for u in range(NU)]
            X0T = [kb.alloc(256, F32, f"X0T{u}") for u in range(NU)]
            XX = [[kb.alloc(512, F32, f"XX{q}{u}") for u in range(NU)] for q in range(2)]
            RR = [[kb.alloc(256, F32, f"RR{q}{u}") for u in range(NU)] for q in range(2)]
            XIN = [kb.alloc(128, F32, f"XIN{u}") for u in range(NU)]
            UNt = [kb.alloc(128, F32, f"UN{u}") for u in range(NU)]
            Sst = [[kb.alloc(128, F32, f"S{p}{u}") for u in range(NU)] for p in range(2)]
            TMPS = [kb.alloc(128, F32, f"tmpS{u}") for u in range(NU)]
            for u in range(NU):
                kb.op("pool", E("memset", VP[u].ap, 0.0), [], [VP[u]])
                kb.op("pool", E("memset", UP[u].ap, 0.0), [], [UP[u]])
            bks = [kb.pbank(i) for i in range(8)]
            ib = [0]

            def nb():
                b_ = bks[ib[0] % 8]; ib[0] += 1
                return b_
            NST = int(os.environ.get('RWB_STEPS', NCH))
            ident2 = bc(CMF(0).rearrange("p (o n) -> p o n", o=1), [128, 2, 128])
            gstep = 0
            for d in range(2):
                for u in range(NU):
                    kb.op("pool", E("memset", Sst[gstep % 2][u].ap, 0.0), [], [Sst[gstep % 2][u]])

                def loads(s_, p):
                    for u in range(NU):
                        hp = u
                        c = ORDER[d][s_]
                        src = RWU[hp, d, c]
                        kb.load(U7[p][u], U7[p][u].ap, src[:, :])
                        pd4 = PD[p][u].ap.rearrange("p (w a t) -> p w a t", w=3, a=2)
                        for w, slot in enumerate((2, 3, 0)):
                            kb.load(PD[p][u], pd4[0:64, w, 0, :], src[0:64, slot * 128:(slot + 1) * 128])
                            kb.load(PD[p][u], pd4[64:128, w, 1, :], src[64:128, slot * 128:(slot + 1) * 128])
                loads(0, gstep % 2)
                for s_ in range(NST):
                    p = gstep % 2; po = 1 - p
                    if s_ + 1 < NST:
                        loads(s_ + 1, po)
                    c = ORDER[d][s_]
                    for u in range(NU):
                        u7 = U7[p][u]; pd4 = PD[p][u].ap.rearrange("p (w a t) -> p w a t", w=3, a=2)
                        bT = nb()
                        for j, slot in enumerate((4, 5, 6)):
                            kb.S.op("pe", E("transpose", out=bT.ap[:, j * 128:(j + 1) * 128], in_=u7.ap[:, slot * 128:(slot + 1) * 128], identity=CMF(0)),
                                    reads=[u7, cmf], writes=[bT], sig=(j == 2))
                        kb.op("act", E("copy", out=KBV[u].ap, in_=bT.ap[:, 0:384]), [bT], [KBV[u]])
                        vp64 = VP[u].ap.rearrange("p (a c) -> p a c", c=64)
                        kb.op("pool", E("tensor_copy", out=vp64[:, 0:4:3, :], in_=KBV[u].ap[:, 256:384].rearrange("p (a c) -> p a c", a=2)), [KBV[u]], [VP[u]])
                        rk_ = u7.ap[:, 0:256]
                        b1 = nb()
                        for a_ in range(2):
                            kb.mm1(b1, b1.ap[:, a_ * 256:(a_ + 1) * 256], pd4[:, 0, a_, :], rk_, [PD[p][u], u7])
                        kb.op("dve", E("tensor_tensor", out=BCt[u].ap, in0=b1.ap, in1=mk3[:, d, 0:512], op=ALU.mult), [b1, mk], [BCt[u]])
                        b1 = nb()
                        for a_ in range(2):
                            kb.mm1(b1, b1.ap[:, a_ * 256:(a_ + 1) * 256], pd4[:, 1, a_, :], rk_, [PD[p][u], u7])
                        kb.op("dve", E("tensor_tensor", out=ZDt[u].ap, in0=b1.ap, in1=mk3[:, d, 512:1024], op=ALU.mult), [b1, mk], [ZDt[u]])
                        zd3 = ZDt[u].ap.rearrange("p (a m) -> p a m", a=2)
                        kb.op("pool", E("tensor_tensor", out=RR[0][u].ap.rearrange("p (a m) -> p a m", a=2), in0=zd3[:, :, 0:128], in1=ident2, op=ALU.add), [ZDt[u], cmf], [RR[0][u]])
                        b1 = nb()
                        for a_ in range(2):
                            kb.mm1(b1, b1.ap[:, a_ * 128:(a_ + 1) * 128], pd4[:, 2, a_, :], u7.ap[:, 384:512], [PD[p][u], u7])
                        kb.op("dve", E("tensor_tensor", out=X0T[u].ap, in0=b1.ap[:, 0:256], in1=mk3[:, d, 1024:1280], op=ALU.mult), [b1, mk], [X0T[u]])
                    for lvl in range(6):
                        q0 = lvl % 2; q1 = 1 - q0
                        hs = {}
                        for u in range(NU):
                            b1 = nb(); hs[u] = b1
                            for a_ in range(2):
                                if lvl == 0:
                                    Xk = ZDt[u].ap[:, a_ * 256:a_ * 256 + 128]; XTk = X0T[u].ap[:, a_ * 128:(a_ + 1) * 128]; rd = [ZDt[u], X0T[u]]
                                else:
                                    Xk = XX[q0][u].ap[:, a_ * 256:a_ * 256 + 128]; XTk = XX[q0][u].ap[:, a_ * 256 + 128:a_ * 256 + 256]; rd = [XX[q0][u]]
                                if lvl < 5:
                                    kb.mm1(b1, b1.ap[:, a_ * 256:a_ * 256 + 128], XTk, Xk, rd)
                                kb.mm1(b1, b1.ap[:, a_ * 256 + 128:a_ * 256 + 256], Xk, XTk, rd)
                        for u in range(NU):
                            b1 = hs[u]
                            if lvl < 5:
                                kb.op("act", E("copy", out=XX[q1][u].ap, in_=b1.ap), [b1], [XX[q1][u]])
                            else:
                                kb.op("act", E("copy", out=XX[q1][u].ap.rearrange("p (a m) -> p a m", a=2)[:, :, 128:256],
                                               in_=b1.ap.rearrange("p (a m) -> p a m", a=2)[:, :, 128:256]), [b1], [XX[q1][u]])
                        for u in range(NU):
                            b1 = nb(); hs[u] = b1
                            for a_ in range(2):
                                kb.mm1(b1, b1.ap[:, a_ * 128:(a_ + 1) * 128], XX[q1][u].ap[:, a_ * 256 + 128:a_ * 256 + 256], RR[q0][u].ap[:, a_ * 128:(a_ + 1) * 128],
                                       [XX[q1][u], RR[q0][u]])
                        for u in range(NU):
                            b1 = hs[u]
                            kb.op("dve", E("tensor_tensor", out=RR[q1][u].ap, in0=b1.ap[:, 0:256], in1=RR[q0][u].ap, op=ALU.add), [b1, RR[q0][u]], [RR[q1][u]])
                    RF = RR[0]
                    hx = {}
                    for u in range(NU):
                        hp = u
                        kb.op("pool", E("tensor_scalar", out=TMPS[u].ap, in0=Sst[p][u].ap, scalar1=gl4[:, hp, d, c:c + 1], scalar2=0.0, op0=ALU.mult, op1=ALU.add), [Sst[p][u], glall], [TMPS[u]])
                        b1 = nb(); hx[u] = b1
                        u7 = U7[p][u]
                        kb.mm1(b1, b1.ap[:, 0:128], u7.ap[:, 0:128], Sst[p][u].ap, [u7, Sst[p][u]], start=True, stop=False, sig=False)
                        kb.mm1(b1, b1.ap[:, 0:64], BCt[u].ap[:, 0:128], KBV[u].ap[:, 256:320], [BCt[u], KBV[u]], start=False, stop=False, sig=False)
                        kb.mm1(b1, b1.ap[:, 64:128], BCt[u].ap[:, 256:384], KBV[u].ap[:, 320:384], [BCt[u], KBV[u]], start=False, stop=True, sig=True)
                    for u in range(NU):
                        kb.op("act", E("copy", out=XIN[u].ap, in_=hx[u].ap[:, 0:128]), [hx[u]], [XIN[u]])
                    for u in range(NU):
                        b1 = nb(); hx[u] = b1
                        for a_ in range(2):
                            kb.mm1(b1, b1.ap[:, a_ * 64:(a_ + 1) * 64], RF[u].ap[:, a_ * 128:(a_ + 1) * 128], XIN[u].ap[:, a_ * 64:(a_ + 1) * 64], [RF[u], XIN[u]])
                    for u in range(NU):
                        kb.op("dve", E("tensor_scalar", out=UNt[u].ap, in0=hx[u].ap[:, 0:128], scalar1=-1.0, scalar2=None, op0=ALU.mult), [hx[u]], [UNt[u]])
                        up64 = UP[u].ap.rearrange("p (a c) -> p a c", c=64)
                        kb.op("pool", E("tensor_copy", out=up64[:, 0:4:3, :], in_=UNt[u].ap.rearrange("p (a c) -> p a c", a=2)), [UNt[u]], [UP[u]])
                    for u in range(NU):
                        b1 = nb(); hx[u] = b1
                        kb.mm(b1, b1.ap[:, 0:128], [(KBV[u].ap[:, 0:128], KBV[u].ap[:, 256:384]), (KBV[u].ap[:, 128:256], UNt[u].ap)], [KBV[u], UNt[u]])
                    for u in range(NU):
                        kb.op("dve", E("tensor_tensor", out=Sst[po][u].ap, in0=hx[u].ap[:, 0:128], in1=CMF(8), op=ALU.mult), [hx[u], cmf], [Sst[po][u]])
                        kb.op("pool", E("tensor_tensor", out=Sst[po][u].ap, in0=Sst[po][u].ap, in1=TMPS[u].ap, op=ALU.add), [Sst[po][u], TMPS[u]], [Sst[po][u]])
                    for u in range(NU):
                        hp = u
                        b1 = nb(); u7 = U7[p][u]
                        vp3 = VP[u].ap.rearrange("p (a c) -> p a c", a=2); up3 = UP[u].ap.rearrange("p (a c) -> p a c", a=2)
                        kb.mm(b1, b1.ap[:, 0:128], [(Sst[p][u].ap, u7.ap[:, 128:256]),
                                                     (vp3[:, 0, :], BCt[u].ap[:, 128:256]), (vp3[:, 1, :], BCt[u].ap[:, 384:512]),
                                                     (up3[:, 0, :], ZDt[u].ap[:, 128:256]), (up3[:, 1, :], ZDt[u].ap[:, 384:512])],
                              [Sst[p][u], u7, VP[u], UP[u], BCt[u], ZDt[u]])
                        if "DBGYD" in kb.dbg:
                            kb.op("dve", E("tensor_copy", out=TMPS[u].ap, in_=b1.ap[:, 0:128]), [b1], [TMPS[u]])
                            kb.store(kb.dram["DBGYD"][d, hp, :, c * 128:(c + 1) * 128], TMPS[u], TMPS[u].ap)
                        ycol = yT[hp].ap[:, c * 128:(c + 1) * 128]
                        if d == 0:
                            kb.op("dve", E("tensor_copy", out=ycol, in_=b1.ap[:, 0:128]), [b1], [yT[hp]])
                        else:
                            kb.op("dve", E("tensor_tensor", out=ycol, in0=b1.ap[:, 0:128], in1=ycol, op=ALU.add), [b1, yT[hp]], [yT[hp]])
                    gstep += 1
            if stop_after == "rwb":
                return
            if "DBGY" in kb.dbg:
                for hp in range(4):
                    kb.store(kb.dram["DBGY"][hp], yT[hp], yT[hp].ap)
            kb.reset(m1)
            gb = [kb.alloc(512, F32, f"gb{i}") for i in range(2)]
            bb_ = [kb.alloc(512, F32, f"bb{i}") for i in range(2)]
            dv_ = [kb.alloc(512, F32, f"dv{i}") for i in range(2)]
            sq_ = [kb.alloc(512, F32, f"sq{i}") for i in range(2)]
            oo = [kb.alloc(512, BF16, f"oo{i}") for i in range(2)]
            pbk = [kb.pbank(i) for i in range(4)]
            it = 0
            for hp in range(4):
                for (t0, n) in TILES:
                    g_ = gb[it % 2]; b_ = bb_[it % 2]; dd = dv_[it % 2]; qq = sq_[it % 2]; o_ = oo[it % 2]
                    pm = pbk[(2 * it) % 4]; pv = pbk[(2 * it + 1) % 4]; it += 1
                    kb.load(g_, g_.ap[:, :n], RWG[hp, :, t0:t0 + n])
                    kb.load(b_, b_.ap[:, :n], RWBON[hp, :, t0:t0 + n])
                    ysl = yT[hp].ap[:, t0:t0 + n]
                    kb.mm(pm, pm.ap[:, :n], [(CMF(5), ysl)], [cmf, yT[hp]])
                    kb.op("dve", E("tensor_tensor", out=dd.ap[:, :n], in0=ysl, in1=pm.ap[:, :n], op=ALU.subtract), [yT[hp], pm], [dd])
                    kb.op("act", E("activation", out=qq.ap[:, :n], in_=dd.ap[:, :n], func=AF.Square), [dd], [qq])
                    kb.mm(pv, pv.ap[:, :n], [(CMF(5), qq.ap[:, :n])], [cmf, qq])
                    kb.op("act", E("activation", out=qq.ap[:, :n], in_=pv.ap[:, :n], func=AF.Sqrt, bias=64e-5), [pv, qq], [qq])
                    kb.op("dve", E("reciprocal", out=qq.ap[:, :n], in_=qq.ap[:, :n]), [qq], [qq])
                    kb.op("dve", E("scalar_tensor_tensor", out=dd.ap[:, :n], in0=dd.ap[:, :n], scalar=V("gn_w", hp), in1=qq.ap[:, :n], op0=ALU.mult, op1=ALU.mult), [dd, qq, vecs], [dd])
                    kb.op("dve", E("scalar_tensor_tensor", out=dd.ap[:, :n], in0=dd.ap[:, :n], scalar=V("gn_b", hp), in1=b_.ap[:, :n], op0=ALU.add, op1=ALU.add), [dd, b_, vecs], [dd])
                    kb.op("pool", E("tensor_tensor", out=o_.ap[:, :n], in0=dd.ap[:, :n], in1=g_.ap[:, :n], op=ALU.mult), [dd, g_], [o_])
                    kb.store(mixT[512 + hp * 128:512 + (hp + 1) * 128, t0:t0 + n], o_, o_.ap[:, :n])
            kb.reset(m0)

        ALLT = [(ti, t0, n, t0) for ti, (t0, n) in enumerate(TILES)]
        LATT = [(ti, t0, n, t0 - NCTX) for ti, (t0, n) in enumerate(TILES) if ti > 0]

        MLPA = [(0 if t0 == 0 else 1, t0, 256, t0) for t0 in range(0, NTOK, 256)]
        MLPL = [(1, t0, 256, t0 - NCTX) for t0 in range(NCTX, NTOK, 256)]
        LATI = [(ti, t0, n, t0) for ti, (t0, n) in enumerate(TILES) if ti > 0]
        chunks0 = [(i * 128, 128) for i in range(5)] + [(640, 96), (736, 96)] + [(832 + i * 128, 128) for i in range(15)]
        chunks1 = [(i * 128, 128) for i in range(16)]
        if "DBGL1" in kb.dbg:
            kb.dram_t("DBGL1", [3, 8, 128, NTOK], F32)
        if "DBGRW" in kb.dbg:
            kb.dram_t("DBGRW", [5, 4, 128, NTOK], F32)
        if "DBGY" in kb.dbg:
            kb.dram_t("DBGY", [4, 128, NTOK], F32)
        if "DBGYD" in kb.dbg:
            kb.dram_t("DBGYD", [2, 4, 128, NTOK], F32)

        def fin():
            S.run()
            return nc
        import os as _os
        if _os.environ.get("ONLY_RWB"):
            stage_rwkv()
            return fin()
        if start_layer == 0:
            stage_mod(0)
            if "DBGMOD" in kb.dbg:
                dm = kb.dram_t("DBGMOD", [128, 96], F32)
                kb.store(dm[:, :], modv[0], modv[0].ap)
            if stop_after == "mod":
                return fin()
            stage_win(0, xT_in, 2752, "w_in0", chunks0, F0T)
            if stop_after == "win0":
                return fin()
            stage_mla()
            if stop_after == "mla":
                return fin()
            stage_rwkv()
            if stop_after in ("rwkv", "rwa", "rwb"):
                return fin()
            stage_wout(0, xT_in, xT, ALLT)
            if stop_after == "wout0":
                return fin()
            stage_mlp(0, xT, xT, MLPA)
            if stop_after == "mlp0":
                return fin()
            x1src = xT
        else:
            x1src = xT_in
        stage_mod(1)
        stage_win(1, x1src, 2048, "w_in1", chunks1, G1T, gelu_chunks=set(range(8)))
        if stop_after == "win1":
            return fin()
        stage_lru()
        if stop_after == "lru":
            return fin()
        stage_wout(1, x1src, xT, LATI)
        if stop_after == "wout1":
            return fin()
        stage_mlp(1, xT, outT, MLPL)
        S.run()
    return nc


def _rope_tables():
    rows = np.repeat(np.arange(32, dtype=np.float32), 64)
    cols = np.tile(np.arange(64, dtype=np.float32), 32)
    inv = (10000.0 ** (-np.arange(0, 16, 2, dtype=np.float32) / 16)).astype(np.float32)
    ar = rows[:, None] * inv
    ac = cols[:, None] * inv
    ang = np.concatenate([ar, ar, ac, ac], -1)
    cos = np.cos(ang); sin = np.sin(ang)
    sgn = np.tile(np.concatenate([-np.ones(8), np.ones(8)]), 2).astype(np.float32)
    out = np.zeros((128, 2, NLAT), np.float32)
    out[64:96, 0, :] = cos.T
    out[64:96, 1, :] = (sin * sgn).T
    return out


def _rot_perm():
    perm = np.zeros(32, np.int64)
    for a in range(2):
        for h in range(2):
            for f in range(8):
                perm[a * 16 + h * 8 + f] = a * 16 + (1 - h) * 8 + f
    return perm


def prep_inputs(inputs):
    I = {k: np.asarray(v) for k, v in inputs.items()}
    shared = {}
    cmats = np.zeros((128, 9, 128), np.float32)
    cmats[:, 0] = np.eye(128)
    cmats[:, 1] = 1.0 / 1024
    cmats[:, 2] = 1.0 / 384
    cmats[:, 3] = 1.0 / 256
    bo = np.zeros((128, 128), np.float32); bo[:64, :64] = 1; bo[64:, 64:] = 1
    cmats[:, 4] = bo
    cmats[:, 5] = bo / 64
    cmats[64, 6, :64] = 1.0
    cmats[:, 7] = 1.0
    cmats[:, 8] = bo
    shared["cmats"] = cmats
    shared["rope"] = _rope_tables()
    ind = np.zeros((128, 8, 8), np.float32)
    for h in range(8):
        ind[:96, h, h] = 1.0
    shared["ind8"] = ind
    ii = np.arange(128)[:, None]; tt = np.arange(128)[None, :]
    msk = np.zeros((128, 2, 1280), np.float32)
    for d, (st_, inc_) in enumerate((((ii < tt), (ii <= tt)), ((ii > tt), (ii >= tt)))):
        st_ = st_.astype(np.float32); inc_ = inc_.astype(np.float32)
        msk[:, d, 0:512] = np.concatenate([st_, inc_, st_, inc_], 1)
        msk[:, d, 512:1024] = np.concatenate([-st_, inc_, -st_, inc_], 1)
        msk[:, d, 1024:1280] = np.concatenate([-st_.T, -st_.T], 1)
    shared["rwmask"] = msk
    for L in range(2):
        p = f"l{L}_"
        for nm in ("mod_w", "w_out", "mlp_w1", "mlp_w2"):
            shared[p + nm] = np.ascontiguousarray(I[p + nm], np.float32)
    w_in = I["l0_w_in"]
    perm = _rot_perm()
    w0e = np.zeros((1024, 2752), np.float32)
    w0e[:, 0:640] = w_in[:, 0:640]
    w0e[:, 640 + 64:640 + 96] = w_in[:, 640:672]
    w0e[:, 736 + 64:736 + 96] = w_in[:, 640:672][:, perm]
    w0e[:, 832:] = w_in[:, 672:]
    shared["w_in0"] = w0e
    wuq = I["l0_mla_w_uq"].reshape(384, 8, 96)
    wrot = np.zeros_like(wuq)
    wrot[:, :, 64:96] = wuq[:, :, 64:96][:, :, perm]
    shared["w_uq"] = np.ascontiguousarray(wuq.reshape(384, 768))
    shared["w_uq_rot"] = np.ascontiguousarray(wrot.reshape(384, 768))
    shared["w_uk"] = np.ascontiguousarray(I["l0_mla_w_uk"])
    shared["w_uv"] = np.ascontiguousarray(I["l0_mla_w_uv"])
    for nm, src in (("w2pad", "l0_rwkv_w2"), ("a2pad", "l0_rwkv_a2")):
        a = np.zeros((128, 2, 512), np.float32)
        a[0:64, 0] = I[src][0]; a[64:128, 1] = I[src][1]
        shared[nm] = a
    shared["g2"] = np.ascontiguousarray(I["l0_rwkv_g2"])
    shared["w_in1"] = np.ascontiguousarray(I["l1_w_in"])
    shared["ga_w"] = np.ascontiguousarray(np.transpose(I["l1_lru_ga_w"], (2, 0, 1, 3)))
    shared["gx_w"] = np.ascontiguousarray(np.transpose(I["l1_lru_gx_w"], (2, 0, 1, 3)))
    vbase = np.zeros((128, NCOL), np.float32)

    def put(name, arr, n):
        vbase[:, COLS[name]:COLS[name] + n] = _pcol(arr, n)
    put("cctxT", I["c_ctx"], 8)
    for L in range(2):
        p = f"l{L}_"
        put(p + "mod_b", I[p + "mod_b"], 48)
        for nm in ("ln1_g", "ln1_b", "ln2_g", "ln2_b"):
            put(p + nm, I[p + nm], 8)
    put("q_norm", I["l0_mla_q_norm"], 3); put("kv_norm", I["l0_mla_kv_norm"], 2); put("mu", I["l0_rwkv_mu"], 15)
    put("w0", I["l0_rwkv_w0"].reshape(-1), 8); put("a0", I["l0_rwkv_a0"].reshape(-1), 8)
    put("k_k", I["l0_rwkv_k_k"], 4); put("k_a", I["l0_rwkv_k_a"], 4); put("r_k", I["l0_rwkv_r_k"].reshape(-1), 4)
    put("gn_w", I["l0_rwkv_gn_w"], 4); put("gn_b", I["l0_rwkv_gn_b"], 4)
    put("conv_w", I["l1_conv_w"].reshape(-1), 32); put("conv_b", I["l1_conv_b"], 8)
    put("ga_b", I["l1_lru_ga_b"].reshape(-1), 16); put("gx_b", I["l1_lru_gx_b"].reshape(-1), 16)
    put("lam", I["l1_lru_lambda"].reshape(-1), 16)
    per_core = []
    for b in range(8):
        v = vbase.copy()
        v[:, COLS["cT"]:COLS["cT"] + 8] = _pcol(I["c"][b], 8)
        xTb = np.ascontiguousarray(np.concatenate([I["ctx"][b].T, I["x"][b].T], axis=1), np.float32)
        per_core.append({"xT": xTb, "vec": v})
    return shared, per_core


_NC_CACHE = {}


def kernel(**inputs):
    shared, per_core = prep_inputs(inputs)
    if "nc" not in _NC_CACHE:
        _NC_CACHE["nc"] = build_program()
    nc = _NC_CACHE["nc"]
    in_maps = [dict(shared, **pc) for pc in per_core]
    res = run_bass_kernel_spmd(nc, in_maps, core_ids=list(range(8)))
    out = np.stack([np.ascontiguousarray(r["outT"].T) for r in res.results], axis=0)
    return out.astype(np.float32)
```

```python
import contextlib
import numpy as np
import concourse.bass as bass
import concourse.mybir as mybir
from concourse.bass_utils import run_bass_kernel_spmd

F32 = mybir.dt.float32
BF16 = mybir.dt.bfloat16
AF = mybir.ActivationFunctionType
ALU = mybir.AluOpType
AX = mybir.AxisListType

NTOK = 2304
NCTX = 256
NLAT = 2048
TILES = [(0, 256), (256, 512), (768, 512), (1280, 512), (1792, 512)]
ALPHA = 4.0 ** 0.25
CDEC = float(np.exp(-0.5))
SCALE = 96.0 ** -0.5
NCH = 18

ENGS = ("pe", "act", "dve", "pool", "sp")
NDMASEM = 8


class Buf:
    __slots__ = ("name", "w", "r")

    def __init__(self, name=""):
        self.name = name
        self.w = None
        self.r = []


class T:
    __slots__ = ("ap", "b")

    def __init__(self, ap, name=""):
        self.ap = ap
        self.b = Buf(name)


def _b(x):
    return x.b if isinstance(x, T) else x


class Sched:
    def __init__(self, nc):
        self.nc = nc
        self.streams = {e: [] for e in ENGS}
        self.cnt = {e: 0 for e in ENGS}
        self.waited = {}
        self.sem = {}
        self.dma_n = {"sp": 0, "pool": 0, "act": 0}
        self.pending_nosig = {e: False for e in ENGS}
        self.ninst = 0

    def _semkeys(self):
        keys = list(ENGS)
        for q in ("sp", "pool", "act"):
            for i in range(NDMASEM):
                keys.append(("dma", q, i))
        return keys

    def _need(self, eng, tok, waits):
        if tok is None:
            return
        key, val = tok
        if self.waited.get((eng, key), 0) >= val:
            return
        if key == eng and eng == "pe":
            return
        self.waited[(eng, key)] = val
        waits[key] = max(waits.get(key, 0), val)

    def _deps(self, eng, reads, writes):
        waits = {}
        for b in reads:
            self._need(eng, _b(b).w, waits)
        for b in writes:
            b = _b(b)
            self._need(eng, b.w, waits)
            for t in b.r:
                self._need(eng, t, waits)
        return waits

    def _commit(self, tok, reads, writes):
        for b in reads:
            b = _b(b)
            b.r.append(tok)
            if len(b.r) > 48:
                d = {}
                for k, v in b.r:
                    d[k] = max(d.get(k, 0), v)
                b.r = list(d.items())
        for b in writes:
            b = _b(b)
            b.w = tok
            b.r = []

    def op(self, eng, fn, reads=(), writes=(), sig=True):
        waits = self._deps(eng, reads, writes)
        if sig:
            self.cnt[eng] += 1
            self.pending_nosig[eng] = False
        else:
            self.pending_nosig[eng] = True
        tok = (eng, self.cnt[eng] if sig else self.cnt[eng] + 1)
        self._commit(tok, reads, writes)
        self.streams[eng].append((waits, fn, eng if sig else None, 1))
        self.ninst += 1

    def dma(self, q, out, in_, reads=(), writes=()):
        n = self.dma_n[q]
        self.dma_n[q] += 1
        slot = n % NDMASEM
        key = ("dma", q, slot)
        val = 16 * (n // NDMASEM + 1)
        waits = self._deps(q, reads, writes)
        if val > 16:
            self._need(q, (key, val - 16), waits)
        tok = (key, val)
        self._commit(tok, reads, writes)
        self.streams[q].append((waits, E("dma_start", out=out, in_=in_), key, 16))
        self.ninst += 1
        return tok

    def all_tokens(self):
        toks = [(e, self.cnt[e]) for e in ENGS if self.cnt[e] > 0]
        for q, n in self.dma_n.items():
            for slot in range(min(n, NDMASEM)):
                last = ((n - 1 - slot) // NDMASEM) * NDMASEM + slot
                toks.append((("dma", q, slot), 16 * (last // NDMASEM + 1)))
        return toks

    def barrier(self):
        for e in ENGS:
            assert not self.pending_nosig[e], e
        toks = self.all_tokens()
        for e in ENGS:
            waits = {}
            for t in toks:
                self._need(e, t, waits)
            if waits:
                self.streams[e].append((waits, None, None, 0))

    def run(self):
        nc = self.nc
        self.barrier()
        with contextlib.ExitStack() as st:
            for k in self._semkeys():
                nm = k if isinstance(k, str) else f"d_{k[1]}_{k[2]}"
                self.sem[k] = st.enter_context(nc.semaphore("s_" + nm))
            block = st.enter_context(nc.Block())
            sem = self.sem

            def mk(ename):
                stream = self.streams[ename]

                def body(e):
                    for waits, fn, sigkey, inc in stream:
                        for k, v in waits.items():
                            e.wait_ge(sem[k], v)
                        if fn is not None:
                            ins = fn(e)
                            if sigkey is not None:
                                ins.then_inc(sem[sigkey], inc)
                return body

            block.tensor(mk("pe"))
            block.scalar(mk("act"))
            block.vector(mk("dve"))
            block.gpsimd(mk("pool"))
            block.sync(mk("sp"))


ARENA_F32 = 47104


class KB:
    def __init__(self, nc, arena, banks, dbg):
        self.nc = nc
        self.S = Sched(nc)
        self.arena = arena
        self.banks = banks
        self.off = 0
        self.dbg = dbg
        self.dram = {}
        self.eng_rr = 0

    def alloc(self, n, dt=F32, name=""):
        words = (n + 1) // 2 if dt == BF16 else n
        words = (words + 7) // 8 * 8
        assert self.off + words <= ARENA_F32, (name, self.off, words)
        ap = self.arena[:, self.off:self.off + words]
        self.off += words
        if dt == BF16:
            ap = ap.bitcast(BF16)[:, 0:n]
        else:
            ap = ap[:, 0:n]
        return T(ap, name)

    def mark(self):
        return self.off

    def reset(self, mark):
        self.S.barrier()
        self.off = mark

    def pbank(self, i):
        return T(self.banks[i][:, :], f"bank{i}")

    def phalf(self, i):
        return T(self.banks[i // 2][:, (i % 2) * 256:(i % 2 + 1) * 256], f"half{i}")

    def dram_t(self, name, shape, dt):
        kind = "ExternalOutput" if name in self.dbg else "Internal"
        t = self.nc.dram_tensor(name, list(shape), dt, kind=kind).ap()
        self.dram[name] = t
        return t

    def mm(self, out, out_ap, pairs, reads):
        n = len(pairs)
        for j, (l, r) in enumerate(pairs):
            self.S.op("pe", E("matmul", out_ap, lhsT=l, rhs=r, start=(j == 0), stop=(j == n - 1)),
                      reads=reads, writes=[out], sig=(j == n - 1))

    def mm1(self, out, out_ap, l, r, reads, start=True, stop=True, sig=True):
        self.S.op("pe", E("matmul", out_ap, lhsT=l, rhs=r, start=start, stop=stop), reads=reads, writes=[out], sig=sig)

    def op(self, eng, fn, reads, writes):
        self.S.op(eng, fn, reads=reads, writes=writes)

    def load(self, dst, dst_ap, src_ap, q="sp"):
        return self.S.dma(q, dst_ap, src_ap, writes=[dst])

    def store(self, dst_ap, src, src_ap, q="sp"):
        return self.S.dma(q, dst_ap, src_ap, reads=[src])


def E(name, *a, **kw):
    return lambda e: getattr(e, name)(*a, **kw)


def bc(ap, shape):
    return ap.to_broadcast(list(shape))


def _colmap():
    cols = {}
    off = 0

    def add(name, n):
        nonlocal off
        cols[name] = off
        off += n
    add("cT", 8); add("cctxT", 8)
    for L in range(2):
        p = f"l{L}_"
        add(p + "mod_b", 48)
        for nm in ("ln1_g", "ln1_b", "ln2_g", "ln2_b"):
            add(p + nm, 8)
    add("q_norm", 3); add("kv_norm", 2); add("mu", 15)
    add("w0", 8); add("a0", 8)
    for nm in ("k_k", "k_a", "r_k", "gn_w", "gn_b"):
        add(nm, 4)
    add("conv_w", 32)
    add("conv_b", 8)
    add("ga_b", 16); add("gx_b", 16); add("lam", 16)
    return cols, off


COLS, NCOL = _colmap()


def _pcol(v, n):
    return np.ascontiguousarray(np.asarray(v, np.float32).reshape(n, 128).T)


def build_program(dbg=(), stop_after=None, start_layer=0):
    nc = bass.Bass("TRN2", target_bir_lowering=False)
    IN = {}

    def inp(name, shape, dt=F32):
        IN[name] = nc.dram_tensor(name, list(shape), dt, kind="ExternalInput").ap()
        return IN[name]

    xT_in = inp("xT", [1024, NTOK])
    vec = inp("vec", [128, NCOL])
    cm = inp("cmats", [128, 9, 128])
    rope_in = inp("rope", [128, 2, NLAT])
    ind_in = inp("ind8", [128, 8, 8])
    msk_in = inp("rwmask", [128, 2, 1280])
    W = {}
    for L in range(2):
        p = f"l{L}_"
        W[p + "mod_w"] = inp(p + "mod_w", [1024, 6144])
        W[p + "w_out"] = inp(p + "w_out", [1024, 1024])
        W[p + "mlp_w1"] = inp(p + "mlp_w1", [1024, 4096])
        W[p + "mlp_w2"] = inp(p + "mlp_w2", [4096, 1024])
    W["w_in0"] = inp("w_in0", [1024, 2752])
    W["w_uq"] = inp("w_uq", [384, 768]); W["w_uq_rot"] = inp("w_uq_rot", [384, 768])
    W["w_uk"] = inp("w_uk", [256, 512]); W["w_uv"] = inp("w_uv", [256, 512])
    W["w2pad"] = inp("w2pad", [128, 2, 512]); W["a2pad"] = inp("a2pad", [128, 2, 512]); W["g2"] = inp("g2", [128, 512])
    W["w_in1"] = inp("w_in1", [1024, 2048])
    W["ga_w"] = inp("ga_w", [128, 2, 8, 128]); W["gx_w"] = inp("gx_w", [128, 2, 8, 128])
    outT = nc.dram_tensor("outT", [1024, NLAT], F32, kind="ExternalOutput").ap()

    with contextlib.ExitStack() as st:
        arena = st.enter_context(nc.sbuf_tensor("arena", [128, ARENA_F32], F32))
        banks = [st.enter_context(nc.psum_tensor(f"bank{i}", [128, 512], F32)) for i in range(8)]
        kb = KB(nc, arena, banks, set(dbg))
        S = kb.S
        xT = kb.dram_t("xTs", [1024, NTOK], F32)
        F0T = kb.dram_t("F0T", [22, 128, NTOK], F32)
        mixT = kb.dram_t("mixT", [1024, NTOK], BF16)
        RWU = kb.dram_t("RWU", [4, 2, NCH, 128, 7 * 128], F32)
        RWGL = kb.dram_t("RWGL", [4, 2, 128, NCH], F32)
        RWG = kb.dram_t("RWG", [4, 128, NTOK], F32)
        RWBON = kb.dram_t("RWBON", [4, 128, NTOK], F32)
        G1T = kb.dram_t("G1T", [16, 128, NTOK], F32)
        HID = None

        vecs = kb.alloc(NCOL, F32, "vecs")
        cmf = kb.alloc(9 * 128, F32, "cmf")
        cmb = kb.alloc(9 * 128, BF16, "cmb")
        modv = [kb.alloc(96, F32, f"modv{L}") for L in range(2)]
        mod1p = [kb.alloc(96, F32, f"mod1p{L}") for L in range(2)]
        kb.load(vecs, vecs.ap, vec[:, :])
        kb.load(cmf, cmf.ap, cm.rearrange("p a b -> p (a b)"))
        kb.load(cmb, cmb.ap, cm.rearrange("p a b -> p (a b)"), q="pool")
        PERSIST = kb.mark()

        def V(name, i=0, n=1):
            o = COLS[name] + i
            return vecs.ap[:, o:o + n]

        def CMF(i):
            return cmf.ap[:, i * 128:(i + 1) * 128]

        def CMB(i):
            return cmb.ap[:, i * 128:(i + 1) * 128]

        def MOD(L, j, k, which):
            o = (j * 8 + k) * 2 + which
            return modv[L].ap[:, o:o + 1]

        def MOD1P(L, j, k, which):
            o = (j * 8 + k) * 2 + which
            return mod1p[L].ap[:, o:o + 1]

        def xview(t):
            return t.rearrange("(k p) n -> p k n", p=128)


        def stage_mod(L):
            p = f"l{L}_"
            m0 = kb.mark()
            cs = kb.alloc(16, F32, "cs")
            mwb = [kb.alloc(8 * 512, F32, f"mw{i}") for i in range(2)]
            ps = kb.pbank(0)
            csv = cs.ap.rearrange("p (k w) -> p k w", w=2)
            kb.op("act", E("activation", out=csv[:, :, 0], in_=V("cT", 0, 8), func=AF.Silu), [vecs], [cs])
            kb.op("act", E("activation", out=csv[:, :, 1], in_=V("cctxT", 0, 8), func=AF.Silu), [vecs], [cs])
            mwv = W[p + "mod_w"].rearrange("(k p) n -> p k n", p=128)
            for piece in range(12):
                mw = mwb[piece % 2]
                mw3 = mw.ap.rearrange("p (k n) -> p k n", k=8)
                kb.load(mw, mw3, mwv[:, :, piece * 512:(piece + 1) * 512], q="sp" if piece % 2 == 0 else "pool")
                for j in range(4):
                    oc = piece * 4 + j
                    kb.mm(ps, ps.ap[:, oc * 2:oc * 2 + 2],
                          [(mw3[:, k, j * 128:(j + 1) * 128], csv[:, k, :]) for k in range(8)], [mw, cs])
            mv3 = modv[L].ap.rearrange("p (a w) -> p a w", w=2)
            kb.op("dve", E("tensor_tensor", out=mv3, in0=ps.ap[:, 0:96].rearrange("p (a w) -> p a w", w=2),
                                                    in1=bc(V(p + "mod_b", 0, 48).rearrange("p (a o) -> p a o", o=1), [128, 48, 2]), op=ALU.add),
                  [ps, vecs], [modv[L]])
            kb.op("dve", E("tensor_scalar", out=mod1p[L].ap, in0=modv[L].ap, scalar1=1.0, scalar2=None, op0=ALU.add),
                  [modv[L]], [mod1p[L]])
            kb.reset(m0)

        def modulate_tile(L, jsh, jsc, xt3, ht3, n, which, xT_T, hT_T):
            for k in range(8):
                if k % 2 == 0:
                    kb.op("act", E("activation", out=ht3[:, k, :n], in_=xt3[:, k, :n], func=AF.Identity,
                                                             scale=MOD1P(L, jsc, k, which), bias=MOD(L, jsh, k, which)),
                          [xT_T, modv[L], mod1p[L]], [hT_T])
                else:
                    kb.op("dve", E("tensor_scalar", out=ht3[:, k, :n], in0=xt3[:, k, :n], scalar1=MOD1P(L, jsc, k, which),
                                                                scalar2=MOD(L, jsh, k, which), op0=ALU.mult, op1=ALU.add),
                          [xT_T, modv[L], mod1p[L]], [hT_T])

        def load_wbf_blocks(dst, dst3, src, nk, ncols, blk=512):
            sv = src.rearrange("(k p) n -> p k n", p=128)
            out = []
            for lo in range(0, ncols, blk):
                hi = min(ncols, lo + blk)
                t_ = T(dst.ap, f"wblk{lo}")
                kb.load(t_, dst3[:, :, lo:hi], sv[:, :, lo:hi], q="pool")
                out.append((lo, hi, t_))
            return out

        def wblk(blocks, c0, c1):
            return [t_ for (lo, hi, t_) in blocks if lo < c1 and hi > c0]

        def load_wbf(dst, dst3, src, nk, ncols, cpp=None):
            sv = src.rearrange("(k p) n -> p k n", p=128)
            for k in range(nk):
                kb.load(dst, dst3[:, k, :], sv[:, k, :], q="pool")

        def stage_win(L, src_x, ncolsW, wname, chunks, dstT, gelu_chunks=()):
            m0 = kb.mark()
            wt = kb.alloc(8 * ncolsW, BF16, "w_in")
            w3 = wt.ap.rearrange("p (k n) -> p k n", k=8)
            wbl = load_wbf_blocks(wt, w3, W[wname], 8, ncolsW)
            xb = [kb.alloc(8 * 512, F32, f"xb{i}") for i in range(2)]
            hb = [kb.alloc(8 * 512, BF16, f"hb{i}") for i in range(2)]
            stg = [kb.alloc(512, F32, f"stg{i}") for i in range(4)]
            pb = [kb.pbank(i) for i in range(4)]
            xv = xview(src_x)
            it = 0
            for ti, (t0, n) in enumerate(TILES):
                which = 1 if ti == 0 else 0
                xt = xb[ti % 2]; ht = hb[ti % 2]
                xt3 = xt.ap.rearrange("p (k n) -> p k n", k=8); ht3 = ht.ap.rearrange("p (k n) -> p k n", k=8)
                kb.load(xt, xt3[:, :, :n], xv[:, :, t0:t0 + n])
                modulate_tile(L, 0, 1, xt3, ht3, n, which, xt, ht)
                for ci, (c0, M) in enumerate(chunks):
                    ps = pb[it % 4]; sg = stg[it % 4]
                    kb.mm(ps, ps.ap[:M, :n], [(w3[:, k, c0:c0 + M], ht3[:, k, :n]) for k in range(8)], wblk(wbl, c0, c0 + M) + [ht])
                    if ci in gelu_chunks:
                        kb.op("act", E("activation", out=sg.ap[:M, :n], in_=ps.ap[:M, :n], func=AF.Gelu), [ps], [sg])
                    elif it % 2 == 0:
                        kb.op("act", E("copy", out=sg.ap[:M, :n], in_=ps.ap[:M, :n]), [ps], [sg])
                    else:
                        kb.op("dve", E("tensor_copy", out=sg.ap[:M, :n], in_=ps.ap[:M, :n]), [ps], [sg])
                    kb.store(dstT[ci, 0:M, t0:t0 + n], sg, sg.ap[:M, :n])
                    it += 1
            kb.reset(m0)

        def ln_tile(zt, zt3, n, gname, bname, dst3_dram, scr, pbs):
            ps_m, ps_q = pbs
            zb = scr["zb"]; zb3 = zb.ap.rearrange("p (k n) -> p k n", k=8)
            kb.op("act", E("activation", out=zb3[:, :, :n], in_=zt3[:, :, :n], func=AF.Square), [zt], [zb])
            kb.mm(ps_m, ps_m.ap[:, :n], [(CMF(1), zt3[:, k, :n]) for k in range(8)], [zt, cmf])
            kb.mm(ps_q, ps_q.ap[:, :n], [(CMB(1), zb3[:, k, :n]) for k in range(8)], [zb, cmb])
            mean = scr["mean"]; rstd = scr["rstd"]; tmp = scr["tmp"]
            kb.op("act", E("copy", out=mean.ap[:, :n], in_=ps_m.ap[:, :n]), [ps_m], [mean])
            kb.op("act", E("activation", out=tmp.ap[:, :n], in_=ps_m.ap[:, :n], func=AF.Square), [ps_m], [tmp])
            kb.op("dve", E("tensor_tensor", out=tmp.ap[:, :n], in0=ps_q.ap[:, :n], in1=tmp.ap[:, :n], op=ALU.subtract), [ps_q, tmp], [tmp])
            kb.op("dve", E("tensor_scalar", out=tmp.ap[:, :n], in0=tmp.ap[:, :n], scalar1=0.0, scalar2=None, op0=ALU.max), [tmp], [tmp])
            kb.op("act", E("activation", out=tmp.ap[:, :n], in_=tmp.ap[:, :n], func=AF.Sqrt, bias=1e-5), [tmp], [tmp])
            kb.op("dve", E("reciprocal", out=rstd.ap[:, :n], in_=tmp.ap[:, :n]), [tmp], [rstd])
            for k in range(8):
                e1, e2 = ("dve", "dve")
                kb.op(e1, E("tensor_tensor", out=zt3[:, k, :n], in0=zt3[:, k, :n], in1=mean.ap[:, :n], op=ALU.subtract), [zt, mean], [zt])
                kb.op(e2, E("tensor_tensor", out=zt3[:, k, :n], in0=zt3[:, k, :n], in1=rstd.ap[:, :n], op=ALU.mult), [zt, rstd], [zt])
                kb.op("act", E("activation", out=zt3[:, k, :n], in_=zt3[:, k, :n], func=AF.Identity, scale=V(gname, k), bias=V(bname, k)), [zt, vecs], [zt])
            kb.store(dst3_dram, zt, zt3[:, :, :n])

        def stage_wout(L, src_x, dst_x, tiles):
            p = f"l{L}_"
            m0 = kb.mark()
            wt = kb.alloc(8 * 1024, BF16, "w_out")
            w3 = wt.ap.rearrange("p (k n) -> p k n", k=8)
            wbl = load_wbf_blocks(wt, w3, W[p + "w_out"], 8, 1024, blk=256)
            xb = [kb.alloc(8 * 512, F32, f"xb{i}") for i in range(2)]
            mb_ = [kb.alloc(8 * 512, BF16, f"mb{i}") for i in range(2)]
            zt_ = [kb.alloc(8 * 512, F32, f"z{i}") for i in range(2)]
            scr = {"zb": kb.alloc(8 * 512, BF16, "zb"), "mean": kb.alloc(512, F32, "mean"), "rstd": kb.alloc(512, F32, "rstd"), "tmp": kb.alloc(512, F32, "tmp")}
            pb = [kb.pbank(i) for i in range(4)]
            pst = (kb.pbank(4), kb.pbank(5))
            xv = xview(src_x); dv = xview(dst_x); mv = mixT.rearrange("(k p) n -> p k n", p=128)
            it = 0
            for (ti, t0, n, d0) in tiles:
                which = 1 if ti == 0 else 0
                xt = xb[ti % 2]; mt = mb_[ti % 2]; zt = zt_[ti % 2]
                xt3 = xt.ap.rearrange("p (k n) -> p k n", k=8); mt3 = mt.ap.rearrange("p (k n) -> p k n", k=8); zt3 = zt.ap.rearrange("p (k n) -> p k n", k=8)
                kb.load(xt, xt3[:, :, :n], xv[:, :, t0:t0 + n])
                kb.load(mt, mt3[:, :, :n], mv[:, :, t0:t0 + n], q="act")
                kb.op("act", E("activation", out=xt3[:, :, :n], in_=xt3[:, :, :n], func=AF.Identity, scale=ALPHA), [xt], [xt])
                for oc in range(8):
                    ps = pb[it % 4]; it += 1
                    kb.mm(ps, ps.ap[:, :n], [(w3[:, k, oc * 128:(oc + 1) * 128], mt3[:, k, :n]) for k in range(8)], wblk(wbl, oc * 128, (oc + 1) * 128) + [mt])
                    kb.op("dve", E("scalar_tensor_tensor", out=zt3[:, oc, :n], in0=ps.ap[:, :n], scalar=MOD(L, 2, oc, which),
                                                                                in1=xt3[:, oc, :n], op0=ALU.mult, op1=ALU.add), [ps, xt, modv[L]], [zt])
                ln_tile(zt, zt3, n, p + "ln1_g", p + "ln1_b", dv[:, :, d0:d0 + n], scr, pst)
            kb.reset(m0)

        def stage_mlp(L, src_x, dst_x, tiles, TW=256):
            p = f"l{L}_"
            m0 = kb.mark()
            w1 = kb.alloc(8 * 4096, BF16, "w1"); w13 = w1.ap.rearrange("p (k n) -> p k n", k=8)
            w2 = kb.alloc(32 * 1024, BF16, "w2"); w23 = w2.ap.rearrange("p (k n) -> p k n", k=32)
            w1b = [T(w1.ap, f"w1b{j}") for j in range(8)]
            w2b = [T(w2.ap, f"w2b{j}") for j in range(8)]
            w1v = W[p + "mlp_w1"].rearrange("(k p) n -> p k n", p=128)
            w2v = W[p + "mlp_w2"].rearrange("(k p) n -> p k n", p=128)
            for j in range(8):
                kb.load(w1b[j], w13[:, :, j * 512:(j + 1) * 512], w1v[:, :, j * 512:(j + 1) * 512], q="pool")
            for j in range(8):
                kb.load(w2b[j], w23[:, 4 * j:4 * j + 4, :], w2v[:, 4 * j:4 * j + 4, :], q="pool")
            xts = [kb.alloc(8 * TW, F32, f"xt{i}") for i in range(2)]
            hts = [kb.alloc(8 * TW, BF16, f"ht{i}") for i in range(2)]
            hid = kb.alloc(32 * TW, BF16, "hid"); hid3 = hid.ap.rearrange("p (k n) -> p k n", k=32)
            rl = [kb.alloc(TW, F32, f"rl{i}") for i in range(3)]
            pb = [kb.pbank(i) for i in range(4)]
            pst = (kb.pbank(4), kb.pbank(5))
            xv = xview(src_x); dv = xview(dst_x)
            it = 0

            def prefetch(i):
                (ti, t0, n, d0) = tiles[i]
                xt = xts[i % 2]; ht = hts[i % 2]
                xt3 = xt.ap.rearrange("p (k n) -> p k n", k=8); ht3 = ht.ap.rearrange("p (k n) -> p k n", k=8)
                kb.load(xt, xt3[:, :, :n], xv[:, :, t0:t0 + n])
                return (xt, xt3, ht, ht3)

            def modul(i, bufs):
                (ti, t0, n, d0) = tiles[i]
                xt, xt3, ht, ht3 = bufs
                modulate_tile(L, 3, 4, xt3, ht3, n, 1 if ti == 0 else 0, xt, ht)
            cur = prefetch(0); modul(0, cur)
            for i, (ti, t0, n, d0) in enumerate(tiles):
                which = 1 if ti == 0 else 0
                xt, xt3, ht, ht3 = cur
                nxt = prefetch(i + 1) if i + 1 < len(tiles) else None
                for fc in range(32):
                    ps = pb[it % 4]; r = rl[it % 3]; it += 1
                    kb.mm(ps, ps.ap[:, :n], [(w13[:, k, fc * 128:(fc + 1) * 128], ht3[:, k, :n]) for k in range(8)], [w1b[fc // 4], ht])
                    kb.op("act", E("activation", out=r.ap[:, :n], in_=ps.ap[:, :n], func=AF.Relu), [ps], [r])
                    kb.op("dve", E("tensor_tensor", out=hid3[:, fc, :n], in0=r.ap[:, :n], in1=r.ap[:, :n], op=ALU.mult), [r], [hid])
                if nxt is not None:
                    modul(i + 1, nxt)
                kb.op("act", E("activation", out=xt3[:, :, :n], in_=xt3[:, :, :n], func=AF.Identity, scale=ALPHA), [xt, ht], [xt])
                for oc in range(8):
                    ps = pb[it % 4]; it += 1
                    kb.mm(ps, ps.ap[:, :n], [(w23[:, fc, oc * 128:(oc + 1) * 128], hid3[:, fc, :n]) for fc in range(32)], w2b + [hid])
                    kb.op("dve", E("scalar_tensor_tensor", out=xt3[:, oc, :n], in0=ps.ap[:, :n], scalar=MOD(L, 5, oc, which),
                                   in1=xt3[:, oc, :n], op0=ALU.mult, op1=ALU.add), [ps, xt, modv[L]], [xt])
                zbT = T(hid.ap[:, 0:8 * TW], "zb_alias"); zbT.b = hid.b
                scr = {"zb": zbT, "mean": rl[0], "rstd": rl[1], "tmp": rl[2]}
                ln_tile(xt, xt3, n, p + "ln2_g", p + "ln2_b", dv[:, :, d0:d0 + n], scr, pst)
                cur = nxt
            kb.reset(m0)

        def stage_mla():
            m0 = kb.mark()
            qT = kb.alloc(8 * NTOK, BF16, "qT"); qT3 = qT.ap.rearrange("p (h n) -> p h n", h=8)
            kT = kb.alloc(8 * NTOK, BF16, "kT"); kT3 = kT.ap.rearrange("p (h n) -> p h n", h=8)
            Va = kb.alloc(18 * 8 * 65, BF16, "Vaug"); Va4 = Va.ap.rearrange("p (c h e) -> p c h e", c=18, h=8)
            negM = kb.alloc(8, F32, "negM")
            qmx = kb.alloc(1, F32, "qmx"); kmx = kb.alloc(1, F32, "kmx")
            m1 = kb.mark()
            wq = kb.alloc(3 * 768, BF16, "wq"); wq3 = wq.ap.rearrange("p (k n) -> p k n", k=3)
            wr = kb.alloc(3 * 768, BF16, "wr"); wr3 = wr.ap.rearrange("p (k n) -> p k n", k=3)
            wk = kb.alloc(2 * 512, BF16, "wk"); wk3 = wk.ap.rearrange("p (k n) -> p k n", k=2)
            wv = kb.alloc(2 * 512, BF16, "wv"); wv3 = wv.ap.rearrange("p (k n) -> p k n", k=2)
            load_wbf(wq, wq3, W["w_uq"], 3, 768); load_wbf(wr, wr3, W["w_uq_rot"], 3, 768)
            load_wbf(wk, wk3, W["w_uk"], 2, 512); load_wbf(wv, wv3, W["w_uv"], 2, 512)
            rope = kb.alloc(2 * NLAT, F32, "rope"); rope3 = rope.ap.rearrange("p (a n) -> p a n", a=2)
            kb.load(rope, rope.ap, rope_in.rearrange("p a n -> p (a n)"))
            ind = kb.alloc(64, BF16, "ind"); ind3 = ind.ap.rearrange("p (a b) -> p a b", a=8)
            kb.load(ind, ind.ap, ind_in.rearrange("p a b -> p (a b)"), q="pool")
            kb.op("pool", E("memset", Va4[:, :, :, 64:65], 1.0), [], [Va])
            kb.op("pool", E("memset", qmx.ap[0:8, :], 0.0), [], [qmx])
            kb.op("pool", E("memset", kmx.ap[0:8, :], 0.0), [], [kmx])
            fq = kb.alloc(3 * 512, F32, "fq"); fq3 = fq.ap.rearrange("p (k n) -> p k n", k=3)
            fkv = kb.alloc(2 * 512, F32, "fkv"); fkv3 = fkv.ap.rearrange("p (k n) -> p k n", k=2)
            krp = kb.alloc(512, F32, "krp"); krr = kb.alloc(512, F32, "krr")
            sq = kb.alloc(3 * 512, F32, "sq"); sq3 = sq.ap.rearrange("p (k n) -> p k n", k=3)
            sd = kb.alloc(512, F32, "sd"); rs = kb.alloc(512, F32, "rs")
            qn = kb.alloc(3 * 512, BF16, "qn"); qn3 = qn.ap.rearrange("p (k n) -> p k n", k=3)
            ckv = kb.alloc(2 * 512, BF16, "ckv"); ckv3 = ckv.ap.rearrange("p (k n) -> p k n", k=2)
            t1 = [kb.alloc(512, F32, f"t1_{i}") for i in range(2)]
            t2 = [kb.alloc(512, F32, f"t2_{i}") for i in range(2)]
            krf = kb.alloc(512, F32, "krf")
            sqq = kb.alloc(8 * 512, BF16, "sqq"); sqq3 = sqq.ap.rearrange("p (h n) -> p h n", h=8)
            nmx = kb.alloc(1, F32, "nmx")
            pA = [kb.pbank(i) for i in range(3)]
            pR = [kb.pbank(3), kb.pbank(4)]
            pK = [kb.pbank(5), kb.pbank(6)]
            pN = kb.pbank(7)
            ia = 0; ir = 0; ik = 0
            for ti, (t0, n) in enumerate(TILES):
                lat = ti > 0
                kb.load(fq, fq3[:, :, :n], F0T[0:3, :, t0:t0 + n].rearrange("k p n -> p k n"))
                kb.load(fkv, fkv3[:, :, :n], F0T[3:5, :, t0:t0 + n].rearrange("k p n -> p k n"))
                kb.load(krp, krp.ap[64:96, :n], F0T[5, 64:96, t0:t0 + n])
                if lat:
                    kb.load(krr, krr.ap[64:96, :n], F0T[6, 64:96, t0:t0 + n])
                kb.op("act", E("activation", out=sq3[:, :, :n], in_=fq3[:, :, :n], func=AF.Square), [fq], [sq])
                kb.mm(pN, pN.ap[:, :n], [(CMF(2), sq3[:, k, :n]) for k in range(3)], [sq, cmf])
                kb.op("act", E("activation", out=sd.ap[:, :n], in_=pN.ap[:, :n], func=AF.Sqrt, bias=1e-6), [pN], [sd])
                kb.op("dve", E("reciprocal", out=rs.ap[:, :n], in_=sd.ap[:, :n]), [sd], [rs])
                for k in range(3):
                    kb.op("dve", E("scalar_tensor_tensor", out=qn3[:, k, :n], in0=fq3[:, k, :n], scalar=V("q_norm", k), in1=rs.ap[:, :n],
                                                                      op0=ALU.mult, op1=ALU.mult), [fq, rs, vecs], [qn])
                if "DBGT" in kb.dbg and ti == 0:
                    dt_ = kb.dram_t("DBGT", [4, 128, 3 * 512], F32)
                    kb.store(dt_[0], sq, sq.ap); kb.store(dt_[1, :, 0:512], sd, sd.ap); kb.store(dt_[2, :, 0:512], rs, rs.ap)
                    kb.store(dt_[3], fq, fq.ap)
                    dt2 = kb.dram_t("DBGT2", [128, 3 * 512], BF16)
                    kb.store(dt2[:, :], qn, qn.ap)
                kb.op("act", E("activation", out=sq3[:, 0:2, :n], in_=fkv3[:, :, :n], func=AF.Square), [fkv, qn], [sq])
                kb.mm(pN, pN.ap[:, :n], [(CMF(3), sq3[:, k, :n]) for k in range(2)], [sq, cmf])
                kb.op("act", E("activation", out=sd.ap[:, :n], in_=pN.ap[:, :n], func=AF.Sqrt, bias=1e-6), [pN], [sd])
                kb.op("dve", E("reciprocal", out=rs.ap[:, :n], in_=sd.ap[:, :n]), [sd], [rs])
                for k in range(2):
                    kb.op("dve", E("scalar_tensor_tensor", out=ckv3[:, k, :n], in0=fkv3[:, k, :n], scalar=V("kv_norm", k), in1=rs.ap[:, :n],
                                                                      op0=ALU.mult, op1=ALU.mult), [fkv, rs, vecs], [ckv])
                if lat:
                    c0 = t0 - NCTX
                    kb.op("dve", E("tensor_tensor", out=krf.ap[64:96, :n], in0=krp.ap[64:96, :n], in1=rope3[64:96, 0, c0:c0 + n], op=ALU.mult), [krp, rope], [krf])
                    kb.op("pool", E("tensor_tensor", out=krr.ap[64:96, :n], in0=krr.ap[64:96, :n], in1=rope3[64:96, 1, c0:c0 + n], op=ALU.mult), [krr, rope], [krr])
                    kb.op("pool", E("tensor_tensor", out=krf.ap[64:96, :n], in0=krf.ap[64:96, :n], in1=krr.ap[64:96, :n], op=ALU.add), [krf, krr], [krf])
                    ksrc = krf
                else:
                    ksrc = krp
                kb.op("act", E("copy", out=kT3[64:96, :, t0:t0 + n], in_=bc(ksrc.ap[64:96, :n].rearrange("p (o n) -> p o n", o=1), [32, 8, n])), [ksrc], [kT])
                for h in range(8):
                    pq = pA[ia % 3]; ia += 1
                    kb.mm(pq, pq.ap[0:96, :n], [(wq3[:, k, h * 96:(h + 1) * 96], qn3[:, k, :n]) for k in range(3)], [wq, qn])
                    kb.op("act", E("copy", out=qT3[0:64, h, t0:t0 + n], in_=pq.ap[0:64, :n]), [pq], [qT])
                    if lat:
                        pr = pR[ir % 2]; a1 = t1[ir % 2]; a2 = t2[ir % 2]; ir += 1
                        kb.mm(pr, pr.ap[0:96, :n], [(wr3[:, k, h * 96:(h + 1) * 96], qn3[:, k, :n]) for k in range(3)], [wr, qn])
                        kb.op("dve", E("tensor_tensor", out=a1.ap[64:96, :n], in0=pr.ap[64:96, :n], in1=rope3[64:96, 1, c0:c0 + n], op=ALU.mult), [pr, rope], [a1])
                        kb.op("dve", E("tensor_tensor", out=a2.ap[64:96, :n], in0=pq.ap[64:96, :n], in1=rope3[64:96, 0, c0:c0 + n], op=ALU.mult), [pq, rope], [a2])
                        kb.op("pool", E("tensor_tensor", out=qT3[64:96, h, t0:t0 + n], in0=a1.ap[64:96, :n], in1=a2.ap[64:96, :n], op=ALU.add), [a1, a2], [qT])
                    else:
                        kb.op("act", E("copy", out=qT3[64:96, h, t0:t0 + n], in_=pq.ap[64:96, :n]), [pq], [qT])
                    pk = pK[ik % 2]; ik += 1
                    kb.mm(pk, pk.ap[0:64, :n], [(wk3[:, k, h * 64:(h + 1) * 64], ckv3[:, k, :n]) for k in range(2)], [wk, ckv])
                    kb.op("dve", E("tensor_copy", out=kT3[0:64, h, t0:t0 + n], in_=pk.ap[0:64, :n]), [pk], [kT])
                for j in range(n // 128):
                    pk = pK[ik % 2]; ik += 1
                    kc = (t0 + j * 128) // 128
                    kb.mm(pk, pk.ap[:, :], [(ckv3[:, k, j * 128:(j + 1) * 128], wv3[:, k, :]) for k in range(2)], [wv, ckv])
                    kb.op("act", E("copy", out=Va4[:, kc, :, 0:64], in_=pk.ap.rearrange("p (h e) -> p h e", h=8)), [pk], [Va])
                for (src3, src, mx) in ((qT3, qT, qmx), (kT3, kT, kmx)):
                    kb.op("act", E("activation", out=sqq3[0:96, :, :n], in_=src3[0:96, :, t0:t0 + n], func=AF.Square), [src], [sqq])
                    pk = pK[ik % 2]; ik += 1
                    kb.mm(pk, pk.ap[0:8, :n], [(ind3[0:96, h, :], sqq3[0:96, h, :n]) for h in range(8)], [ind, sqq])
                    kb.op("dve", E("tensor_reduce", out=nmx.ap[0:8, :], in_=pk.ap[0:8, :n], axis=AX.X, op=ALU.max), [pk], [nmx])
                    kb.op("dve", E("tensor_tensor", out=mx.ap[0:8, :], in0=mx.ap[0:8, :], in1=nmx.ap[0:8, :], op=ALU.max), [mx, nmx], [mx])
            dg = kb.alloc(8, F32, "dg")
            kb.op("dve", E("tensor_tensor", out=nmx.ap[0:8, :], in0=qmx.ap[0:8, :], in1=kmx.ap[0:8, :], op=ALU.mult), [qmx, kmx], [nmx])
            kb.op("act", E("activation", out=nmx.ap[0:8, :], in_=nmx.ap[0:8, :], func=AF.Sqrt), [nmx], [nmx])
            kb.op("dve", E("tensor_scalar", out=dg.ap[0:8, :], in0=CMF(0)[0:8, 0:8], scalar1=nmx.ap[0:8, 0:1], scalar2=-1.03 * SCALE, op0=ALU.mult, op1=ALU.mult),
                  [nmx, cmf], [dg])
            pk = pK[ik % 2]; ik += 1
            kb.mm(pk, pk.ap[:, 0:8], [(CMF(7)[0:8, :], dg.ap[0:8, :])], [cmf, dg])
            kb.op("dve", E("tensor_copy", out=negM.ap, in_=pk.ap[:, 0:8]), [pk], [negM])
            if "DBGQK" in kb.dbg:
                dq = kb.dram_t("DBGQK", [2, 128, 8 * NTOK], BF16)
                kb.store(dq[0], qT, qT.ap); kb.store(dq[1], kT, kT.ap)
                dv_ = kb.dram_t("DBGV", [128, 18 * 8 * 65], BF16)
                kb.store(dv_[:, :], Va, Va.ap)
                dn = kb.dram_t("DBGNM", [128, 8], F32)
                kb.store(dn[:, :], negM, negM.ap)
            kb.reset(m1)
            PT = [kb.alloc(512, BF16, f"PT{i}") for i in range(4)]
            osb = [kb.alloc(512, F32, f"osb{i}") for i in range(2)]
            rec = [kb.alloc(512, F32, f"rec{i}") for i in range(2)]
            att = [kb.alloc(512, BF16, f"att{i}") for i in range(2)]
            pS = [kb.pbank(i) for i in range(4)]
            pO = [kb.pbank(4), kb.pbank(5)]
            pB = [kb.pbank(6), kb.pbank(7)]
            isc = 0; io = 0
            pending = [None]

            def epilogue(po, ob, rc, ao, pb_, h, t0, n):
                kb.op("dve", E("tensor_copy", out=ob.ap[0:65, :n], in_=po.ap[0:65, :n]), [po], [ob])
                kb.mm(pb_, pb_.ap[0:64, :n], [(CMF(6)[0:65, 0:64], ob.ap[0:65, :n])], [cmf, ob])
                kb.op("dve", E("reciprocal", out=rc.ap[0:64, :n], in_=pb_.ap[0:64, :n]), [pb_], [rc])
                kb.op("dve", E("tensor_tensor", out=ao.ap[0:64, :n], in0=ob.ap[0:64, :n], in1=rc.ap[0:64, :n], op=ALU.mult), [ob, rc], [ao])
                kb.store(mixT[h * 64:(h + 1) * 64, t0:t0 + n], ao, ao.ap[0:64, :n])
            for h in range(8):
                for ti, (t0, n) in enumerate(TILES):
                    nk = 2 if ti == 0 else 18
                    po = pO[io % 2]; ob = osb[io % 2]; rc = rec[io % 2]; ao = att[io % 2]; pb_ = pB[io % 2]; io += 1
                    slots = {}
                    for i in range(nk + 2):
                        if i < nk:
                            ps = pS[isc % 4]; pt = PT[isc % 4]; isc += 1
                            slots[i] = (ps, pt)
                            kb.mm1(ps, ps.ap[:, :n], kT3[0:96, h, i * 128:(i + 1) * 128], qT3[0:96, h, t0:t0 + n], [kT, qT])
                            kb.op("act", E("activation", out=pt.ap[:, :n], in_=ps.ap[:, :n], func=AF.Exp, scale=SCALE, bias=negM.ap[:, h:h + 1]),
                                  [ps, negM], [pt])
                        if i == 2 and pending[0] is not None:
                            epilogue(*pending[0]); pending[0] = None
                        if i >= 2:
                            j = i - 2
                            ps, pt = slots.pop(j)
                            kb.mm1(po, po.ap[0:65, :n], Va4[:, j, h, :], pt.ap[:, :n], [Va, pt], start=(j == 0), stop=(j == nk - 1), sig=(j == nk - 1))
                    if pending[0] is not None:
                        epilogue(*pending[0])
                    pending[0] = (po, ob, rc, ao, pb_, h, t0, n)
            epilogue(*pending[0])
            kb.reset(m0)


        def stage_lru():
            m0 = kb.mark()
            gaw = kb.alloc(2048, BF16, "gaw"); gaw4 = gaw.ap.rearrange("p (d k e) -> p d k e", d=2, k=8)
            gxw = kb.alloc(2048, BF16, "gxw"); gxw4 = gxw.ap.rearrange("p (d k e) -> p d k e", d=2, k=8)
            kb.load(gaw, gaw.ap, W["ga_w"].rearrange("p d k e -> p (d k e)"), q="pool")
            kb.load(gxw, gxw.ap, W["gx_w"].rearrange("p d k e -> p (d k e)"), q="pool")
            cl = kb.alloc(16, F32, "cl")
            kb.op("act", E("activation", out=cl.ap, in_=V("lam", 0, 16), func=AF.Exp, scale=-1.0), [vecs], [cl])
            kb.op("act", E("activation", out=cl.ap, in_=cl.ap, func=AF.Ln, bias=1.0), [cl], [cl])
            kb.op("dve", E("tensor_scalar", out=cl.ap, in0=cl.ap, scalar1=-8.0, scalar2=None, op0=ALU.mult), [cl], [cl])
            XW = 2310
            xh = kb.alloc(XW, F32, "xh")
            kb.op("pool", E("memset", xh.ap, 0.0), [], [xh])
            xc = kb.alloc(NTOK, F32, "xc"); xcb = kb.alloc(NTOK, BF16, "xcb")
            rr_ = [kb.alloc(NTOK, F32, f"rr{i}") for i in range(2)]; ig_ = [kb.alloc(NTOK, F32, f"ig{i}") for i in range(2)]
            aa_ = [kb.alloc(NTOK, F32, "aa0")] * 2; a2_ = [kb.alloc(NTOK, F32, "a20")] * 2
            uu_ = [kb.alloc(NTOK, F32, "uu0")] * 2
            xh2 = kb.alloc(XW, F32, "xh2")
            kb.op("pool", E("memset", xh2.ap, 0.0), [], [xh2])
            xhs = [xh, xh2]
            dgs = [kb.alloc(512, F32, f"dg{i}") for i in range(2)]
            xc2 = kb.alloc(NTOK, F32, "xc2"); xcb2 = kb.alloc(NTOK, BF16, "xcb2")
            xcs = [xc, xc2]; xcbs = [xcb, xcb2]
            hd = [kb.alloc(NTOK, F32, f"hd{d}") for d in range(2)]
            gg = kb.alloc(NLAT, F32, "gg"); yy = kb.alloc(NLAT, BF16, "yy")
            pb = [kb.pbank(i) for i in range(4)]
            ip = 0
            segs = [(0, 0, 256), (256, 259, 2048)]
            for c in range(8):
                xh = xhs[c % 2]; xc = xcs[c % 2]; xcb = xcbs[c % 2]
                kb.load(xh, xh.ap[:, 2:258], G1T[8 + c, :, 0:256])
                kb.load(xh, xh.ap[:, 261:2309], G1T[8 + c, :, 256:NTOK])
                kb.load(gg, gg.ap, G1T[c, :, 256:NTOK])
                dg = dgs[c % 2]; dg3 = dg.ap.rearrange("p (j m) -> p j m", j=4)
                for j in range(4):
                    kb.op("dve", E("tensor_scalar", out=dg3[:, j, :], in0=CMF(0), scalar1=V("conv_w", j * 8 + c), scalar2=None, op0=ALU.mult), [cmf, vecs], [dg])
                for (t0, n) in TILES:
                    b0 = 0 if t0 == 0 else 259 + (t0 - 256)
                    ps = pb[ip % 4]; ip += 1
                    kb.mm(ps, ps.ap[:, :n], [(dg3[:, j, :], xh.ap[:, b0 + j:b0 + j + n]) for j in range(4)], [dg, xh])
                    kb.op("act", E("activation", out=xc.ap[:, t0:t0 + n], in_=ps.ap[:, :n], func=AF.Identity, bias=V("conv_b", c)), [ps, vecs], [xc])
                kb.op("act", E("copy", out=xcb.ap, in_=xc.ap), [xc], [xcb])
                if "DBGL1" in kb.dbg:
                    kb.store(kb.dram["DBGL1"][0, c], xc, xc.ap)
                for d in range(2):
                    rr = rr_[d]; ig = ig_[d]; aa = aa_[d]; a2 = a2_[d]; uu = uu_[d]
                    for (t0, n) in TILES:
                        ps = pb[ip % 4]; ip += 1
                        kb.mm(ps, ps.ap[:, :n], [(gaw4[:, d, c, :], xcb.ap[:, t0:t0 + n])], [gaw, xcb])
                        kb.op("act", E("activation", out=rr.ap[:, t0:t0 + n], in_=ps.ap[:, :n], func=AF.Sigmoid, bias=V("ga_b", d * 8 + c)), [ps, vecs], [rr])
                        ps = pb[ip % 4]; ip += 1
                        kb.mm(ps, ps.ap[:, :n], [(gxw4[:, d, c, :], xcb.ap[:, t0:t0 + n])], [gxw, xcb])
                        kb.op("act", E("activation", out=ig.ap[:, t0:t0 + n], in_=ps.ap[:, :n], func=AF.Sigmoid, bias=V("gx_b", d * 8 + c)), [ps, vecs], [ig])
                    kb.op("act", E("activation", out=aa.ap, in_=rr.ap, func=AF.Exp, scale=cl.ap[:, d * 8 + c:d * 8 + c + 1]), [rr, cl], [aa])
                    kb.op("act", E("activation", out=a2.ap, in_=aa.ap, func=AF.Square), [aa], [a2])
                    kb.op("act", E("activation", out=a2.ap, in_=a2.ap, func=AF.Sqrt, scale=-1.0, bias=1.0), [a2], [a2])
                    kb.op("dve", E("tensor_tensor", out=uu.ap, in0=ig.ap, in1=xc.ap, op=ALU.mult), [ig, xc], [uu])
                    kb.op("dve", E("tensor_tensor", out=uu.ap, in0=uu.ap, in1=a2.ap, op=ALU.mult), [uu, a2], [uu])
                    h_ = hd[d]
                    if d == 0:
                        kb.op("dve", E("tensor_tensor_scan", out=h_.ap, data0=aa.ap, data1=uu.ap, initial=0.0, op0=ALU.mult, op1=ALU.add), [aa, uu], [h_])
                    else:
                        kb.op("dve", E("tensor_tensor_scan", out=h_.ap[:, 0:256][:, ::-1], data0=aa.ap[:, 0:256][:, ::-1], data1=uu.ap[:, 0:256][:, ::-1],
                                       initial=0.0, op0=ALU.mult, op1=ALU.add), [aa, uu], [h_])
                        kb.op("dve", E("tensor_tensor_scan", out=h_.ap[:, 256:NTOK][:, ::-1], data0=aa.ap[:, 256:NTOK][:, ::-1], data1=uu.ap[:, 256:NTOK][:, ::-1],
                                       initial=h_.ap[:, 0:1], op0=ALU.mult, op1=ALU.add), [aa, uu, h_], [h_])
                    if "DBGL1" in kb.dbg:
                        kb.store(kb.dram["DBGL1"][1 + d, c], h_, h_.ap)
                kb.op("dve", E("tensor_tensor", out=hd[0].ap[:, 256:NTOK], in0=hd[0].ap[:, 256:NTOK], in1=hd[1].ap[:, 256:NTOK], op=ALU.add), [hd[0], hd[1]], [hd[0]])
                kb.op("dve", E("tensor_tensor", out=yy.ap, in0=hd[0].ap[:, 256:NTOK], in1=gg.ap, op=ALU.mult), [hd[0], gg], [yy])
                kb.store(mixT[c * 128:(c + 1) * 128, 256:NTOK], yy, yy.ap)
            kb.reset(m0)


        def stage_rwkv():
            m0 = kb.mark()
            W_ = 2312
            ORDER = [list(range(NCH)), [1, 0] + list(range(NCH - 1, 1, -1))]
            import os
            SKIP_RWA = bool(os.environ.get("ONLY_RWB"))
            X = [kb.alloc(W_, F32, f"X{i}") for i in range(5)]
            tw = kb.alloc(NTOK, F32, "tw"); alr = kb.alloc(NTOK, F32, "alr"); sg = kb.alloc(NTOK, F32, "sg")
            rS = kb.alloc(NTOK, F32, "rS"); kS = kb.alloc(NTOK, F32, "kS"); vS = kb.alloc(NTOK, F32, "vS")
            kk = kb.alloc(NTOK, F32, "kk"); pbn = kb.alloc(NTOK, F32, "pbn")
            msk = kb.alloc(NTOK + 1, F32, "msk")
            ob = [kb.alloc(NTOK, F32, f"ob{i}") for i in range(3)]
            stg = [kb.alloc(512, F32, f"stg{i}") for i in range(2)]
            mud = kb.alloc(30, F32, "mud"); oka = kb.alloc(4, F32, "oka"); glt = kb.alloc(NCH, F32, "glt")
            w2p = kb.alloc(1024, F32, "w2p"); a2p = kb.alloc(1024, F32, "a2p"); g2t = kb.alloc(512, F32, "g2t")
            w2p3 = w2p.ap.rearrange("p (d n) -> p d n", d=2); a2p3 = a2p.ap.rearrange("p (d n) -> p d n", d=2)
            kb.load(w2p, w2p.ap, W["w2pad"].rearrange("p d n -> p (d n)"))
            kb.load(a2p, a2p.ap, W["a2pad"].rearrange("p d n -> p (d n)"))
            kb.load(g2t, g2t.ap, W["g2"][:, :])
            pb = [kb.pbank(i) for i in range(4)]
            ipb = [0]

            def nps():
                p_ = pb[ipb[0] % 4]; ipb[0] += 1
                return p_
            kb.op("dve", E("tensor_scalar", out=mud.ap[:, 0:15], in0=V("mu", 0, 15), scalar1=-1.0, scalar2=1.0, op0=ALU.mult, op1=ALU.add), [vecs], [mud])
            kb.op("dve", E("tensor_scalar", out=mud.ap[:, 15:30], in0=V("mu", 0, 15), scalar1=0.5, scalar2=None, op0=ALU.mult), [vecs, mud], [mud])
            kb.op("dve", E("tensor_scalar", out=oka.ap, in0=V("k_a", 0, 4), scalar1=-1.0, scalar2=1.0, op0=ALU.mult, op1=ALU.add), [vecs], [oka])
            kb.op("pool", E("memset", msk.ap, 1.0), [], [msk])
            kb.op("pool", E("memset", msk.ap[:, 0:NTOK + 1:128], 0.0), [msk], [msk])
            kb.op("pool", E("memset", X[1].ap, 0.0), [], [X[1]])
            maskf = msk.ap[:, 0:NTOK]; maskr = msk.ap[:, 1:NTOK + 1]

            def shift(j, dst, func=None):
                fh, ss, uu = X[1], X[2], X[3]
                kb.load(fh, fh.ap[:, 1:257], F0T[7 + j, :, 0:256])
                kb.load(fh, fh.ap[:, 258:2306], F0T[7 + j, :, 256:NTOK])
                kb.op("dve", E("tensor_tensor", out=ss.ap[:, 0:256], in0=fh.ap[:, 0:256], in1=fh.ap[:, 2:258], op=ALU.add), [fh], [ss])
                kb.op("dve", E("tensor_tensor", out=ss.ap[:, 256:NTOK], in0=fh.ap[:, 257:2305], in1=fh.ap[:, 259:2307], op=ALU.add), [fh, ss], [ss])
                kb.op("act", E("activation", out=uu.ap[:, 0:256], in_=fh.ap[:, 1:257], func=AF.Identity, scale=mud.ap[:, j:j + 1]), [fh, mud], [uu])
                kb.op("act", E("activation", out=uu.ap[:, 256:NTOK], in_=fh.ap[:, 258:2306], func=AF.Identity, scale=mud.ap[:, j:j + 1]), [fh, mud, uu], [uu])
                kb.op("dve", E("scalar_tensor_tensor", out=dst.ap[:, 0:NTOK], in0=ss.ap[:, 0:NTOK], scalar=mud.ap[:, 15 + j:16 + j], in1=uu.ap[:, 0:NTOK],
                               op0=ALU.mult, op1=ALU.add), [ss, uu, mud], [dst])
                if func is not None:
                    kb.op("act", E("activation", out=dst.ap[:, 0:NTOK], in_=dst.ap[:, 0:NTOK], func=func), [dst], [dst])
            if not SKIP_RWA:
                shift(12, tw, AF.Tanh); shift(13, alr); shift(14, sg, AF.Sigmoid)
            RWU7 = [[RWU[hp, d].rearrange("c p (i t) -> p c i t", i=7) for d in range(2)] for hp in range(4)]
            for hp in range(0 if SKIP_RWA else 4):
                shift(hp, rS); shift(4 + hp, kS); shift(8 + hp, vS)
                for d in range(2):
                    kb.store(RWU7[hp][d][:, :, 6, :], vS, vS.ap.rearrange("p (c t) -> p c t", t=128))
                sq = X[4]
                kb.op("act", E("activation", out=kk.ap, in_=kS.ap, func=AF.Identity, scale=V("k_k", hp)), [kS, vecs], [kk])
                kb.op("act", E("activation", out=sq.ap[:, 0:NTOK], in_=kk.ap, func=AF.Square), [kk], [sq])
                for (t0, n) in TILES:
                    ps = nps()
                    kb.mm(ps, ps.ap[:, :n], [(CMF(4), sq.ap[:, t0:t0 + n])], [cmf, sq])
                    kb.op("act", E("activation", out=X[3].ap[:, t0:t0 + n], in_=ps.ap[:, :n], func=AF.Sqrt, bias=1e-12), [ps], [X[3]])
                kb.op("dve", E("reciprocal", out=X[3].ap[:, 0:NTOK], in_=X[3].ap[:, 0:NTOK]), [X[3]], [X[3]])
                kb.op("dve", E("tensor_tensor", out=kk.ap, in0=kk.ap, in1=X[3].ap[:, 0:NTOK], op=ALU.mult), [kk, X[3]], [kk])
                if "DBGRW" in kb.dbg:
                    kb.store(kb.dram["DBGRW"][0, hp], kk, kk.ap)
                iob = 0
                for d in range(2):
                    x1, x2, x3, x4, x5 = [X[i] for i in range(5)]
                    x1a = x1.ap[:, 0:NTOK]; x2a = x2.ap[:, 0:NTOK]; x3a = x3.ap[:, 0:NTOK]; x4a = x4.ap[:, 0:NTOK]; x5a = x5.ap[:, 0:NTOK]
                    for (t0, n) in TILES:
                        ps = nps()
                        kb.mm(ps, ps.ap[:, :n], [(w2p3[:, d, hp * 128:(hp + 1) * 128], tw.ap[:, t0:t0 + n])], [w2p, tw])
                        kb.op("act", E("activation", out=x1.ap[:, t0:t0 + n], in_=ps.ap[:, :n], func=AF.Sigmoid, bias=V("w0", d * 4 + hp)), [ps, vecs], [x1])
                        ps = nps()
                        kb.mm(ps, ps.ap[:, :n], [(a2p3[:, d, hp * 128:(hp + 1) * 128], alr.ap[:, t0:t0 + n])], [a2p, alr])
                        kb.op("act", E("activation", out=x2.ap[:, t0:t0 + n], in_=ps.ap[:, :n], func=AF.Sigmoid, bias=V("a0", d * 4 + hp)), [ps, vecs], [x2])
                    if "DBGRW" in kb.dbg:
                        kb.store(kb.dram["DBGRW"][1 + d, hp], x1, x1a)
                        kb.store(kb.dram["DBGRW"][3 + d, hp], x2, x2a)
                    kb.op("dve", E("tensor_scalar", out=x3a, in0=x2a, scalar1=V("k_a", hp), scalar2=oka.ap[:, hp:hp + 1], op0=ALU.mult, op1=ALU.add), [x2, vecs, oka], [x3])
                    kb.op("dve", E("tensor_tensor", out=x3a, in0=x3a, in1=kS.ap, op=ALU.mult), [x3, kS], [x3])
                    kb.op("dve", E("tensor_tensor", out=x2a, in0=x2a, in1=kk.ap, op=ALU.mult), [x2, kk], [x2])
                    if d == 0:
                        kb.op("dve", E("tensor_tensor_scan", out=x4a, data0=maskf, data1=x1a, initial=0.0, op0=ALU.mult, op1=ALU.add), [msk, x1], [x4])
                    else:
                        kb.op("dve", E("tensor_tensor_scan", out=x4a[:, ::-1], data0=maskr[:, ::-1], data1=x1a[:, ::-1], initial=0.0, op0=ALU.mult, op1=ALU.add), [msk, x1], [x4])
                    kb.op("dve", E("tensor_tensor", out=x1a, in0=x4a, in1=x1a, op=ALU.subtract), [x4, x1], [x1])
                    kb.op("act", E("activation", out=x5a, in_=x1a, func=AF.Exp, scale=-CDEC), [x1], [x5])
                    o = ob[iob % 3]; iob += 1
                    kb.op("dve", E("tensor_tensor", out=o.ap, in0=kk.ap, in1=x5a, op=ALU.mult), [kk, x5], [o])
                    kb.store(RWU7[hp][d][:, :, 0, :], o, o.ap.rearrange("p (c t) -> p c t", t=128), q="pool")
                    kb.op("act", E("activation", out=x5a, in_=x4a, func=AF.Exp, scale=-CDEC), [x4, o], [x5])
                    o = ob[iob % 3]; iob += 1
                    kb.op("dve", E("tensor_tensor", out=o.ap, in0=rS.ap, in1=x5a, op=ALU.mult), [rS, x5], [o])
                    kb.store(RWU7[hp][d][:, :, 1, :], o, o.ap.rearrange("p (c t) -> p c t", t=128))
                    e1v = x5a.rearrange("p (c t) -> p c t", t=128)
                    kb.op("dve", E("tensor_copy", out=glt.ap, in_=(e1v[:, :, 127] if d == 0 else e1v[:, :, 0])), [x5], [glt])
                    kb.store(RWGL[hp, d], glt, glt.ap)
                    kb.op("act", E("activation", out=x1a, in_=x4a, func=AF.Exp, scale=CDEC), [x4, x1], [x1])
                    o = ob[iob % 3]; iob += 1
                    kb.op("dve", E("tensor_tensor", out=o.ap, in0=x3a, in1=x1a, op=ALU.mult), [x3, x1], [o])
                    kb.store(RWU7[hp][d][:, :, 2, :], o, o.ap.rearrange("p (c t) -> p c t", t=128), q="pool")
                    o = ob[iob % 3]; iob += 1
                    kb.op("dve", E("tensor_tensor", out=o.ap, in0=x2a, in1=x1a, op=ALU.mult), [x2, x1], [o])
                    kb.store(RWU7[hp][d][:, :, 3, :], o, o.ap.rearrange("p (c t) -> p c t", t=128))
                    x1v = x1a.rearrange("p (c t) -> p c t", t=128)
                    kb.op("dve", E("tensor_tensor", out=x1v, in0=x1v, in1=bc(glt.ap.rearrange("p (c o) -> p c o", o=1), [128, NCH, 128]), op=ALU.mult), [x1, glt], [x1])
                    o = ob[iob % 3]; iob += 1
                    kb.op("pool", E("tensor_tensor", out=o.ap, in0=x3a, in1=x1a, op=ALU.mult), [x3, x1], [o])
                    kb.store(RWU7[hp][d][:, :, 4, :], o, o.ap.rearrange("p (c t) -> p c t", t=128), q="pool")
                    o = ob[iob % 3]; iob += 1
                    kb.op("dve", E("tensor_tensor", out=o.ap, in0=x2a, in1=x1a, op=ALU.mult), [x2, x1], [o])
                    kb.store(RWU7[hp][d][:, :, 5, :], o, o.ap.rearrange("p (c t) -> p c t", t=128))
                    if d == 0:
                        kb.op("dve", E("scalar_tensor_tensor", out=pbn.ap, in0=rS.ap, scalar=V("r_k", hp), in1=x3a, op0=ALU.mult, op1=ALU.mult), [rS, x3, vecs], [pbn])
                    else:
                        kb.op("dve", E("scalar_tensor_tensor", out=x4a, in0=rS.ap, scalar=V("r_k", hp), in1=x3a, op0=ALU.mult, op1=ALU.mult), [rS, x3, vecs, x4], [x4])
                        kb.op("dve", E("tensor_tensor", out=pbn.ap, in0=pbn.ap, in1=x4a, op=ALU.add), [pbn, x4], [pbn])
                ist = 0
                for (t0, n) in TILES:
                    ps = nps(); sgb = stg[ist % 2]; ist += 1
                    kb.mm(ps, ps.ap[:, :n], [(CMF(4), pbn.ap[:, t0:t0 + n])], [cmf, pbn])
                    kb.op("dve", E("tensor_tensor", out=sgb.ap[:, :n], in0=ps.ap[:, :n], in1=vS.ap[:, t0:t0 + n], op=ALU.mult), [ps, vS], [sgb])
                    kb.store(RWBON[hp, :, t0:t0 + n], sgb, sgb.ap[:, :n])
                    ps = nps(); sgb = stg[ist % 2]; ist += 1
                    kb.mm(ps, ps.ap[:, :n], [(g2t.ap[:, hp * 128:(hp + 1) * 128], sg.ap[:, t0:t0 + n])], [g2t, sg])
                    kb.op("act", E("copy", out=sgb.ap[:, :n], in_=ps.ap[:, :n]), [ps], [sgb])
                    kb.store(RWG[hp, :, t0:t0 + n], sgb, sgb.ap[:, :n])
            kb.reset(m0)
            if stop_after == "rwa":
                return
            import os
            yT = [kb.alloc(NTOK, F32, f"yT{hp}") for hp in range(4)]
            m1 = kb.mark()
            NU = 4
            mk = kb.alloc(2 * 1280, F32, "mk"); mk3 = mk.ap.rearrange("p (d m) -> p d m", d=2)
            kb.load(mk, mk.ap, msk_in.rearrange("p d m -> p (d m)"))
            glall = kb.alloc(4 * 2 * NCH, F32, "glall"); gl4 = glall.ap.rearrange("p (a d c) -> p a d c", a=4, d=2)
            kb.load(glall, gl4, RWGL.rearrange("a d p c -> p a d c"))
            U7 = [[kb.alloc(896, F32, f"U7_{p}_{u}") for u in range(NU)] for p in range(2)]
            PD = [[kb.alloc(768, F32, f"PD_{p}_{u}") for u in range(NU)] for p in range(2)]
            for p in range(2):
                for u in range(NU):
                    kb.op("pool", E("memset", PD[p][u].ap, 0.0), [], [PD[p][u]])
            KBV = [kb.alloc(384, F32, f"KBV{u}") for u in range(NU)]
            VP = [kb.alloc(256, F32, f"VP{u}") for u in range(NU)]
            UP = [kb.alloc(256, F32, f"UP{u}") for u in range(NU)]
            BCt = [kb.alloc(512, F32, f"BC{u}") for u in range(NU)]
            ZDt = [kb.alloc(512, F32, f"ZD{u}") for u in range(NU)]
            X0T = [kb.alloc(256, F32, f"X0T{u}") for u in range(NU)]
            XX = [[kb.alloc(512, F32, f"XX{q}{u}") for u in range(NU)] for q in range(2)]
            RR = [[kb.alloc(256, F32, f"RR{q}{u}") for u in range(NU)] for q in range(2)]
            XIN = [kb.alloc(128, F32, f"XIN{u}") for u in range(NU)]
            UNt = [kb.alloc(128, F32, f"UN{u}") for u in range(NU)]
            Sst = [[kb.alloc(128, F32, f"S{p}{u}") for u in range(NU)] for p in range(2)]
            TMPS = [kb.alloc(128, F32, f"tmpS{u}") for u in range(NU)]
            for u in range(NU):
                kb.op("pool", E("memset", VP[u].ap, 0.0), [], [VP[u]])
                kb.op("pool", E("memset", UP[u].ap, 0.0), [], [UP[u]])
            bks = [kb.pbank(i) for i in range(8)]
            ib = [0]

            def nb():
                b_ = bks[ib[0] % 8]; ib[0] += 1
                return b_
            NST = int(os.environ.get('RWB_STEPS', NCH))
            ident2 = bc(CMF(0).rearrange("p (o n) -> p o n", o=1), [128, 2, 128])
            gstep = 0
            for d in range(2):
                for u in range(NU):
                    kb.op("pool", E("memset", Sst[gstep % 2][u].ap, 0.0), [], [Sst[gstep % 2][u]])

                def loads(s_, p):
                    for u in range(NU):
                        hp = u
                        c = ORDER[d][s_]
                        src = RWU[hp, d, c]
                        kb.load(U7[p][u], U7[p][u].ap, src[:, :])
                        pd4 = PD[p][u].ap.rearrange("p (w a t) -> p w a t", w=3, a=2)
                        for w, slot in enumerate((2, 3, 0)):
                            kb.load(PD[p][u], pd4[0:64, w, 0, :], src[0:64, slot * 128:(slot + 1) * 128])
                            kb.load(PD[p][u], pd4[64:128, w, 1, :], src[64:128, slot * 128:(slot + 1) * 128])
                loads(0, gstep % 2)
                for s_ in range(NST):
                    p = gstep % 2; po = 1 - p
                    if s_ + 1 < NST:
                        loads(s_ + 1, po)
                    c = ORDER[d][s_]
                    for u in range(NU):
                        u7 = U7[p][u]; pd4 = PD[p][u].ap.rearrange("p (w a t) -> p w a t", w=3, a=2)
                        bT = nb()
                        for j, slot in enumerate((4, 5, 6)):
                            kb.S.op("pe", E("transpose", out=bT.ap[:, j * 128:(j + 1) * 128], in_=u7.ap[:, slot * 128:(slot + 1) * 128], identity=CMF(0)),
                                    reads=[u7, cmf], writes=[bT], sig=(j == 2))
                        kb.op("act", E("copy", out=KBV[u].ap, in_=bT.ap[:, 0:384]), [bT], [KBV[u]])
                        vp64 = VP[u].ap.rearrange("p (a c) -> p a c", c=64)
                        kb.op("pool", E("tensor_copy", out=vp64[:, 0:4:3, :], in_=KBV[u].ap[:, 256:384].rearrange("p (a c) -> p a c", a=2)), [KBV[u]], [VP[u]])
                        rk_ = u7.ap[:, 0:256]
                        b1 = nb()
                        for a_ in range(2):
                            kb.mm1(b1, b1.ap[:, a_ * 256:(a_ + 1) * 256], pd4[:, 0, a_, :], rk_, [PD[p][u], u7])
                        kb.op("dve", E("tensor_tensor", out=BCt[u].ap, in0=b1.ap, in1=mk3[:, d, 0:512], op=ALU.mult), [b1, mk], [BCt[u]])
                        b1 = nb()
                        for a_ in range(2):
                            kb.mm1(b1, b1.ap[:, a_ * 256:(a_ + 1) * 256], pd4[:, 1, a_, :], rk_, [PD[p][u], u7])
                        kb.op("dve", E("tensor_tensor", out=ZDt[u].ap, in0=b1.ap, in1=mk3[:, d, 512:1024], op=ALU.mult), [b1, mk], [ZDt[u]])
                        zd3 = ZDt[u].ap.rearrange("p (a m) -> p a m", a=2)
                        kb.op("pool", E("tensor_tensor", out=RR[0][u].ap.rearrange("p (a m) -> p a m", a=2), in0=zd3[:, :, 0:128], in1=ident2, op=ALU.add), [ZDt[u], cmf], [RR[0][u]])
                        b1 = nb()
                        for a_ in range(2):
                            kb.mm1(b1, b1.ap[:, a_ * 128:(a_ + 1) * 128], pd4[:, 2, a_, :], u7.ap[:, 384:512], [PD[p][u], u7])
                        kb.op("dve", E("tensor_tensor", out=X0T[u].ap, in0=b1.ap[:, 0:256], in1=mk3[:, d, 1024:1280], op=ALU.mult), [b1, mk], [X0T[u]])
                    for lvl in range(6):
                        q0 = lvl % 2; q1 = 1 - q0
                        hs = {}
                        for u in range(NU):
                            b1 = nb(); hs[u] = b1
                            for a_ in range(2):
                                if lvl == 0:
                                    Xk = ZDt[u].ap[:, a_ * 256:a_ * 256 + 128]; XTk = X0T[u].ap[:, a_ * 128:(a_ + 1) * 128]; rd = [ZDt[u], X0T[u]]
                                else:
                                    Xk = XX[q0][u].ap[:, a_ * 256:a_ * 256 + 128]; XTk = XX[q0][u].ap[:, a_ * 256 + 128:a_ * 256 + 256]; rd = [XX[q0][u]]
                                if lvl < 5:
                                    kb.mm1(b1, b1.ap[:, a_ * 256:a_ * 256 + 128], XTk, Xk, rd)
                                kb.mm1(b1, b1.ap[:, a_ * 256 + 128:a_ * 256 + 256], Xk, XTk, rd)
                        for u in range(NU):
                            b1 = hs[u]
                            if lvl < 5:
                                kb.op("act", E("copy", out=XX[q1][u].ap, in_=b1.ap), [b1], [XX[q1][u]])
                            else:
                                kb.op("act", E("copy", out=XX[q1][u].ap.rearrange("p (a m) -> p a m", a=2)[:, :, 128:256],
                                               in_=b1.ap.rearrange("p (a m) -> p a m", a=2)[:, :, 128:256]), [b1], [XX[q1][u]])
                        for u in range(NU):
                            b1 = nb(); hs[u] = b1
                            for a_ in range(2):
                                kb.mm1(b1, b1.ap[:, a_ * 128:(a_ + 1) * 128], XX[q1][u].ap[:, a_ * 256 + 128:a_ * 256 + 256], RR[q0][u].ap[:, a_ * 128:(a_ + 1) * 128],
                                       [XX[q1][u], RR[q0][u]])
                        for u in range(NU):
                            b1 = hs[u]
                            kb.op("dve", E("tensor_tensor", out=RR[q1][u].ap, in0=b1.ap[:, 0:256], in1=RR[q0][u].ap, op=ALU.add), [b1, RR[q0][u]], [RR[q1][u]])
                    RF = RR[0]
                    hx = {}
                    for u in range(NU):
                        hp = u
                        kb.op("pool", E("tensor_scalar", out=TMPS[u].ap, in0=Sst[p][u].ap, scalar1=gl4[:, hp, d, c:c + 1], scalar2=0.0, op0=ALU.mult, op1=ALU.add), [Sst[p][u], glall], [TMPS[u]])
                        b1 = nb(); hx[u] = b1
                        u7 = U7[p][u]
                        kb.mm1(b1, b1.ap[:, 0:128], u7.ap[:, 0:128], Sst[p][u].ap, [u7, Sst[p][u]], start=True, stop=False, sig=False)
                        kb.mm1(b1, b1.ap[:, 0:64], BCt[u].ap[:, 0:128], KBV[u].ap[:, 256:320], [BCt[u], KBV[u]], start=False, stop=False, sig=False)
                        kb.mm1(b1, b1.ap[:, 64:128], BCt[u].ap[:, 256:384], KBV[u].ap[:, 320:384], [BCt[u], KBV[u]], start=False, stop=True, sig=True)
                    for u in range(NU):
                        kb.op("act", E("copy", out=XIN[u].ap, in_=hx[u].ap[:, 0:128]), [hx[u]], [XIN[u]])
                    for u in range(NU):
                        b1 = nb(); hx[u] = b1
                        for a_ in range(2):
                            kb.mm1(b1, b1.ap[:, a_ * 64:(a_ + 1) * 64], RF[u].ap[:, a_ * 128:(a_ + 1) * 128], XIN[u].ap[:, a_ * 64:(a_ + 1) * 64], [RF[u], XIN[u]])
                    for u in range(NU):
                        kb.op("dve", E("tensor_scalar", out=UNt[u].ap, in0=hx[u].ap[:, 0:128], scalar1=-1.0, scalar2=None, op0=ALU.mult), [hx[u]], [UNt[u]])
                        up64 = UP[u].ap.rearrange("p (a c) -> p a c", c=64)
                        kb.op("pool", E("tensor_copy", out=up64[:, 0:4:3, :], in_=UNt[u].ap.rearrange("p (a c) -> p a c", a=2)), [UNt[u]], [UP[u]])
                    for u in range(NU):
                        b1 = nb(); hx[u] = b1
                        kb.mm(b1, b1.ap[:, 0:128], [(KBV[u].ap[:, 0:128], KBV[u].ap[:, 256:384]), (KBV[u].ap[:, 128:256], UNt[u].ap)], [KBV[u], UNt[u]])
                    for u in range(NU):
                        kb.op("dve", E("tensor_tensor", out=Sst[po][u].ap, in0=hx[u].ap[:, 0:128], in1=CMF(8), op=ALU.mult), [hx[u], cmf], [Sst[po][u]])
                        kb.op("pool", E("tensor_tensor", out=Sst[po][u].ap, in0=Sst[po][u].ap, in1=TMPS[u].ap, op=ALU.add), [Sst[po][u], TMPS[u]], [Sst[po][u]])
                    for u in range(NU):
                        hp = u
                        b1 = nb(); u7 = U7[p][u]
                        vp3 = VP[u].ap.rearrange("p (a c) -> p a c", a=2); up3 = UP[u].ap.rearrange("p (a c) -> p a c", a=2)
                        kb.mm(b1, b1.ap[:, 0:128], [(Sst[p][u].ap, u7.ap[:, 128:256]),
                                                     (vp3[:, 0, :], BCt[u].ap[:, 128:256]), (vp3[:, 1, :], BCt[u].ap[:, 384:512]),
                                                     (up3[:, 0, :], ZDt[u].ap[:, 128:256]), (up3[:, 1, :], ZDt[u].ap[:, 384:512])],
                              [Sst[p][u], u7, VP[u], UP[u], BCt[u], ZDt[u]])
                        if "DBGYD" in kb.dbg:
                            kb.op("dve", E("tensor_copy", out=TMPS[u].ap, in_=b1.ap[:, 0:128]), [b1], [TMPS[u]])
                            kb.store(kb.dram["DBGYD"][d, hp, :, c * 128:(c + 1) * 128], TMPS[u], TMPS[u].ap)
                        ycol = yT[hp].ap[:, c * 128:(c + 1) * 128]
                        if d == 0:
                            kb.op("dve", E("tensor_copy", out=ycol, in_=b1.ap[:, 0:128]), [b1], [yT[hp]])
                        else:
                            kb.op("dve", E("tensor_tensor", out=ycol, in0=b1.ap[:, 0:128], in1=ycol, op=ALU.add), [b1, yT[hp]], [yT[hp]])
                    gstep += 1
            if stop_after == "rwb":
                return
            if "DBGY" in kb.dbg:
                for hp in range(4):
                    kb.store(kb.dram["DBGY"][hp], yT[hp], yT[hp].ap)
            kb.reset(m1)
            gb = [kb.alloc(512, F32, f"gb{i}") for i in range(2)]
            bb_ = [kb.alloc(512, F32, f"bb{i}") for i in range(2)]
            dv_ = [kb.alloc(512, F32, f"dv{i}") for i in range(2)]
            sq_ = [kb.alloc(512, F32, f"sq{i}") for i in range(2)]
            oo = [kb.alloc(512, BF16, f"oo{i}") for i in range(2)]
            pbk = [kb.pbank(i) for i in range(4)]
            it = 0
            for hp in range(4):
                for (t0, n) in TILES:
                    g_ = gb[it % 2]; b_ = bb_[it % 2]; dd = dv_[it % 2]; qq = sq_[it % 2]; o_ = oo[it % 2]
                    pm = pbk[(2 * it) % 4]; pv = pbk[(2 * it + 1) % 4]; it += 1
                    kb.load(g_, g_.ap[:, :n], RWG[hp, :, t0:t0 + n])
                    kb.load(b_, b_.ap[:, :n], RWBON[hp, :, t0:t0 + n])
                    ysl = yT[hp].ap[:, t0:t0 + n]
                    kb.mm(pm, pm.ap[:, :n], [(CMF(5), ysl)], [cmf, yT[hp]])
                    kb.op("dve", E("tensor_tensor", out=dd.ap[:, :n], in0=ysl, in1=pm.ap[:, :n], op=ALU.subtract), [yT[hp], pm], [dd])
                    kb.op("act", E("activation", out=qq.ap[:, :n], in_=dd.ap[:, :n], func=AF.Square), [dd], [qq])
                    kb.mm(pv, pv.ap[:, :n], [(CMF(5), qq.ap[:, :n])], [cmf, qq])
                    kb.op("act", E("activation", out=qq.ap[:, :n], in_=pv.ap[:, :n], func=AF.Sqrt, bias=64e-5), [pv, qq], [qq])
                    kb.op("dve", E("reciprocal", out=qq.ap[:, :n], in_=qq.ap[:, :n]), [qq], [qq])
                    kb.op("dve", E("scalar_tensor_tensor", out=dd.ap[:, :n], in0=dd.ap[:, :n], scalar=V("gn_w", hp), in1=qq.ap[:, :n], op0=ALU.mult, op1=ALU.mult), [dd, qq, vecs], [dd])
                    kb.op("dve", E("scalar_tensor_tensor", out=dd.ap[:, :n], in0=dd.ap[:, :n], scalar=V("gn_b", hp), in1=b_.ap[:, :n], op0=ALU.add, op1=ALU.add), [dd, b_, vecs], [dd])
                    kb.op("pool", E("tensor_tensor", out=o_.ap[:, :n], in0=dd.ap[:, :n], in1=g_.ap[:, :n], op=ALU.mult), [dd, g_], [o_])
                    kb.store(mixT[512 + hp * 128:512 + (hp + 1) * 128, t0:t0 + n], o_, o_.ap[:, :n])
            kb.reset(m0)

        ALLT = [(ti, t0, n, t0) for ti, (t0, n) in enumerate(TILES)]
        LATT = [(ti, t0, n, t0 - NCTX) for ti, (t0, n) in enumerate(TILES) if ti > 0]

        MLPA = [(0 if t0 == 0 else 1, t0, 256, t0) for t0 in range(0, NTOK, 256)]
        MLPL = [(1, t0, 256, t0 - NCTX) for t0 in range(NCTX, NTOK, 256)]
        LATI = [(ti, t0, n, t0) for ti, (t0, n) in enumerate(TILES) if ti > 0]
        chunks0 = [(i * 128, 128) for i in range(5)] + [(640, 96), (736, 96)] + [(832 + i * 128, 128) for i in range(15)]
        chunks1 = [(i * 128, 128) for i in range(16)]
        if "DBGL1" in kb.dbg:
            kb.dram_t("DBGL1", [3, 8, 128, NTOK], F32)
        if "DBGRW" in kb.dbg:
            kb.dram_t("DBGRW", [5, 4, 128, NTOK], F32)
        if "DBGY" in kb.dbg:
            kb.dram_t("DBGY", [4, 128, NTOK], F32)
        if "DBGYD" in kb.dbg:
            kb.dram_t("DBGYD", [2, 4, 128, NTOK], F32)

        def fin():
            S.run()
            return nc
        import os as _os
        if _os.environ.get("ONLY_RWB"):
            stage_rwkv()
            return fin()
        if start_layer == 0:
            stage_mod(0)
            if "DBGMOD" in kb.dbg:
                dm = kb.dram_t("DBGMOD", [128, 96], F32)
                kb.store(dm[:, :], modv[0], modv[0].ap)
            if stop_after == "mod":
                return fin()
            stage_win(0, xT_in, 2752, "w_in0", chunks0, F0T)
            if stop_after == "win0":
                return fin()
            stage_mla()
            if stop_after == "mla":
                return fin()
            stage_rwkv()
            if stop_after in ("rwkv", "rwa", "rwb"):
                return fin()
            stage_wout(0, xT_in, xT, ALLT)
            if stop_after == "wout0":
                return fin()
            stage_mlp(0, xT, xT, MLPA)
            if stop_after == "mlp0":
                return fin()
            x1src = xT
        else:
            x1src = xT_in
        stage_mod(1)
        stage_win(1, x1src, 2048, "w_in1", chunks1, G1T, gelu_chunks=set(range(8)))
        if stop_after == "win1":
            return fin()
        stage_lru()
        if stop_after == "lru":
            return fin()
        stage_wout(1, x1src, xT, LATI)
        if stop_after == "wout1":
            return fin()
        stage_mlp(1, xT, outT, MLPL)
        S.run()
    return nc


def _rope_tables():
    rows = np.repeat(np.arange(32, dtype=np.float32), 64)
    cols = np.tile(np.arange(64, dtype=np.float32), 32)
    inv = (10000.0 ** (-np.arange(0, 16, 2, dtype=np.float32) / 16)).astype(np.float32)
    ar = rows[:, None] * inv
    ac = cols[:, None] * inv
    ang = np.concatenate([ar, ar, ac, ac], -1)
    cos = np.cos(ang); sin = np.sin(ang)
    sgn = np.tile(np.concatenate([-np.ones(8), np.ones(8)]), 2).astype(np.float32)
    out = np.zeros((128, 2, NLAT), np.float32)
    out[64:96, 0, :] = cos.T
    out[64:96, 1, :] = (sin * sgn).T
    return out


def _rot_perm():
    perm = np.zeros(32, np.int64)
    for a in range(2):
        for h in range(2):
            for f in range(8):
                perm[a * 16 + h * 8 + f] = a * 16 + (1 - h) * 8 + f
    return perm


def prep_inputs(inputs):
    I = {k: np.asarray(v) for k, v in inputs.items()}
    shared = {}
    cmats = np.zeros((128, 9, 128), np.float32)
    cmats[:, 0] = np.eye(128)
    cmats[:, 1] = 1.0 / 1024
    cmats[:, 2] = 1.0 / 384
    cmats[:, 3] = 1.0 / 256
    bo = np.zeros((128, 128), np.float32); bo[:64, :64] = 1; bo[64:, 64:] = 1
    cmats[:, 4] = bo
    cmats[:, 5] = bo / 64
    cmats[64, 6, :64] = 1.0
    cmats[:, 7] = 1.0
    cmats[:, 8] = bo
    shared["cmats"] = cmats
    shared["rope"] = _rope_tables()
    ind = np.zeros((128, 8, 8), np.float32)
    for h in range(8):
        ind[:96, h, h] = 1.0
    shared["ind8"] = ind
    ii = np.arange(128)[:, None]; tt = np.arange(128)[None, :]
    msk = np.zeros((128, 2, 1280), np.float32)
    for d, (st_, inc_) in enumerate((((ii < tt), (ii <= tt)), ((ii > tt), (ii >= tt)))):
        st_ = st_.astype(np.float32); inc_ = inc_.astype(np.float32)
        msk[:, d, 0:512] = np.concatenate([st_, inc_, st_, inc_], 1)
        msk[:, d, 512:1024] = np.concatenate([-st_, inc_, -st_, inc_], 1)
        msk[:, d, 1024:1280] = np.concatenate([-st_.T, -st_.T], 1)
    shared["rwmask"] = msk
    for L in range(2):
        p = f"l{L}_"
        for nm in ("mod_w", "w_out", "mlp_w1", "mlp_w2"):
            shared[p + nm] = np.ascontiguousarray(I[p + nm], np.float32)
    w_in = I["l0_w_in"]
    perm = _rot_perm()
    w0e = np.zeros((1024, 2752), np.float32)
    w0e[:, 0:640] = w_in[:, 0:640]
    w0e[:, 640 + 64:640 + 96] = w_in[:, 640:672]
    w0e[:, 736 + 64:736 + 96] = w_in[:, 640:672][:, perm]
    w0e[:, 832:] = w_in[:, 672:]
    shared["w_in0"] = w0e
    wuq = I["l0_mla_w_uq"].reshape(384, 8, 96)
    wrot = np.zeros_like(wuq)
    wrot[:, :, 64:96] = wuq[:, :, 64:96][:, :, perm]
    shared["w_uq"] = np.ascontiguousarray(wuq.reshape(384, 768))
    shared["w_uq_rot"] = np.ascontiguousarray(wrot.reshape(384, 768))
    shared["w_uk"] = np.ascontiguousarray(I["l0_mla_w_uk"])
    shared["w_uv"] = np.ascontiguousarray(I["l0_mla_w_uv"])
    for nm, src in (("w2pad", "l0_rwkv_w2"), ("a2pad", "l0_rwkv_a2")):
        a = np.zeros((128, 2, 512), np.float32)
        a[0:64, 0] = I[src][0]; a[64:128, 1] = I[src][1]
        shared[nm] = a
    shared["g2"] = np.ascontiguousarray(I["l0_rwkv_g2"])
    shared["w_in1"] = np.ascontiguousarray(I["l1_w_in"])
    shared["ga_w"] = np.ascontiguousarray(np.transpose(I["l1_lru_ga_w"], (2, 0, 1, 3)))
    shared["gx_w"] = np.ascontiguousarray(np.transpose(I["l1_lru_gx_w"], (2, 0, 1, 3)))
    vbase = np.zeros((128, NCOL), np.float32)

    def put(name, arr, n):
        vbase[:, COLS[name]:COLS[name] + n] = _pcol(arr, n)
    put("cctxT", I["c_ctx"], 8)
    for L in range(2):
        p = f"l{L}_"
        put(p + "mod_b", I[p + "mod_b"], 48)
        for nm in ("ln1_g", "ln1_b", "ln2_g", "ln2_b"):
            put(p + nm, I[p + nm], 8)
    put("q_norm", I["l0_mla_q_norm"], 3); put("kv_norm", I["l0_mla_kv_norm"], 2); put("mu", I["l0_rwkv_mu"], 15)
    put("w0", I["l0_rwkv_w0"].reshape(-1), 8); put("a0", I["l0_rwkv_a0"].reshape(-1), 8)
    put("k_k", I["l0_rwkv_k_k"], 4); put("k_a", I["l0_rwkv_k_a"], 4); put("r_k", I["l0_rwkv_r_k"].reshape(-1), 4)
    put("gn_w", I["l0_rwkv_gn_w"], 4); put("gn_b", I["l0_rwkv_gn_b"], 4)
    put("conv_w", I["l1_conv_w"].reshape(-1), 32); put("conv_b", I["l1_conv_b"], 8)
    put("ga_b", I["l1_lru_ga_b"].reshape(-1), 16); put("gx_b", I["l1_lru_gx_b"].reshape(-1), 16)
    put("lam", I["l1_lru_lambda"].reshape(-1), 16)
    per_core = []
    for b in range(8):
        v = vbase.copy()
        v[:, COLS["cT"]:COLS["cT"] + 8] = _pcol(I["c"][b], 8)
        xTb = np.ascontiguousarray(np.concatenate([I["ctx"][b].T, I["x"][b].T], axis=1), np.float32)
        per_core.append({"xT": xTb, "vec": v})
    return shared, per_core


_NC_CACHE = {}


def kernel(**inputs):
    shared, per_core = prep_inputs(inputs)
    if "nc" not in _NC_CACHE:
        _NC_CACHE["nc"] = build_program()
    nc = _NC_CACHE["nc"]
    in_maps = [dict(shared, **pc) for pc in per_core]
    res = run_bass_kernel_spmd(nc, in_maps, core_ids=list(range(8)))
    out = np.stack([np.ascontiguousarray(r["outT"].T) for r in res.results], axis=0)
    return out.astype(np.float32)
```
